# Optimizing a Trainium2 kernel written in Bass

```python
import math
import jax, jax.numpy as jnp
from jax import lax
import numpy as np

D_MODEL = 1024
BATCH = 32
SEQ = 256
DEPTH = 4
DEC_BATCH = 8
DEC_SEQ = 4096
PAST_LEN = 256

GRID_W = 64
HEAD_DIM = 64
GQA_HEADS = 8
GQA_KV_HEADS = 2
GQA_GROUP = GQA_HEADS // GQA_KV_HEADS
GQA_WIDTH = GQA_HEADS * HEAD_DIM
CONV_WIDTH = D_MODEL // 4
DIFF_HEADS = 4
DIFF_QK_DIM = 32
DIFF_V_DIM = 2 * DIFF_QK_DIM
DIFF_WIDTH = DIFF_HEADS * DIFF_V_DIM
MIX_WIDTH = GQA_WIDTH + CONV_WIDTH + DIFF_WIDTH
IN_SIZES = (GQA_WIDTH, GQA_KV_HEADS * HEAD_DIM, GQA_KV_HEADS * HEAD_DIM,
            CONV_WIDTH, CONV_WIDTH, CONV_WIDTH,
            DIFF_HEADS * 2 * DIFF_QK_DIM, DIFF_HEADS * 2 * DIFF_QK_DIM, DIFF_WIDTH)
IN_WIDTH = sum(IN_SIZES)
D_FF = 2816
CONV_W = 3
Q_BLOCK = 128
ROPE_THETA = 10000.0
NORM_EPS = 1e-6
N_MOD = 6

kernel_name = 'hybrid_prefix_dit_step'


def rms_norm(x, g):
    xf = x.astype(jnp.float32)
    y = xf * lax.rsqrt(jnp.mean(xf * xf, axis=-1, keepdims=True) + NORM_EPS)
    return (y * g.astype(jnp.float32)).astype(x.dtype)


def split_cols(z, sizes):
    out, s = [], 0
    for n in sizes:
        out.append(z[..., s:s + n])
        s += n
    return out


def dwconv3(x, w, b):
    t = x.shape[1]
    xp = jnp.pad(x, ((0, 0), (1, 1), (0, 0)))
    return xp[:, :t] * w[0] + xp[:, 1:t + 1] * w[1] + xp[:, 2:] * w[2] + b


def axial_rope_tables(n_rows, dim):
    row = jnp.repeat(jnp.arange(n_rows), GRID_W).astype(jnp.float32)
    col = jnp.tile(jnp.arange(GRID_W), n_rows).astype(jnp.float32)
    nf = dim // 4
    freqs = ROPE_THETA ** (-jnp.arange(nf, dtype=jnp.float32) / nf)
    ar = row[:, None] * freqs[None, :]
    ac = col[:, None] * freqs[None, :]
    ang = jnp.concatenate([ar, ar, ac, ac], axis=-1)
    return jnp.cos(ang), jnp.sin(ang)


def apply_rope(x, cos, sin):
    d = x.shape[-1]
    xr = x.reshape(*x.shape[:-1], 2, 2, d // 4)
    rot = jnp.stack([-xr[..., 1, :], xr[..., 0, :]], axis=-2).reshape(x.shape)
    bshape = (1, x.shape[1]) + (1,) * (x.ndim - 3) + (d,)
    return (x * cos.reshape(bshape) + rot * sin.reshape(bshape)).astype(x.dtype)


def over_query_blocks(fn, q):
    b, t = q.shape[:2]
    nb = t // Q_BLOCK
    qb = jnp.moveaxis(q.reshape(b, nb, Q_BLOCK, *q.shape[2:]), 1, 0)
    out = lax.map(fn, qb)
    return jnp.moveaxis(out, 0, 1).reshape(b, t, *out.shape[3:])


def gqa_attend(q, k, v):
    b, t = q.shape[:2]
    q5 = q.reshape(b, t, GQA_KV_HEADS, GQA_GROUP, HEAD_DIM)
    scale = HEAD_DIM ** -0.5

    def blk(qb):
        s = jnp.einsum('bqkgd,bskd->bkgqs', qb, k).astype(jnp.float32) * scale
        p = jax.nn.softmax(s, axis=-1).astype(v.dtype)
        return jnp.einsum('bkgqs,bskd->bqkgd', p, v)

    return over_query_blocks(blk, q5).reshape(b, t, GQA_WIDTH)


def diff_attend(q, k, v, lam):
    scale = DIFF_QK_DIM ** -0.5

    def blk(qb):
        s = jnp.einsum('bqhcd,bshcd->bhcqs', qb, k).astype(jnp.float32) * scale
        p = jax.nn.softmax(s, axis=-1)
        a = (p[:, :, 0] - lam * p[:, :, 1]).astype(v.dtype)
        return jnp.einsum('bhqs,bshd->bqhd', a, v)

    return over_query_blocks(blk, q)


def trunk_layer(x, cond, lam_init, rope, ctx, w_mod, b_mod, norm1_g, w_in, gqa_qn_g, gqa_kn_g,
                conv_w, conv_b, diff_qn_g, diff_kn_g, diff_lambda, diff_subln_g, w_out,
                norm2_g, ffn_up, ffn_conv_w, ffn_conv_b, ffn_down):
    b, t = x.shape[:2]
    mods = (jax.nn.silu(cond) @ w_mod + b_mod).reshape(cond.shape[0], 1, N_MOD, D_MODEL)
    sh1, sc1, g1, sh2, sc2, g2 = [mods[:, :, i] for i in range(N_MOD)]

    h = rms_norm(x, norm1_g) * (1 + sc1) + sh1
    z = h @ w_in
    qg, kg, vg, cb, cc, cu, qd, kd, vd = split_cols(z, IN_SIZES)
    qg = rms_norm(qg.reshape(b, t, GQA_HEADS, HEAD_DIM), gqa_qn_g)
    kg = rms_norm(kg.reshape(b, t, GQA_KV_HEADS, HEAD_DIM), gqa_kn_g)
    vg = vg.reshape(b, t, GQA_KV_HEADS, HEAD_DIM)
    qd = rms_norm(qd.reshape(b, t, DIFF_HEADS, 2, DIFF_QK_DIM), diff_qn_g)
    kd = rms_norm(kd.reshape(b, t, DIFF_HEADS, 2, DIFF_QK_DIM), diff_kn_g)
    vd = vd.reshape(b, t, DIFF_HEADS, DIFF_V_DIM)
    own = (kg, vg, kd, vd)

    if rope is None:
        kg_all, vg_all, kd_all, vd_all = kg, vg, kd, vd
    else:
        cos64, sin64, cos32, sin32 = rope
        qg = apply_rope(qg, cos64, sin64)
        qd = apply_rope(qd, cos32, sin32)
        ck_g, cv_g, ck_d, cv_d = ctx
        kg_all = jnp.concatenate([apply_rope(kg, cos64, sin64), ck_g.astype(kg.dtype)], axis=1)
        vg_all = jnp.concatenate([vg, cv_g.astype(vg.dtype)], axis=1)
        kd_all = jnp.concatenate([apply_rope(kd, cos32, sin32), ck_d.astype(kd.dtype)], axis=1)
        vd_all = jnp.concatenate([vd, cv_d.astype(vd.dtype)], axis=1)

    o_g = gqa_attend(qg, kg_all, vg_all)
    o_c = cb * dwconv3(cc * cu, conv_w, conv_b)
    lf = diff_lambda.astype(jnp.float32)
    lam = jnp.exp(jnp.sum(lf[0] * lf[1])) - jnp.exp(jnp.sum(lf[2] * lf[3])) + lam_init
    o_d = diff_attend(qd, kd_all, vd_all, lam)
    o_d = (rms_norm(o_d, diff_subln_g) * (1.0 - lam_init)).reshape(b, t, DIFF_WIDTH)
    x = x + g1 * (jnp.concatenate([o_g, o_c, o_d], axis=-1) @ w_out)

    h = rms_norm(x, norm2_g) * (1 + sc2) + sh2
    a, u = jnp.split(h @ ffn_up, 2, axis=-1)
    f = jax.nn.silu(dwconv3(a, ffn_conv_w, ffn_conv_b)) * u
    x = x + g2 * (f @ ffn_down)
    return x, own


def setup_inputs(seed: int = 0) -> dict:
    key = jax.random.key(seed)
    ks = jax.random.split(key, 32)
    f32 = jnp.float32
    nrm = lambda k, shape, s: jax.random.normal(k, shape, f32) * s
    return {
        'x_prompt': nrm(ks[0], (BATCH, SEQ, D_MODEL), 1.0),
        'x_sample': nrm(ks[1], (DEC_BATCH, DEC_SEQ, D_MODEL), 1.0),
        'cache_gqa_k': nrm(ks[2], (DEC_BATCH, DEPTH, PAST_LEN, GQA_KV_HEADS, HEAD_DIM), 1.0),
        'cache_gqa_v': nrm(ks[3], (DEC_BATCH, DEPTH, PAST_LEN, GQA_KV_HEADS, HEAD_DIM), 1.0),
        'cache_diff_k': nrm(ks[4], (DEC_BATCH, DEPTH, PAST_LEN, DIFF_HEADS, 2, DIFF_QK_DIM), 1.0),
        'cache_diff_v': nrm(ks[5], (DEC_BATCH, DEPTH, PAST_LEN, DIFF_HEADS, DIFF_V_DIM), 1.0),
        'c': nrm(ks[6], (DEC_BATCH, D_MODEL), 1.0),
        'c_ctx': nrm(ks[7], (D_MODEL,), 1.0),
        'w_mod': nrm(ks[8], (DEPTH, D_MODEL, N_MOD * D_MODEL), 0.5 * D_MODEL ** -0.5),
        'b_mod': nrm(ks[9], (DEPTH, N_MOD * D_MODEL), 0.01),
        'norm1_g': 1.0 + nrm(ks[10], (DEPTH, D_MODEL), 0.02),
        'w_in': nrm(ks[11], (DEPTH, D_MODEL, IN_WIDTH), D_MODEL ** -0.5),
        'gqa_qn_g': 1.0 + nrm(ks[12], (DEPTH, HEAD_DIM), 0.02),
        'gqa_kn_g': 1.0 + nrm(ks[13], (DEPTH, HEAD_DIM), 0.02),
        'conv_w': nrm(ks[14], (DEPTH, CONV_W, CONV_WIDTH), CONV_W ** -0.5),
        'conv_b': nrm(ks[15], (DEPTH, CONV_WIDTH), 0.01),
        'diff_qn_g': 1.0 + nrm(ks[16], (DEPTH, DIFF_QK_DIM), 0.02),
        'diff_kn_g': 1.0 + nrm(ks[17], (DEPTH, DIFF_QK_DIM), 0.02),
        'diff_lambda': nrm(ks[18], (DEPTH, 4, DIFF_QK_DIM), 0.1),
        'diff_subln_g': 1.0 + nrm(ks[19], (DEPTH, DIFF_V_DIM), 0.02),
        'w_out': nrm(ks[20], (DEPTH, MIX_WIDTH, D_MODEL), MIX_WIDTH ** -0.5),
        'norm2_g': 1.0 + nrm(ks[21], (DEPTH, D_MODEL), 0.02),
        'ffn_up': nrm(ks[22], (DEPTH, D_MODEL, 2 * D_FF), D_MODEL ** -0.5),
        'ffn_conv_w': nrm(ks[23], (DEPTH, CONV_W, D_FF), CONV_W ** -0.5),
        'ffn_conv_b': nrm(ks[24], (DEPTH, D_FF), 0.01),
        'ffn_down': nrm(ks[25], (DEPTH, D_FF, D_MODEL), D_FF ** -0.5),
    }


def reference(x_prompt, x_sample, cache_gqa_k, cache_gqa_v, cache_diff_k, cache_diff_v, c, c_ctx,
              w_mod, b_mod, norm1_g, w_in, gqa_qn_g, gqa_kn_g, conv_w, conv_b, diff_qn_g,
              diff_kn_g, diff_lambda, diff_subln_g, w_out, norm2_g, ffn_up, ffn_conv_w,
              ffn_conv_b, ffn_down):
    def layer_weights(l):
        return (w_mod[l], b_mod[l], norm1_g[l], w_in[l], gqa_qn_g[l], gqa_kn_g[l], conv_w[l],
                conv_b[l], diff_qn_g[l], diff_kn_g[l], diff_lambda[l], diff_subln_g[l], w_out[l],
                norm2_g[l], ffn_up[l], ffn_conv_w[l], ffn_conv_b[l], ffn_down[l])

    xp = x_prompt
    cond_ctx = c_ctx[None, :]
    ks_g, vs_g, ks_d, vs_d = [], [], [], []
    for l in range(DEPTH):
        lam_init = 0.8 - 0.6 * math.exp(-0.3 * l)
        xp, (kg, vg, kd, vd) = trunk_layer(xp, cond_ctx, lam_init, None, None, *layer_weights(l))
        ks_g.append(kg)
        vs_g.append(vg)
        ks_d.append(kd)
        vs_d.append(vd)
    new_gqa_k = jnp.stack(ks_g, axis=1)
    new_gqa_v = jnp.stack(vs_g, axis=1)
    new_diff_k = jnp.stack(ks_d, axis=1)
    new_diff_v = jnp.stack(vs_d, axis=1)

    n_rows = x_sample.shape[1] // GRID_W
    cos64, sin64 = axial_rope_tables(n_rows, HEAD_DIM)
    cos32, sin32 = axial_rope_tables(n_rows, DIFF_QK_DIM)
    rope = (cos64, sin64, cos32, sin32)
    xs = x_sample
    for l in range(DEPTH):
        lam_init = 0.8 - 0.6 * math.exp(-0.3 * l)
        ctx = (cache_gqa_k[:, l], cache_gqa_v[:, l], cache_diff_k[:, l], cache_diff_v[:, l])
        xs, _ = trunk_layer(xs, c, lam_init, rope, ctx, *layer_weights(l))

    return (xp, xs, new_gqa_k, new_gqa_v, new_diff_k, new_diff_v)
```

```python
import math
import numpy as np
import ml_dtypes
import concourse.bass as bass
import concourse.mybir as mybir
from concourse.bass_utils import run_bass_kernel_spmd
from concourse.ap import AP
from contextlib import ExitStack

F32 = mybir.dt.float32
BF16 = mybir.dt.bfloat16
U8 = mybir.dt.uint8
ALU = mybir.AluOpType
AF = mybir.ActivationFunctionType
AX = mybir.AxisListType

D = 1024
GRID_W = 64
DFF = 2816
NJ = DFF // 128
EPS = 1e-6
COMPUTE = ("tensor", "vector", "scalar", "gpsimd")
ALLENG = ("sync", "tensor", "vector", "scalar", "gpsimd")


class Cfg:
    def __init__(self, TS=4096, TP=256, NP=4, L=4, PAST=256):
        self.TS, self.TP, self.NP, self.L, self.PAST = TS, TP, NP, L, PAST
        self.NTOK = TS + NP * TP
        self.stop = 99


def pf_layout(L):
    o = {}
    c = 0
    for name, n in (("cond", 16), ("bmod", L * 48), ("n1g", L * 8), ("n2g", L * 8), ("cw", L * 6),
                    ("cb", L * 2), ("fcw", L * NJ * 3), ("fcb", L * NJ)):
        o[name] = c
        c += n
    o["_n"] = c
    return o


PB_N = 64 + 64 + 32 + 32 + 64 + 128


class Rec:
    def __getattr__(self, name):
        def f(*a, **k):
            self.call = (name, a, k)
            return self
        return f


class Bld:
    def __init__(self, cfg):
        self.cfg = cfg
        self.nc = bass.Bass("TRN2", target_bir_lowering=False)
        self.es = ExitStack()
        self.prog = {e: [] for e in ALLENG}
        self.cnt = {e: 0 for e in COMPUTE}
        self.semh = {}
        self.dval = {}
        self.seen = {e: {} for e in ALLENG}
        self.lastw = {}
        self.readers = {}
        self.arena_off = 0
        self.ARENA = 212480
        self.arena = self.es.enter_context(self.nc.sbuf_tensor("arena", [128, self.ARENA], U8))
        self.P = []
        self.Pb = []
        pall = self.es.enter_context(self.nc.psum_tensor("psall", [128, 4096], F32))
        self.Pall = pall[:, :]
        pallb = pall[:, :].bitcast(BF16)
        for i in range(8):
            self.P.append(self.Pall[:, i * 512:(i + 1) * 512])
            self.Pb.append(pallb[:, i * 1024:(i + 1) * 1024])
        self.tpi = 0
        self.defer = False
        self.emitting = False
        import collections
        self.q = collections.deque()

    def buf(self, cols, dt, at=None):
        nb = cols * (4 if dt == F32 else 2)
        nb = (nb + 31) // 32 * 32
        if at is None:
            at = self.arena_off
            self.arena_off += nb
            assert self.arena_off <= self.ARENA, ("arena overflow", self.arena_off)
        return self.arena[:, at:at + nb].bitcast(dt)[:, 0:cols]

    def sem(self, key):
        if key not in self.semh:
            self.semh[key] = self.es.enter_context(self.nc.semaphore("s%d" % len(self.semh)))
        return self.semh[key]

    def _wait(self, eng, deps):
        need = {}
        for d in deps:
            for k, v in d.items():
                if need.get(k, 0) < v:
                    need[k] = v
        for k, v in need.items():
            if k == eng:
                if eng == "tensor":
                    continue
                if v < self.cnt[eng] - 1:
                    continue
            if self.seen[eng].get(k, 0) >= v:
                continue
            self.seen[eng][k] = v
            sem = self.sem(k)
            self.prog[eng].append(lambda e, sem=sem, v=v: e.wait_ge(sem, v))

    def _deps(self, r, w, ww):
        deps = []
        for k in r:
            if k in self.lastw:
                deps.append(self.lastw[k])
        for k in w:
            if k in self.lastw:
                deps.append(self.lastw[k])
            if k in self.readers:
                deps.append(self.readers[k])
        for k in ww:
            if k in self.readers:
                deps.append(self.readers[k])
        return deps

    def _reg(self, tokk, tokv, r, w, ww):
        for k in r:
            d = self.readers.setdefault(k, {})
            d[tokk] = tokv
        for k in w:
            self.lastw[k] = {tokk: tokv}
            self.readers[k] = {}
        for k in ww:
            d = self.lastw.setdefault(k, {})
            d[tokk] = tokv

    def pop_until_pe(self):
        while self.q:
            ent = self.q.popleft()
            self.emitting = True
            if ent[0] == "op":
                self.op(*ent[1:])
            else:
                self.dma(*ent[1:])
            self.emitting = False
            if ent[0] == "op" and ent[1] == "tensor":
                return

    def flush(self):
        while self.q:
            self.pop_until_pe()

    def op(self, eng, fn, r=(), w=(), ww=()):
        if self.defer and not self.emitting:
            rec = Rec()
            fn(rec)
            call = rec.call
            self.q.append(("op", eng, (lambda e, call=call: getattr(e, call[0])(*call[1], **call[2])), tuple(r), tuple(w), tuple(ww)))
            return
        self._wait(eng, self._deps(r, w, ww))
        self.cnt[eng] += 1
        sem = self.sem(eng)
        rec = Rec()
        fn(rec)
        name, a, k = rec.call
        self.prog[eng].append(lambda e, name=name, a=a, k=k, sem=sem: getattr(e, name)(*a, **k).then_inc(sem, 1))
        self._reg(eng, self.cnt[eng], r, w, ww)

    def dma(self, q, out, in_, r=(), w=(), key=None):
        if self.defer and not self.emitting:
            self.q.append(("dma", q, out, in_, tuple(r), tuple(w), key))
            return
        self._wait(q, self._deps(r, w, ()))
        self.dval[key] = self.dval.get(key, 0) + 16
        sem = self.sem(key)
        self.prog[q].append(lambda e, out=out, in_=in_, sem=sem: e.dma_start(out=out, in_=in_).then_inc(sem, 16))
        self._reg(key, self.dval[key], r, w, ())

    def barrier(self):
        toks = {e: self.cnt[e] for e in COMPUTE if self.cnt[e] > 0}
        toks.update(self.dval)
        for e in ALLENG:
            for k, v in toks.items():
                if k == e:
                    continue
                if self.seen[e].get(k, 0) >= v:
                    continue
                self.seen[e][k] = v
                sem = self.sem(k)
                self.prog[e].append(lambda en, sem=sem, v=v: en.wait_ge(sem, v))

    def finish(self):
        self.barrier()
        nc = self.nc
        prog = self.prog
        with nc.Block() as block:
            for name in ALLENG:
                def mk(name):
                    def body(e):
                        for f in prog[name]:
                            f(e)
                    return body
                getattr(block, name)(mk(name))

    def tpbank(self, banks):
        self.tpi += 1
        return banks[self.tpi % len(banks)]


def V(ap, pat, off=0):
    return AP(ap.tensor, ap.offset + off, [list(ap.ap[0])] + [list(x) for x in pat])


def build(cfg):
    b = Bld(cfg)
    nc = b.nc
    L, TS, TP, NP, PAST, NTOK = cfg.L, cfg.TS, cfg.TP, cfg.NP, cfg.PAST, cfg.NTOK
    PF = pf_layout(L)
    NKC_S = (TS + PAST) // 128
    dt_ = nc.dram_tensor
    xall = dt_("xall", [NTOK, D], F32, kind="ExternalInput").ap()
    ck = dt_("ck", [L, PAST, 128], F32, kind="ExternalInput").ap()
    cv = dt_("cv", [L, PAST, 128], F32, kind="ExternalInput").ap()
    cdk = dt_("cdk", [L, PAST, 256], F32, kind="ExternalInput").ap()
    cdv = dt_("cdv", [L, PAST, 256], F32, kind="ExternalInput").ap()
    pf_d = dt_("pf", [128, PF["_n"]], F32, kind="ExternalInput").ap()
    pb_d = dt_("pb", [L, PB_N], F32, kind="ExternalInput").ap()
    w_mod = dt_("w_mod", [L, D, 6 * D], F32, kind="ExternalInput").ap()
    w_in = dt_("w_in", [L, D, 2304], F32, kind="ExternalInput").ap()
    w_out = dt_("w_out", [L, D, D], F32, kind="ExternalInput").ap()
    f_up = dt_("f_up", [L, D, 2 * DFF], F32, kind="ExternalInput").ap()
    f_dn = dt_("f_dn", [L, DFF, D], F32, kind="ExternalInput").ap()
    rope_d = dt_("rope", [TS, 192], F32, kind="ExternalInput").ap()
    idb_d = dt_("identb", [128, 128], BF16, kind="ExternalInput").ap()
    c32_d = dt_("cst32", [128, 256], F32, kind="ExternalInput").ap()
    y = dt_("y", [NTOK, D], F32, kind="ExternalOutput").ap()
    ngk = dt_("ngk", [NP, L, TP, 128], F32, kind="ExternalOutput").ap()
    ngv = dt_("ngv", [NP, L, TP, 128], F32, kind="ExternalOutput").ap()
    ndk = dt_("ndk", [NP, L, TP, 256], F32, kind="ExternalOutput").ap()
    ndv = dt_("ndv", [NP, L, TP, 256], F32, kind="ExternalOutput").ap()

    identb = b.buf(128, BF16)
    cst32 = b.buf(256, F32)
    ident32, ones32 = cst32[:, 0:128], cst32[:, 128:256]
    pf = b.buf(PF["_n"], F32)
    pbl = b.buf(PB_N, F32)
    modsT = b.buf(2 * L * 48, F32)
    condb = b.buf(16, BF16)
    SBT = b.buf(32, F32)
    lamt = b.buf(16, F32)
    gsubS = b.buf(64, F32)
    Gb = b.buf(1024, F32)
    epsc = b.buf(1, F32)
    small = b.buf(256, F32)
    diag = b.buf(128, F32)
    xs = [b.buf(1024, F32) for _ in range(2)]
    xn = [b.buf(1024, BF16) for _ in range(2)]
    Fb = [b.buf(1024, F32) for _ in range(4)]
    hTh = b.buf(8 * 16, BF16)
    X0 = b.arena_off
    XSZ = 34880
    b.arena_off += XSZ
    BIG0 = b.arena_off
    BIGSZ = 132 * 1024
    b.arena_off += BIGSZ
    assert b.arena_off <= b.ARENA, b.arena_off

    class Lay:
        def __init__(self, base):
            self.o = base

        def buf(self, cols, dt):
            nb = (cols * (4 if dt == F32 else 2) + 31) // 32 * 32
            r = b.buf(cols, dt, at=self.o)
            self.o += nb
            return r

    rowsb = b.buf(1024, F32, at=X0)
    la = Lay(X0)
    hT = la.buf(8 * 512, BF16)
    tab = [la.buf(192, F32) for _ in range(2)]
    v32 = [la.buf(384, F32) for _ in range(2)]
    kb = la.buf(384, BF16)
    pext = la.buf(514, F32)
    assert la.o <= X0 + XSZ
    lb = Lay(X0)
    hTq = [lb.buf(8 * 128, BF16) for _ in range(2)]
    tabB = [lb.buf(192, F32) for _ in range(2)]
    qbs = [lb.buf(768, BF16) for _ in range(2)]
    QT = lb.buf(6 * 512, BF16)
    PbA = lb.buf(3 * 2 * 512, BF16)
    otok = lb.buf(4 * 768, BF16)
    ocatT = lb.buf(6 * 512, BF16)
    assert lb.o <= X0 + XSZ
    lc = Lay(X0)
    hTc = lc.buf(8 * 512, BF16)
    aext = [lc.buf(514, F32) for _ in range(2)]
    fT = lc.buf(NJ * 512, BF16)
    assert lc.o <= X0 + XSZ, lc.o - X0
    lg = Lay(BIG0)
    w_in_s = lg.buf(8 * 2304, BF16)
    w_out_s = lg.buf(8 * 1024, BF16)

    class Seq:
        pass
    seqs = []
    sq = Seq()
    sq.kind, sq.row0, sq.T, sq.NT, sq.rope, sq.nkc, sq.pi = 0, 0, TS, 512, True, NKC_S, -1
    sq.KgT = lg.buf(NKC_S * 128, BF16)
    sq.KdT = lg.buf(2 * NKC_S * 128, BF16)
    sq.Vg = lg.buf(NKC_S * 2 * 65, BF16)
    sq.Vd = lg.buf(NKC_S * 4 * 65, BF16)
    sq.ocT = lg.buf(2 * TS, BF16)
    seqs.append(sq)
    pq = Seq()
    pq.KgT = lg.buf(TP, BF16)
    pq.KdT = lg.buf(2 * TP, BF16)
    pq.Vg = lg.buf((TP // 128) * 2 * 65, BF16)
    pq.Vd = lg.buf((TP // 128) * 4 * 65, BF16)
    pq.ocT = lg.buf(2 * TP, BF16)
    for pi in range(NP):
        s_ = Seq()
        s_.__dict__.update(pq.__dict__)
        s_.kind, s_.row0, s_.T, s_.NT, s_.rope, s_.nkc, s_.pi = 1, TS + pi * TP, TP, TP, False, TP // 128, pi
        seqs.append(s_)
    ckb = lg.buf(2 * 384, BF16)
    QT2 = lg.buf(6 * 512, BF16)
    QTs = [QT, QT2]
    assert lg.o <= BIG0 + BIGSZ, lg.o - BIG0
    lf = Lay(BIG0)
    fup_s = lf.buf(8 * 2 * DFF, BF16)
    fdn_s = lf.buf(NJ * 1024, BF16)
    assert lf.o <= BIG0 + BIGSZ, lf.o - BIG0
    wst = [b.buf(8 * 512, BF16, at=BIG0 + i * 8192) for i in range(2)]

    P, Pb = b.P, b.Pb
    op, dma = b.op, b.dma

    def pfv(name, off, n):
        return pf[:, PF[name] + off: PF[name] + off + n]

    dma("sync", identb, idb_d, w=["identb"], key="setup")
    dma("sync", cst32, c32_d, w=["cst32"], key="setup")
    dma("sync", pf, pf_d, w=["pf"], key="setup")
    op("vector", lambda e: e.memset(epsc, EPS), w=["epsc"])
    b.barrier()
    op("scalar", lambda e: e.activation(out=condb, in_=pfv("cond", 0, 16), func=AF.Silu), w=["condb"])
    one11 = ones32[0:1, 0:1]
    bi = 0
    for l in range(L):
        wv = w_mod[l].rearrange("(k p) n -> p k n", p=128)
        for blk in range(12):
            slot = bi % 2
            bi += 1
            dma("gpsimd", wst[slot].rearrange("p (k n) -> p k n", k=8), wv[:, :, blk * 512:(blk + 1) * 512],
                w=[("wst", slot)], key=("wst", slot))
            for kind in range(2):
                for k in range(8):
                    op("tensor", lambda e, kind=kind, k=k, slot=slot: e.matmul(
                        P[kind][0:1, 0:512], lhsT=condb[:, kind * 8 + k: kind * 8 + k + 1],
                        rhs=wst[slot][:, k * 512:(k + 1) * 512], start=(k == 0), stop=(k == 7)),
                       r=[("wst", slot), "condb"], w=[("ps", kind)])
                op("scalar", lambda e, kind=kind: e.activation(out=rowsb[0:1, kind * 512:(kind + 1) * 512],
                                                               in_=P[kind][0:1, 0:512], func=AF.Copy),
                   r=[("ps", kind)], w=[("row", kind)])
                for c in range(4):
                    op("tensor", lambda e, kind=kind, c=c: e.matmul(
                        P[2 + kind][:, c:c + 1], lhsT=rowsb[0:1, kind * 512 + c * 128: kind * 512 + (c + 1) * 128],
                        rhs=one11, start=True, stop=True),
                       r=[("row", kind), "cst32"], w=[("ps", 2 + kind)])
                mo = (kind * L + l) * 48 + blk * 4
                op("vector", lambda e, kind=kind, mo=mo, l=l, blk=blk: e.tensor_tensor(
                    out=modsT[:, mo:mo + 4], in0=P[2 + kind][:, 0:4], in1=pfv("bmod", l * 48 + blk * 4, 4), op=ALU.add),
                   r=[("ps", 2 + kind), "pf"], ww=["modsT"])
    b.barrier()

    def mT(kind, l, i):
        o = (kind * L + l) * 48 + i * 8
        return modsT[:, o:o + 8]

    sl = [0]

    def norm_a(xap, xkey, np_):
        sl[0] += 1
        i = sl[0] % 2
        ss, sd, rs = small[0:np_, i:i + 1], small[0:np_, 2 + i:3 + i], small[0:np_, 4 + i:5 + i]
        xnb = xn[i]
        op("scalar", lambda e: e.activation(out=xnb[0:np_, :], in_=xap, func=AF.Square, accum_out=ss),
           r=[xkey], w=[("xn", i), ("ss", i)])
        op("scalar", lambda e: e.activation(out=sd, in_=ss, func=AF.Sqrt, scale=1.0 / D, bias=epsc[0:np_, :]),
           r=[("ss", i), "epsc"], w=[("sd", i)])
        op("vector", lambda e: e.reciprocal(out=rs, in_=sd), r=[("sd", i)], w=[("rs", i)])
        op("vector", lambda e: e.tensor_scalar(out=xnb[0:np_, :], in0=xap, scalar1=rs, scalar2=None, op0=ALU.mult),
           r=[xkey, ("rs", i)], w=[("xn", i)])
        return i

    def norm_b(i, np_, ST, BT, dst, dkey, tpbanks):
        xnb = xn[i]
        tb = b.tpbank(tpbanks)
        for c in range(8):
            op("tensor", lambda e, c=c: e.transpose(out=Pb[tb][:, c * 128:c * 128 + np_],
                                                    in_=xnb[0:np_, c * 128:(c + 1) * 128],
                                                    identity=identb[0:np_, 0:np_]),
               r=[("xn", i), "identb"], w=[("ps", tb)])
        for c in range(8):
            op("vector", lambda e, c=c: e.tensor_scalar(out=dst(c), in0=Pb[tb][:, c * 128:c * 128 + np_],
                                                        scalar1=ST[:, c:c + 1], scalar2=BT[:, c:c + 1],
                                                        op0=ALU.mult, op1=ALU.add),
               r=[("ps", tb), "SBT", "modsT"], ww=[dkey])

    def norm_sub(xap, xkey, np_, ST, BT, dst, dkey, tpbanks):
        i = norm_a(xap, xkey, np_)
        norm_b(i, np_, ST, BT, dst, dkey, tpbanks)

    def qknorm(src, skey, G, hd, gain, dst, dkey, zs, zkey="F0"):
        n = G * hd
        sl[0] += 1
        i = sl[0] % 2
        ssq, sdq, rq = small[:, 8 + i * 8:8 + i * 8 + G], small[:, 24 + i * 8:24 + i * 8 + G], small[:, 40 + i * 8:40 + i * 8 + G]
        zv = zs[:, 0:n]
        op("scalar", lambda e: e.activation(out=zv, in_=src, func=AF.Square), r=[skey], w=[zkey])
        op("vector", lambda e: e.tensor_reduce(out=ssq, in_=V(zv, [[hd, G], [1, hd]]), axis=AX.X, op=ALU.add),
           r=[zkey], w=[("ssq", i)])
        op("scalar", lambda e: e.activation(out=sdq, in_=ssq, func=AF.Sqrt, scale=1.0 / hd, bias=epsc),
           r=[("ssq", i)], w=[("sdq", i)])
        op("vector", lambda e: e.reciprocal(out=rq, in_=sdq), r=[("sdq", i)], w=[("rq", i)])
        op("vector", lambda e: e.tensor_tensor(out=V(dst, [[hd, G], [1, hd]]), in0=V(src, [[hd, G], [1, hd]]),
                                               in1=V(rq, [[1, G], [0, hd]]), op=ALU.mult),
           r=[skey, ("rq", i)], ww=[dkey])
        op("vector", lambda e: e.tensor_tensor(out=V(dst, [[hd, G], [1, hd]]), in0=V(dst, [[hd, G], [1, hd]]),
                                               in1=V(gain, [[0, G], [1, hd]]), op=ALU.mult),
           r=["pbl", dkey], ww=[dkey])

    def rope(src, skey, dst, dkey, parts, tb, tkey, t1, t2, k1, k2):
        for (off, G, hd, co, so) in parts:
            q = hd // 4
            n = G * hd
            op("vector", lambda e, off=off, G=G, hd=hd, co=co, n=n: e.tensor_tensor(
                out=V(t1[:, off:off + n], [[hd, G], [1, hd]]), in0=V(src[:, off:off + n], [[hd, G], [1, hd]]),
                in1=V(tb[:, co:co + hd], [[0, G], [1, hd]]), op=ALU.mult), r=[skey, tkey], ww=[k1])
            for pr in range(2):
                op("vector", lambda e, off=off, G=G, hd=hd, so=so, q=q, pr=pr: e.tensor_tensor(
                    out=V(t2, [[hd, G], [2 * q, 2], [1, q]], off + pr * q),
                    in0=V(src, [[hd, G], [2 * q, 2], [1, q]], off + (1 - pr) * q),
                    in1=V(tb, [[0, G], [2 * q, 2], [1, q]], so + pr * q), op=ALU.mult),
                   r=[skey, tkey], ww=[k2])
        W = sum(p[1] * p[2] for p in parts)
        o0 = parts[0][0]
        op("vector", lambda e: e.tensor_tensor(out=dst[:, o0:o0 + W], in0=t1[:, o0:o0 + W], in1=t2[:, o0:o0 + W], op=ALU.add),
           r=[k1, k2], w=[dkey])

    def make_gate(kind, l, gi):
        gT = mT(kind, l, gi)
        for c in range(8):
            op("vector", lambda e, c=c: e.tensor_scalar(out=diag, in0=ident32, scalar1=gT[:, c:c + 1], scalar2=None,
                                                        op0=ALU.mult), r=["cst32", "modsT"], w=["diag"])
            bk = 6 + c // 4
            op("tensor", lambda e, c=c, bk=bk: e.matmul(P[bk][:, (c % 4) * 128:(c % 4 + 1) * 128], lhsT=ones32,
                                                        rhs=diag, start=True, stop=True),
               r=["diag", "cst32"], w=[("ps", bk)])
        for h in range(2):
            op("scalar", lambda e, h=h: e.activation(out=Gb[:, h * 512:(h + 1) * 512], in_=P[6 + h], func=AF.Copy),
               r=[("ps", 6 + h)], ww=["Gb"])

    def xsrc(l, ph):
        return xall if (l == 0 and ph != "C") else y

    xsl = [0]

    def load_x(l, ph, row0, npart=128):
        xsl[0] += 1
        i = xsl[0] % 2
        dma("sync", xs[i][0:npart, :], xsrc(l, ph)[row0:row0 + npart, :], r=[("xres", row0 // 128)], w=[("xs", i)],
            key=("xs", i))
        return i

    def load_halo(l, ph, s_):
        nb = s_.T // s_.NT - 1
        xsl[0] += 1
        i = xsl[0] % 2
        src = xsrc(l, ph)
        for bb in range(nb):
            r0 = s_.row0 + (bb + 1) * s_.NT - 1
            dma("sync", xs[i][2 * bb:2 * bb + 2, :], src[r0:r0 + 2, :],
                r=[("xres", r0 // 128), ("xres", (r0 + 1) // 128)] if bb > 0 else
                [("xres", r0 // 128), ("xres", (r0 + 1) // 128)], w=[("xs", i)] if bb == 0 else [], key=("xs", i))
        b.lastw[("xs", i)] = {("xs", i): b.dval[("xs", i)]}
        return i, 2 * nb

    class Stop(Exception):
        pass

    def chk(n):
        if cfg.stop <= n:
            raise Stop()

    A0, CB, CC, CU, Q0 = 0, 768, 1024, 1280, 1536

    def phaseA(l, s_):
        kind = s_.kind
        ST, BT = SBT[:, kind * 8:kind * 8 + 8], mT(kind, l, 0)
        NT, T = s_.NT, s_.T
        ntile, nsub = T // NT, NT // 128
        gk64, gk32 = pbl[:, 64:128], pbl[:, 160:192]
        nh = 0
        if ntile > 1:
            hi, nh = load_halo(l, "A", s_)
            norm_sub(xs[hi][0:nh, :], ("xs", hi), nh, ST, BT, lambda c: hTh[:, c * 16:c * 16 + nh], "hTh", [0, 1])
        chk(2.1)
        for t in range(ntile):
            t0 = t * NT
            for sb in range(nsub):
                i = load_x(l, "A", s_.row0 + t0 + sb * 128)
                norm_sub(xs[i], ("xs", i), 128, ST, BT,
                         lambda c, sb=sb: hT[:, c * 512 + sb * 128: c * 512 + (sb + 1) * 128], "hT", [0, 1])
            hL = (2 * (t - 1)) if t > 0 else None
            hR = (2 * t + 1) if t < ntile - 1 else None
            chk(2.2)
            for m in range(2):
                for (bank, col0) in ((2, CC + m * 128), (3, CU + m * 128), (4, CB + m * 128)):
                    for k in range(8):
                        op("tensor", lambda e, bank=bank, col0=col0, k=k: e.matmul(
                            P[bank][:, 0:NT], lhsT=w_in_s[:, k * 2304 + col0: k * 2304 + col0 + 128],
                            rhs=hT[:, k * 512: k * 512 + NT], start=(k == 0), stop=(k == 7)),
                           r=["hT", "w_in"], w=[("ps", bank)])
                if nh:
                    for (o, col0) in ((0, CC + m * 128), (16, CU + m * 128)):
                        for k in range(8):
                            op("tensor", lambda e, o=o, col0=col0, k=k: e.matmul(
                                P[5][:, o:o + nh], lhsT=w_in_s[:, k * 2304 + col0: k * 2304 + col0 + 128],
                                rhs=hTh[:, k * 16: k * 16 + nh], start=(k == 0), stop=(k == 7)),
                               r=["hTh", "w_in"], w=[("ps", 5)])
                cuS = Fb[0]
                op("scalar", lambda e: e.activation(out=cuS[:, 0:NT], in_=P[3][:, 0:NT], func=AF.Copy),
                   r=[("ps", 3)], w=["F0"])
                op("vector", lambda e: e.tensor_tensor(out=pext[:, 1:NT + 1], in0=P[2][:, 0:NT], in1=cuS[:, 0:NT],
                                                       op=ALU.mult), r=[("ps", 2), "F0"], w=["pext"])
                if nh:
                    op("scalar", lambda e: e.activation(out=small[:, 64:64 + nh], in_=P[5][:, 16:16 + nh], func=AF.Copy),
                       r=[("ps", 5)], w=["hcu"])
                for (hx, col) in ((hL, 0), (hR, NT + 1)):
                    if hx is None:
                        op("vector", lambda e, col=col: e.memset(pext[:, col:col + 1], 0.0), ww=["pext"])
                    else:
                        op("vector", lambda e, col=col, hx=hx: e.tensor_tensor(
                            out=pext[:, col:col + 1], in0=P[5][:, hx:hx + 1], in1=small[:, 64 + hx:65 + hx], op=ALU.mult),
                           r=[("ps", 5), "hcu"], ww=["pext"])
                tcv = Fb[3]
                cw0 = PF["cw"] + (l * 2 + m) * 3
                cbo = PF["cb"] + l * 2 + m
                op("vector", lambda e, cw0=cw0, cbo=cbo: e.tensor_scalar(
                    out=tcv[:, 0:NT], in0=pext[:, 0:NT], scalar1=pf[:, cw0:cw0 + 1], scalar2=pf[:, cbo:cbo + 1],
                    op0=ALU.mult, op1=ALU.add), r=["pext"], w=["F3"])
                for kk in (1, 2):
                    op("vector", lambda e, cw0=cw0, kk=kk: e.scalar_tensor_tensor(
                        out=tcv[:, 0:NT], in0=pext[:, kk:kk + NT], scalar=pf[:, cw0 + kk:cw0 + kk + 1], in1=tcv[:, 0:NT],
                        op0=ALU.mult, op1=ALU.add), r=["pext", "F3"], w=["F3"])
                op("vector", lambda e, m=m: e.tensor_tensor(out=s_.ocT[:, m * T + t0: m * T + t0 + NT], in0=P[4][:, 0:NT],
                                                             in1=tcv[:, 0:NT], op=ALU.mult), r=[("ps", 4), "F3"])
            chk(2.3)
            for sb in range(nsub):
                g = t * nsub + sb
                for (bank, c0_, n) in ((6, A0, 512), (7, A0 + 512, 256)):
                    for k in range(8):
                        op("tensor", lambda e, bank=bank, c0_=c0_, n=n, k=k, sb=sb: e.matmul(
                            P[bank][:, 0:n], lhsT=hT[:, k * 512 + sb * 128: k * 512 + (sb + 1) * 128],
                            rhs=w_in_s[:, k * 2304 + c0_: k * 2304 + c0_ + n], start=(k == 0), stop=(k == 7)),
                           r=["hT", "w_in"], w=[("ps", bank)])
                ki = 1 + g % 2
                kf = Fb[ki]
                kkey = "F%d" % ki
                qknorm(P[6][:, 0:128], ("ps", 6), 2, 64, gk64, kf[:, 0:128], kkey, Fb[0])
                qknorm(P[6][:, 128:384], ("ps", 6), 8, 32, gk32, kf[:, 128:384], kkey, Fb[0])
                if s_.pi >= 0:
                    rows = slice(g * 128, (g + 1) * 128)
                    dma("sync", ngk[s_.pi, l, rows, :], kf[:, 0:128], r=[kkey], key=("kf", ki))
                    dma("sync", ndk[s_.pi, l, rows, :], kf[:, 128:384], r=[kkey], key=("kf", ki))
                if s_.rope:
                    xsl[0] += 1
                    ti = xsl[0] % 2
                    dma("sync", tab[ti], rope_d[g * 128:(g + 1) * 128, :], w=[("tab", ti)], key=("tab", ti))
                    rope(kf, kkey, kb, "kb", [(0, 2, 64, 0, 64), (128, 8, 32, 128, 160)], tab[ti], ("tab", ti),
                         Fb[3], Fb[3][:, 384:768], "F3", "F3")
                else:
                    op("vector", lambda e: e.tensor_copy(out=kb, in_=kf[:, 0:384]), r=[kkey], w=["kb"])
                tb = b.tpbank([0, 1])
                for blk in range(3):
                    op("tensor", lambda e, blk=blk: e.transpose(out=Pb[tb][:, blk * 128:(blk + 1) * 128],
                                                                in_=kb[:, blk * 128:(blk + 1) * 128], identity=identb),
                       r=["kb"], w=[("ps", tb)])
                nk = s_.nkc * 128
                op("scalar", lambda e, g=g: e.activation(out=s_.KgT[:, g * 128:(g + 1) * 128], in_=Pb[tb][:, 0:128],
                                                         func=AF.Copy), r=[("ps", tb)])
                op("scalar", lambda e, g=g, nk=nk: e.activation(
                    out=V(s_.KdT, [[nk, 2], [1, 128]], g * 128), in_=V(Pb[tb], [[128, 2], [1, 128]], 128), func=AF.Copy),
                   r=[("ps", tb)])
                op("scalar", lambda e, g=g: e.activation(out=V(s_.Vg, [[65, 2], [1, 64]], g * 130),
                                                         in_=V(P[6], [[64, 2], [1, 64]], 384), func=AF.Copy),
                   r=[("ps", 6)])
                op("scalar", lambda e, g=g: e.activation(out=V(s_.Vd, [[65, 4], [1, 64]], g * 260),
                                                         in_=V(P[7], [[64, 4], [1, 64]], 0), func=AF.Copy),
                   r=[("ps", 7)])
                if s_.pi >= 0:
                    vi = g % 2
                    op("vector", lambda e, vi=vi: e.tensor_copy(out=v32[vi][:, 0:128], in_=P[6][:, 384:512]),
                       r=[("ps", 6)], w=[("v32", vi)])
                    op("vector", lambda e, vi=vi: e.tensor_copy(out=v32[vi][:, 128:384], in_=P[7][:, 0:256]),
                       r=[("ps", 7)], ww=[("v32", vi)])
                    rows = slice(g * 128, (g + 1) * 128)
                    dma("sync", ngv[s_.pi, l, rows, :], v32[vi][:, 0:128], r=[("v32", vi)], key=("v32", vi))
                    dma("sync", ndv[s_.pi, l, rows, :], v32[vi][:, 128:384], r=[("v32", vi)], key=("v32", vi))
        chk(2.4)
        if s_.pi < 0:
            nown = T // 128
            nk = s_.nkc * 128
            for sb in range(PAST // 128):
                rows = slice(sb * 128, (sb + 1) * 128)
                ci = 1 + sb % 2
                cf = Fb[ci]
                ckey = "F%d" % ci
                dma("sync", cf[:, 0:128], ck[l, rows, :], w=[ckey], key=("cst", ci))
                dma("sync", cf[:, 128:384], cdk[l, rows, :], key=("cst", ci))
                dma("sync", cf[:, 384:512], cv[l, rows, :], key=("cst", ci))
                dma("sync", cf[:, 512:768], cdv[l, rows, :], key=("cst", ci))
                b.lastw[ckey] = {("cst", ci): b.dval[("cst", ci)]}
                g = nown + sb
                op("vector", lambda e, cf=cf, sb=sb: e.tensor_copy(out=ckb[:, sb * 384:(sb + 1) * 384], in_=cf[:, 0:384]),
                   r=[ckey], w=[("ckb", sb)])
                op("scalar", lambda e, g=g, cf=cf: e.activation(out=V(s_.Vg, [[65, 2], [1, 64]], g * 130),
                                                                in_=V(cf, [[64, 2], [1, 64]], 384), func=AF.Copy), r=[ckey])
                op("scalar", lambda e, g=g, cf=cf: e.activation(out=V(s_.Vd, [[65, 4], [1, 64]], g * 260),
                                                                in_=V(cf, [[64, 4], [1, 64]], 512), func=AF.Copy), r=[ckey])
                chk(2.5)
                tb = b.tpbank([0, 1])
                for blk in range(3):
                    op("tensor", lambda e, blk=blk, sb=sb: e.transpose(
                        out=Pb[tb][:, blk * 128:(blk + 1) * 128], in_=ckb[:, sb * 384 + blk * 128: sb * 384 + (blk + 1) * 128],
                        identity=identb), r=[("ckb", sb)], w=[("ps", tb)])
                op("scalar", lambda e, g=g: e.activation(out=s_.KgT[:, g * 128:(g + 1) * 128], in_=Pb[tb][:, 0:128],
                                                         func=AF.Copy), r=[("ps", tb)])
                op("scalar", lambda e, g=g, nk=nk: e.activation(
                    out=V(s_.KdT, [[nk, 2], [1, 128]], g * 128), in_=V(Pb[tb], [[128, 2], [1, 128]], 128), func=AF.Copy),
                   r=[("ps", tb)])

    def attention(maps, nkc, NT):
        nsub = NT // 128
        Pall = b.Pall
        for k in range(nkc + 2):
            if k < nkc:
                for m in maps:
                    sb_ = 2 * m["i"] + k % 2
                    op("tensor", lambda e, m=m, k=k, sb_=sb_: e.matmul(P[sb_][:, 0:NT], lhsT=m["kT"](k), rhs=m["q"],
                                                                       start=True, stop=True, tile_position=m["tp"]),
                       r=[m["qk"]], w=[("ps", sb_)])
            if k >= 2:
                kk = k - 2
                sl_ = kk % 3
                for m in maps:
                    po = sl_ * 1024 + m["i"] * 512
                    for sb in range(nsub):
                        op("tensor", lambda e, m=m, kk=kk, sb=sb, po=po: e.matmul(
                            P[m["O"]][:, sb * 65:(sb + 1) * 65], lhsT=PbA[:, po + sb * 128: po + (sb + 1) * 128], rhs=m["v"](kk),
                            start=(kk == 0 and sb == 0), stop=(kk == nkc - 1), skip_group_check=True),
                           r=[("pb", sl_)], w=[("ps", m["O"])])
            b.pop_until_pe()
            if 1 <= k <= nkc:
                kk = k - 1
                sl_ = kk % 3
                op("scalar", lambda e, kk=kk, sl_=sl_: e.activation(
                    out=V(PbA, [[512, 2], [1, NT]], sl_ * 1024), in_=V(Pall, [[1024, 2], [1, NT]], (kk % 2) * 512),
                    func=AF.Exp, scale=maps[0]["scale"]),
                   r=[("ps", kk % 2), ("ps", 2 + kk % 2)], w=[("pb", sl_)])

    def phaseB(l, s_, first_of_kind):
        kind = s_.kind
        ST, BT = SBT[:, kind * 8:kind * 8 + 8], mT(kind, l, 0)
        NT, T, nkc = s_.NT, s_.T, s_.nkc
        ntile, nsub = T // NT, NT // 128
        gq64, gq32 = pbl[:, 0:64], pbl[:, 128:160]
        nk = nkc * 128
        if first_of_kind:
            make_gate(kind, l, 2)
        st = {}

        def pn(t, sb):
            i = load_x(l, "B", s_.row0 + t * NT + sb * 128)
            st[(t, sb)] = norm_a(xs[i], ("xs", i), 128)

        def pt(t, sb):
            hq = hTq[sb % 2]
            norm_b(st[(t, sb)], 128, ST, BT, lambda c, hq=hq: hq[:, c * 128:(c + 1) * 128], ("hTq", sb % 2), [6])

        def pz(t, sb):
            hq = hTq[sb % 2]
            hkey = ("hTq", sb % 2)
            for (bank, c0_, n) in ((6, Q0, 512), (7, Q0 + 512, 256)):
                for k in range(8):
                    op("tensor", lambda e, bank=bank, c0_=c0_, n=n, k=k, hq=hq: e.matmul(
                        P[bank][:, 0:n], lhsT=hq[:, k * 128:(k + 1) * 128],
                        rhs=w_in_s[:, k * 2304 + c0_: k * 2304 + c0_ + n], start=(k == 0), stop=(k == 7)),
                       r=[hkey, "w_in"], w=[("ps", bank)])
            qf = Fb[1]
            qknorm(P[6][:, 0:512], ("ps", 6), 8, 64, gq64, qf[:, 0:512], "F1", Fb[0])
            qknorm(P[7][:, 0:256], ("ps", 7), 8, 32, gq32, qf[:, 512:768], "F1", Fb[0])
            if s_.rope:
                g = t * nsub + sb
                xsl[0] += 1
                ti = xsl[0] % 2
                dma("sync", tabB[ti], rope_d[g * 128:(g + 1) * 128, :], w=[("tabB", ti)], key=("tabB", ti))
                rope(qf, "F1", qbs[sb % 2], ("qb", sb % 2), [(0, 8, 64, 0, 64), (512, 8, 32, 128, 160)], tabB[ti], ("tabB", ti),
                     Fb[2], Fb[3], "F2", "F3")
            else:
                op("vector", lambda e: e.tensor_copy(out=qbs[sb % 2], in_=qf[:, 0:768]), r=["F1"], w=[("qb", sb % 2)])

        def pq(t, sb):
            QTt = QTs[t % 2]
            qb = qbs[sb % 2]
            for (b0_, nb_) in ((0, 4), (4, 2)):
                for blk in range(nb_):
                    op("tensor", lambda e, blk=blk, b0_=b0_: e.transpose(
                        out=Pb[7][:, 512 + blk * 128: 512 + (blk + 1) * 128],
                        in_=qb[:, (b0_ + blk) * 128:(b0_ + blk + 1) * 128], identity=identb),
                       r=[("qb", sb % 2)], w=[("ps", 7)])
                op("scalar", lambda e, sb=sb, b0_=b0_, nb_=nb_, QTt=QTt: e.activation(
                    out=V(QTt, [[512, nb_], [1, 128]], b0_ * 512 + sb * 128),
                    in_=V(Pb[7], [[128, nb_], [1, 128]], 512), func=AF.Copy),
                   r=[("ps", 7)], ww=[("QT", t % 2)])

        def ey(t, sb):
            t0 = t * NT
            for half in range(2):
                for c in range(8):
                    if c < 4:
                        lt = ocatT[:, c * 512 + sb * 128: c * 512 + (sb + 1) * 128]
                    elif c < 6:
                        lt = s_.ocT[:, (c - 4) * T + t0 + sb * 128: (c - 4) * T + t0 + (sb + 1) * 128]
                    else:
                        lt = ocatT[:, (c - 2) * 512 + sb * 128: (c - 2) * 512 + (sb + 1) * 128]
                    op("tensor", lambda e, half=half, c=c, lt=lt: e.matmul(
                        P[6 + half], lhsT=lt, rhs=w_out_s[:, c * 1024 + half * 512: c * 1024 + (half + 1) * 512],
                        start=(c == 0), stop=(c == 7)), r=["ocatT", "w_out"], w=[("ps", 6 + half)])
            residual(l, "B", s_.row0 + t0 + sb * 128)

        def eo(t):
            for sb in range(nsub):
                tb = b.tpbank([6, 7])
                for blk in range(6):
                    op("tensor", lambda e, blk=blk, sb=sb: e.transpose(
                        out=Pb[tb][:, blk * 128:(blk + 1) * 128], in_=otok[:, sb * 768 + blk * 128: sb * 768 + (blk + 1) * 128],
                        identity=identb), r=["otok"], w=[("ps", tb)])
                op("scalar", lambda e, sb=sb: e.activation(out=V(ocatT, [[512, 6], [1, 128]], sb * 128),
                                                           in_=V(Pb[tb], [[128, 6], [1, 128]]), func=AF.Copy),
                   r=[("ps", tb)], ww=["ocatT"])

        for sb in range(nsub):
            pn(0, sb), pt(0, sb), pz(0, sb), pq(0, sb)
        for t in range(ntile):
            t0 = t * NT
            QT = QTs[t % 2]
            b.defer = True
            if t > 0:
                for sb in range(nsub):
                    ey(t - 1, sb)
            if t + 1 < ntile:
                tn = t + 1
                pn(tn, 0), pn(tn, 1), pt(tn, 0), pt(tn, 1), pn(tn, 2), pn(tn, 3)
                pz(tn, 0), pt(tn, 2), pz(tn, 1), pq(tn, 0), pt(tn, 3), pz(tn, 2), pq(tn, 1), pz(tn, 3), pq(tn, 2), pq(tn, 3)
            b.defer = False

            def after_group():
                pass
            for j in range(4):
                gcn[0] += 1
                ob = (4, 5)
                maps = []
                for r_ in range(2):
                    maps.append(dict(
                        i=r_, S=(2 * r_, 2 * r_ + 1), O=ob[r_], tp=(64 * r_, 0), scale=0.125, qk=("QT", t % 2),
                        kT=lambda kc, r_=r_: s_.KgT[64 * r_:64 * r_ + 64, kc * 128:(kc + 1) * 128],
                        q=QT[64 * r_:64 * r_ + 64, j * 512: j * 512 + NT],
                        v=lambda kc, r_=r_: s_.Vg[:, kc * 130 + r_ * 65: kc * 130 + r_ * 65 + 65]))
                attention(maps, nkc, NT)
                for r_ in range(2):
                    h = j + 4 * r_
                    Ov = P[ob[r_]]
                    rs = small[:, 72 + r_ * 4: 72 + r_ * 4 + nsub]
                    op("vector", lambda e, Ov=Ov, rs=rs: e.reciprocal(out=rs, in_=V(Ov, [[65, nsub], [1, 1]], 64)),
                       r=[("ps", ob[r_])], w=[("rsg", r_)])
                    op("vector", lambda e, Ov=Ov, rs=rs, h=h: e.tensor_tensor(
                        out=V(otok, [[768, nsub], [1, 64]], h * 64), in0=V(Ov, [[65, nsub], [1, 64]]),
                        in1=V(rs, [[1, nsub], [0, 64]]), op=ALU.mult), r=[("ps", ob[r_]), ("rsg", r_)], ww=["otok"])
                after_group()
            for h in range(4):
                hb, hh = h // 2, h % 2
                gcn[0] += 1
                ob = (4, 5)
                maps = []
                for c_ in range(2):
                    gi = 2 * hh + c_
                    maps.append(dict(
                        i=c_, S=(2 * c_, 2 * c_ + 1), O=ob[c_], tp=(32 * gi, 0), scale=32 ** -0.5, qk=("QT", t % 2),
                        kT=lambda kc, gi=gi, hb=hb: s_.KdT[32 * gi:32 * gi + 32, hb * nk + kc * 128: hb * nk + (kc + 1) * 128],
                        q=QT[32 * gi:32 * gi + 32, (4 + hb) * 512: (4 + hb) * 512 + NT],
                        v=lambda kc, h=h: s_.Vd[:, kc * 260 + h * 65: kc * 260 + h * 65 + 65]))
                attention(maps, nkc, NT)
                if True:
                    Oa, Ob = P[ob[0]], P[ob[1]]
                    ka, kb_ = ("ps", ob[0]), ("ps", ob[1])
                    r0 = small[:, 80:80 + nsub]
                    r1 = small[:, 84:84 + nsub]
                    n = nsub * 64
                    t0_, t1_ = Fb[0][:, 0:n], Fb[0][:, 256:256 + n]
                    op("vector", lambda e, Oa=Oa: e.reciprocal(out=r0, in_=V(Oa, [[65, nsub], [1, 1]], 64)), r=[ka], w=["r0"])
                    op("vector", lambda e, Ob=Ob: e.reciprocal(out=r1, in_=V(Ob, [[65, nsub], [1, 1]], 64)), r=[kb_], w=["r1"])
                    op("vector", lambda e, Oa=Oa: e.tensor_tensor(out=V(t0_, [[64, nsub], [1, 64]]), in0=V(Oa, [[65, nsub], [1, 64]]),
                                                                  in1=V(r0, [[1, nsub], [0, 64]]), op=ALU.mult),
                       r=[ka, "r0"], w=["F0"])
                    op("vector", lambda e, Ob=Ob: e.tensor_tensor(out=V(t1_, [[64, nsub], [1, 64]]), in0=V(Ob, [[65, nsub], [1, 64]]),
                                                                  in1=V(r1, [[1, nsub], [0, 64]]), op=ALU.mult),
                       r=[kb_, "r1"], ww=["F0"])
                    od = Fb[0][:, 512:512 + n]
                    op("vector", lambda e: e.scalar_tensor_tensor(out=od, in0=t1_, scalar=lamt[:, 4 + l:5 + l], in1=t0_,
                                                                  op0=ALU.mult, op1=ALU.add), r=["F0", "lamt"], ww=["F0"])
                    sqd = Fb[0][:, 768:768 + n]
                    op("scalar", lambda e: e.activation(out=sqd, in_=od, func=AF.Square), r=["F0"], ww=["F0"])
                    ssd, sdd, rrd = small[:, 88:88 + nsub], small[:, 92:92 + nsub], small[:, 96:96 + nsub]
                    op("vector", lambda e: e.tensor_reduce(out=ssd, in_=V(sqd, [[64, nsub], [1, 64]]), axis=AX.X, op=ALU.add),
                       r=["F0"], w=["ssd"])
                    op("scalar", lambda e: e.activation(out=sdd, in_=ssd, func=AF.Sqrt, scale=1.0 / 64, bias=epsc),
                       r=["ssd"], w=["sdd"])
                    op("vector", lambda e: e.reciprocal(out=rrd, in_=sdd), r=["sdd"], w=["rrd"])
                    op("vector", lambda e: e.tensor_tensor(out=V(od, [[64, nsub], [1, 64]]), in0=V(od, [[64, nsub], [1, 64]]),
                                                           in1=V(rrd, [[1, nsub], [0, 64]]), op=ALU.mult),
                       r=["F0", "rrd"], ww=["F0"])
                    op("vector", lambda e, h=h: e.tensor_tensor(out=V(otok, [[768, nsub], [1, 64]], 512 + h * 64),
                                                                in0=V(od, [[64, nsub], [1, 64]]),
                                                                in1=V(gsubS, [[0, nsub], [1, 64]]), op=ALU.mult),
                       r=["F0", "gsubS"], ww=["otok"])
                after_group()
            b.flush()
            eo(t)
            if t == ntile - 1:
                for sb in range(nsub):
                    ey(t, sb)

    rsl = [0]
    gcn = [0]

    def residual(l, ph, row0):
        i = load_x(l, ph, row0)
        rsl[0] += 1
        ri = 2 + rsl[0] % 2
        tr = Fb[ri]
        rk = "F%d" % ri
        for half in range(2):
            op("vector", lambda e, half=half: e.tensor_tensor(out=tr[:, half * 512:(half + 1) * 512], in0=P[6 + half],
                                                              in1=Gb[:, half * 512:(half + 1) * 512], op=ALU.mult),
               r=[("ps", 6 + half), "Gb"], w=[rk] if half == 0 else [], ww=[] if half == 0 else [rk])
        op("gpsimd", lambda e: e.tensor_tensor(out=tr, in0=tr, in1=xs[i], op=ALU.add), r=[rk, ("xs", i)], w=[rk])
        dma("sync", y[row0:row0 + 128, :], tr, r=[rk], w=[("xres", row0 // 128)], key=rk)

    def phaseC(l, s_, first_of_kind):
        kind = s_.kind
        ST, BT = SBT[:, 16 + kind * 8:16 + kind * 8 + 8], mT(kind, l, 3)
        NT, T = s_.NT, s_.T
        ntile, nsub = T // NT, NT // 128
        if first_of_kind:
            make_gate(kind, l, 5)
        nh = 0
        if ntile > 1:
            hi, nh = load_halo(l, "C", s_)
            norm_sub(xs[hi][0:nh, :], ("xs", hi), nh, ST, BT, lambda c: hTh[:, c * 16:c * 16 + nh], "hTh", [4])
        for t in range(ntile):
            t0 = t * NT
            for sb in range(nsub):
                i = load_x(l, "C", s_.row0 + t0 + sb * 128)
                norm_sub(xs[i], ("xs", i), 128, ST, BT,
                         lambda c, sb=sb: hTc[:, c * 512 + sb * 128: c * 512 + (sb + 1) * 128], "hTc", [4])
            hL = (2 * (t - 1)) if t > 0 else None
            hR = (2 * t + 1) if t < ntile - 1 else None
            for j in range(NJ):
                pa, pu = j % 2, 2 + j % 2
                for (bank, co) in ((pa, j * 256), (pu, j * 256 + 128)):
                    for k in range(8):
                        op("tensor", lambda e, bank=bank, co=co, k=k: e.matmul(
                            P[bank][:, 0:NT], lhsT=fup_s[:, k * 2 * DFF + co: k * 2 * DFF + co + 128],
                            rhs=hTc[:, k * 512: k * 512 + NT], start=(k == 0), stop=(k == 7)),
                           r=["hTc", ("fup", j // 2)], w=[("ps", bank)])
                if nh:
                    for k in range(8):
                        op("tensor", lambda e, j=j, k=k: e.matmul(
                            P[5][:, j * 16: j * 16 + nh], lhsT=fup_s[:, k * 2 * DFF + j * 256: k * 2 * DFF + j * 256 + 128],
                            rhs=hTh[:, k * 16: k * 16 + nh], start=(k == 0), stop=(k == 7)),
                           r=["hTh", ("fup", j // 2)], w=[("ps", 5)])
                ax = aext[j % 2]
                akey = ("aext", j % 2)
                op("scalar", lambda e, ax=ax, pa=pa: e.activation(out=ax[:, 1:NT + 1], in_=P[pa][:, 0:NT], func=AF.Copy),
                   r=[("ps", pa)], w=[akey])
                for (hx, col) in ((hL, 0), (hR, NT + 1)):
                    if hx is None:
                        op("vector", lambda e, ax=ax, col=col: e.memset(ax[:, col:col + 1], 0.0), ww=[akey])
                    else:
                        op("vector", lambda e, ax=ax, col=col, hx=hx, j=j: e.tensor_copy(
                            out=ax[:, col:col + 1], in_=P[5][:, j * 16 + hx: j * 16 + hx + 1]), r=[("ps", 5)], ww=[akey])
                tcv = Fb[0][:, (j % 2) * 512:(j % 2) * 512 + NT]
                tk = ("tcv", j % 2)
                sil = Fb[1][:, (j % 2) * 512:(j % 2) * 512 + NT]
                sk = ("sil", j % 2)
                w0 = PF["fcw"] + (l * NJ + j) * 3
                bo = PF["fcb"] + l * NJ + j
                op("vector", lambda e, ax=ax, tcv=tcv, w0=w0, bo=bo: e.tensor_scalar(
                    out=tcv, in0=ax[:, 0:NT], scalar1=pf[:, w0:w0 + 1], scalar2=pf[:, bo:bo + 1],
                    op0=ALU.mult, op1=ALU.add), r=[akey], w=[tk])
                for kk in (1, 2):
                    op("vector", lambda e, ax=ax, tcv=tcv, w0=w0, kk=kk: e.scalar_tensor_tensor(
                        out=tcv, in0=ax[:, kk:kk + NT], scalar=pf[:, w0 + kk:w0 + kk + 1], in1=tcv,
                        op0=ALU.mult, op1=ALU.add), r=[akey, tk], w=[tk])
                op("scalar", lambda e, tcv=tcv, sil=sil: e.activation(out=sil, in_=tcv, func=AF.Silu), r=[tk], w=[sk])
                op("vector", lambda e, sil=sil, pu=pu, j=j: e.tensor_tensor(out=fT[:, j * 512: j * 512 + NT], in0=P[pu][:, 0:NT],
                                                                            in1=sil, op=ALU.mult),
                   r=[("ps", pu), sk], ww=["fT"])
            for sb in range(nsub):
                for half in range(2):
                    for j in range(NJ):
                        op("tensor", lambda e, half=half, j=j, sb=sb: e.matmul(
                            P[6 + half], lhsT=fT[:, j * 512 + sb * 128: j * 512 + (sb + 1) * 128],
                            rhs=fdn_s[:, j * 1024 + half * 512: j * 1024 + (half + 1) * 512],
                            start=(j == 0), stop=(j == NJ - 1)), r=["fT", ("fdn", j // 11)], w=[("ps", 6 + half)])
                residual(l, "C", s_.row0 + t0 + sb * 128)

    try:
      chk(1)
      for l in range(L):
          lam_init = 0.8 - 0.6 * math.exp(-0.3 * l)
          wi = w_in[l].rearrange("(k p) n -> p k n", p=128)
          wis = w_in_s.rearrange("p (k n) -> p k n", k=8)
          for c0_ in range(0, 2304, 768):
              dma("gpsimd", wis[:, :, c0_:c0_ + 768], wi[:, :, c0_:c0_ + 768], w=[] if c0_ else ["w_in"], key="w_in")
          b.lastw["w_in"] = {"w_in": b.dval["w_in"]}
          wo = w_out[l].rearrange("(k p) n -> p k n", p=128)
          dma("gpsimd", w_out_s.rearrange("p (k n) -> p k n", k=8), wo, w=["w_out"], key="w_out")
          dma("sync", pbl, AP(pb_d.tensor, l * PB_N, [[0, 128], [1, PB_N]]), w=["pbl"], key="pbl")
          for kind in range(2):
              op("vector", lambda e, kind=kind: e.scalar_tensor_tensor(
                  out=SBT[:, kind * 8:kind * 8 + 8], in0=mT(kind, l, 1), scalar=1.0, in1=pfv("n1g", l * 8, 8),
                  op0=ALU.add, op1=ALU.mult), r=["modsT", "pf"], ww=["SBT"])
              op("vector", lambda e, kind=kind: e.scalar_tensor_tensor(
                  out=SBT[:, 16 + kind * 8:16 + kind * 8 + 8], in0=mT(kind, l, 4), scalar=1.0, in1=pfv("n2g", l * 8, 8),
                  op0=ALU.add, op1=ALU.mult), r=["modsT", "pf"], ww=["SBT"])
          dl = pbl[:, 256:384]
          pr = small[:, 128:192]
          op("vector", lambda e: e.tensor_tensor(out=V(pr, [[32, 2], [1, 32]]), in0=V(dl, [[64, 2], [1, 32]]),
                                                 in1=V(dl, [[64, 2], [1, 32]], 32), op=ALU.mult), r=["pbl"], w=["pr"])
          op("vector", lambda e: e.tensor_reduce(out=lamt[:, 0:2], in_=V(pr, [[32, 2], [1, 32]]), axis=AX.X, op=ALU.add),
             r=["pr"], w=["lam0"])
          op("scalar", lambda e: e.activation(out=lamt[:, 2:4], in_=lamt[:, 0:2], func=AF.Exp), r=["lam0"], w=["lam1"])
          op("vector", lambda e, li=lam_init: e.scalar_tensor_tensor(
              out=lamt[:, 4 + l:5 + l], in0=lamt[:, 3:4], scalar=-li, in1=lamt[:, 2:3], op0=ALU.add, op1=ALU.subtract),
             r=["lam1"], w=["lamt"])
          op("vector", lambda e, li=lam_init: e.tensor_scalar(out=gsubS, in0=pbl[:, 192:256], scalar1=1.0 - li, scalar2=None,
                                                              op0=ALU.mult), r=["pbl"], w=["gsubS"])
          for s_ in (seqs[0], seqs[1]):
              nk_ = s_.nkc
              op("vector", lambda e, a=V(s_.Vg, [[65, nk_ * 2], [1, 1]], 64): e.memset(a, 1.0))
              op("vector", lambda e, a=V(s_.Vd, [[65, nk_ * 4], [1, 1]], 64): e.memset(a, 1.0))
          b.barrier()
          chk(2)
          prevk = -1
          for s_ in seqs:
              phaseA(l, s_)
              b.barrier()
              chk(3)
              phaseB(l, s_, s_.kind != prevk)
              prevk = s_.kind
              b.barrier()
          fu = f_up[l].rearrange("(k p) n -> p k n", p=128)
          fus = fup_s.rearrange("p (k n) -> p k n", k=8)
          for pc in range(NJ // 2):
              dma("gpsimd", fus[:, :, pc * 512:(pc + 1) * 512], fu[:, :, pc * 512:(pc + 1) * 512], w=[("fup", pc)],
                  key=("fup", pc))
          fd = f_dn[l].rearrange("(j p) n -> p j n", p=128)
          fds = fdn_s.rearrange("p (j n) -> p j n", j=NJ)
          for pc in range(2):
              dma("gpsimd", fds[:, pc * 11:(pc + 1) * 11, :], fd[:, pc * 11:(pc + 1) * 11, :], w=[("fdn", pc)],
                  key=("fdn", pc))
          prevk = -1
          for s_ in seqs:
              phaseC(l, s_, s_.kind != prevk)
              prevk = s_.kind
          b.barrier()
    except Stop:
        pass
    b.finish()
    return nc


_NC_CACHE = {}


def _rope_table(TS):
    n_rows = TS // GRID_W
    row = np.repeat(np.arange(n_rows), GRID_W).astype(np.float32)
    col = np.tile(np.arange(GRID_W), n_rows).astype(np.float32)
    out = []
    for dim in (64, 32):
        nf = dim // 4
        freqs = (np.float32(10000.0) ** (-np.arange(nf, dtype=np.float32) / np.float32(nf))).astype(np.float32)
        ar = row[:, None] * freqs[None, :]
        ac = col[:, None] * freqs[None, :]
        ang = np.concatenate([ar, ar, ac, ac], axis=-1).astype(np.float32)
        cos, sin = np.cos(ang), np.sin(ang)
        sgn = np.concatenate([-np.ones(nf), np.ones(nf), -np.ones(nf), np.ones(nf)]).astype(np.float32)
        out += [cos.astype(np.float32), (sin * sgn[None, :]).astype(np.float32)]
    return np.ascontiguousarray(np.concatenate(out, axis=-1), dtype=np.float32)


def _fm(v):
    v = np.asarray(v, dtype=np.float32)
    n = v.shape[-1] // 128
    r = v.reshape(v.shape[:-1] + (n, 128))
    return np.moveaxis(r, -1, 0)


def run(cfg, inp, n_cores):
    L, TS, TP, NP = cfg.L, cfg.TS, cfg.TP, cfg.NP
    f = lambda k: np.asarray(inp[k], dtype=np.float32)
    key = (cfg.TS, cfg.TP, cfg.NP, cfg.L, cfg.PAST)
    if key not in _NC_CACHE:
        _NC_CACHE[key] = build(cfg)
    nc = _NC_CACHE[key]
    PF = pf_layout(L)
    qg = [np.arange(h * 64, (h + 1) * 64) for h in (0, 4, 1, 5, 2, 6, 3, 7)]
    o_qg, o_kg, o_vg, o_cb, o_cc, o_cu, o_qd, o_kd, o_vd = 0, 512, 640, 768, 1024, 1280, 1536, 1792, 2048
    perm = np.concatenate([np.arange(o_kg, o_kg + 128), np.arange(o_kd, o_kd + 256), np.arange(o_vg, o_vg + 128),
                           np.arange(o_vd, o_vd + 256), np.arange(o_cb, o_cb + 256), np.arange(o_cc, o_cc + 256),
                           np.arange(o_cu, o_cu + 256), np.concatenate(qg), np.arange(o_qd, o_qd + 256)])
    w_in_p = np.ascontiguousarray(f("w_in")[:, :, perm])
    fu = f("ffn_up")
    fu_p = np.ascontiguousarray(
        np.stack([fu[:, :, :DFF].reshape(L, D, NJ, 128), fu[:, :, DFF:].reshape(L, D, NJ, 128)], axis=3).reshape(L, D, 2 * DFF))
    pb = np.ascontiguousarray(np.concatenate([f("gqa_qn_g"), f("gqa_kn_g"), f("diff_qn_g"), f("diff_kn_g"), f("diff_subln_g"),
                                              f("diff_lambda").reshape(L, 128)], axis=1))
    identb = np.eye(128, dtype=np.float32).astype(ml_dtypes.bfloat16)
    cst32 = np.ascontiguousarray(np.concatenate([np.eye(128, dtype=np.float32), np.ones((128, 128), np.float32)], axis=1))
    rope = _rope_table(TS)
    shared = dict(w_mod=f("w_mod"), w_in=w_in_p, w_out=f("w_out"), f_up=fu_p, f_dn=f("ffn_down"), rope=rope, identb=identb,
                  cst32=cst32, pb=pb)
    xp, xsm = f("x_prompt"), f("x_sample")
    in_maps = []
    for c in range(n_cores):
        pfa = np.zeros((128, PF["_n"]), np.float32)
        pfa[:, PF["cond"]:PF["cond"] + 8] = _fm(f("c")[c])
        pfa[:, PF["cond"] + 8:PF["cond"] + 16] = _fm(f("c_ctx"))
        pfa[:, PF["bmod"]:PF["bmod"] + L * 48] = _fm(f("b_mod")).reshape(128, L * 48)
        pfa[:, PF["n1g"]:PF["n1g"] + L * 8] = _fm(f("norm1_g")).reshape(128, L * 8)
        pfa[:, PF["n2g"]:PF["n2g"] + L * 8] = _fm(f("norm2_g")).reshape(128, L * 8)
        cw = _fm(f("conv_w"))
        pfa[:, PF["cw"]:PF["cw"] + L * 6] = np.transpose(cw, (0, 1, 3, 2)).reshape(128, L * 6)
        pfa[:, PF["cb"]:PF["cb"] + L * 2] = _fm(f("conv_b")).reshape(128, L * 2)
        fcw = _fm(f("ffn_conv_w"))
        pfa[:, PF["fcw"]:PF["fcw"] + L * NJ * 3] = np.transpose(fcw, (0, 1, 3, 2)).reshape(128, L * NJ * 3)
        pfa[:, PF["fcb"]:PF["fcb"] + L * NJ] = _fm(f("ffn_conv_b")).reshape(128, L * NJ)
        xall = np.ascontiguousarray(np.concatenate([xsm[c], xp[c * NP:(c + 1) * NP].reshape(NP * TP, D)], axis=0))
        m = dict(shared)
        m.update(xall=xall, pf=pfa,
                 ck=np.ascontiguousarray(f("cache_gqa_k")[c].reshape(L, cfg.PAST, 128)),
                 cv=np.ascontiguousarray(f("cache_gqa_v")[c].reshape(L, cfg.PAST, 128)),
                 cdk=np.ascontiguousarray(f("cache_diff_k")[c].reshape(L, cfg.PAST, 256)),
                 cdv=np.ascontiguousarray(f("cache_diff_v")[c].reshape(L, cfg.PAST, 256)))
        in_maps.append(m)
    res = run_bass_kernel_spmd(nc, in_maps, core_ids=list(range(n_cores)))
    R = res.results
    ys = np.stack([R[c]["y"][:TS] for c in range(n_cores)], axis=0)
    yp = np.concatenate([R[c]["y"][TS:].reshape(NP, TP, D) for c in range(n_cores)], axis=0)
    gk = np.concatenate([R[c]["ngk"] for c in range(n_cores)], axis=0).reshape(n_cores * NP, L, TP, 2, 64)
    gv = np.concatenate([R[c]["ngv"] for c in range(n_cores)], axis=0).reshape(n_cores * NP, L, TP, 2, 64)
    dk = np.concatenate([R[c]["ndk"] for c in range(n_cores)], axis=0).reshape(n_cores * NP, L, TP, 4, 2, 32)
    dv = np.concatenate([R[c]["ndv"] for c in range(n_cores)], axis=0).reshape(n_cores * NP, L, TP, 4, 64)
    return (yp.astype(np.float32), ys.astype(np.float32), gk.astype(np.float32), gv.astype(np.float32),
            dk.astype(np.float32), dv.astype(np.float32))


def kernel(**inputs):
    cfg = Cfg()
    return run(cfg, inputs, 8)
```

```python
import math
import numpy as np
import ml_dtypes
import concourse.bass as bass
import concourse.mybir as mybir
from concourse.bass_utils import run_bass_kernel_spmd
from concourse.ap import AP
from contextlib import ExitStack

F32 = mybir.dt.float32
BF16 = mybir.dt.bfloat16
U8 = mybir.dt.uint8
ALU = mybir.AluOpType
AF = mybir.ActivationFunctionType
AX = mybir.AxisListType

D = 1024
GRID_W = 64
DFF = 2816
NJ = DFF // 128
EPS = 1e-6
COMPUTE = ("tensor", "vector", "scalar", "gpsimd")
ALLENG = ("sync", "tensor", "vector", "scalar", "gpsimd")


class Cfg:
    def __init__(self, TS=4096, TP=256, NP=4, L=4, PAST=256):
        self.TS, self.TP, self.NP, self.L, self.PAST = TS, TP, NP, L, PAST
        self.NTOK = TS + NP * TP
        self.stop = 99


def pf_layout(L):
    o = {}
    c = 0
    for name, n in (("cond", 16), ("bmod", L * 48), ("n1g", L * 8), ("n2g", L * 8), ("cw", L * 6),
                    ("cb", L * 2), ("fcw", L * NJ * 3), ("fcb", L * NJ)):
        o[name] = c
        c += n
    o["_n"] = c
    return o


PB_N = 64 + 64 + 32 + 32 + 64 + 128


class Rec:
    def __getattr__(self, name):
        def f(*a, **k):
            self.call = (name, a, k)
            return self
        return f


class Bld:
    def __init__(self, cfg):
        self.cfg = cfg
        self.nc = bass.Bass("TRN2", target_bir_lowering=False)
        self.es = ExitStack()
        self.prog = {e: [] for e in ALLENG}
        self.cnt = {e: 0 for e in COMPUTE}
        self.semh = {}
        self.dval = {}
        self.seen = {e: {} for e in ALLENG}
        self.lastw = {}
        self.readers = {}
        self.arena_off = 0
        self.ARENA = 212480
        self.arena = self.es.enter_context(self.nc.sbuf_tensor("arena", [128, self.ARENA], U8))
        self.P = []
        self.Pb = []
        pall = self.es.enter_context(self.nc.psum_tensor("psall", [128, 4096], F32))
        self.Pall = pall[:, :]
        pallb = pall[:, :].bitcast(BF16)
        for i in range(8):
            self.P.append(self.Pall[:, i * 512:(i + 1) * 512])
            self.Pb.append(pallb[:, i * 1024:(i + 1) * 1024])
        self.tpi = 0
        self.defer = False
        self.emitting = False
        import collections
        self.q = collections.deque()

    def buf(self, cols, dt, at=None):
        nb = cols * (4 if dt == F32 else 2)
        nb = (nb + 31) // 32 * 32
        if at is None:
            at = self.arena_off
            self.arena_off += nb
            assert self.arena_off <= self.ARENA, ("arena overflow", self.arena_off)
        return self.arena[:, at:at + nb].bitcast(dt)[:, 0:cols]

    def sem(self, key):
        if key not in self.semh:
            self.semh[key] = self.es.enter_context(self.nc.semaphore("s%d" % len(self.semh)))
        return self.semh[key]

    def _wait(self, eng, deps):
        need = {}
        for d in deps:
            for k, v in d.items():
                if need.get(k, 0) < v:
                    need[k] = v
        for k, v in need.items():
            if k == eng:
                if eng == "tensor":
                    continue
                if v < self.cnt[eng] - 1:
                    continue
            if self.seen[eng].get(k, 0) >= v:
                continue
            self.seen[eng][k] = v
            sem = self.sem(k)
            self.prog[eng].append(lambda e, sem=sem, v=v: e.wait_ge(sem, v))

    def _deps(self, r, w, ww):
        deps = []
        for k in r:
            if k in self.lastw:
                deps.append(self.lastw[k])
        for k in w:
            if k in self.lastw:
                deps.append(self.lastw[k])
            if k in self.readers:
                deps.append(self.readers[k])
        for k in ww:
            if k in self.readers:
                deps.append(self.readers[k])
        return deps

    def _reg(self, tokk, tokv, r, w, ww):
        for k in r:
            d = self.readers.setdefault(k, {})
            d[tokk] = tokv
        for k in w:
            self.lastw[k] = {tokk: tokv}
            self.readers[k] = {}
        for k in ww:
            d = self.lastw.setdefault(k, {})
            d[tokk] = tokv

    def pop_until_pe(self):
        while self.q:
            ent = self.q.popleft()
            self.emitting = True
            if ent[0] == "op":
                self.op(*ent[1:])
            else:
                self.dma(*ent[1:])
            self.emitting = False
            if ent[0] == "op" and ent[1] == "tensor":
                return

    def flush(self):
        while self.q:
            self.pop_until_pe()

    def op(self, eng, fn, r=(), w=(), ww=()):
        if self.defer and not self.emitting:
            rec = Rec()
            fn(rec)
            call = rec.call
            self.q.append(("op", eng, (lambda e, call=call: getattr(e, call[0])(*call[1], **call[2])), tuple(r), tuple(w), tuple(ww)))
            return
        self._wait(eng, self._deps(r, w, ww))
        self.cnt[eng] += 1
        sem = self.sem(eng)
        rec = Rec()
        fn(rec)
        name, a, k = rec.call
        self.prog[eng].append(lambda e, name=name, a=a, k=k, sem=sem: getattr(e, name)(*a, **k).then_inc(sem, 1))
        self._reg(eng, self.cnt[eng], r, w, ww)

    def dma(self, q, out, in_, r=(), w=(), key=None):
        if self.defer and not self.emitting:
            self.q.append(("dma", q, out, in_, tuple(r), tuple(w), key))
            return
        self._wait(q, self._deps(r, w, ()))
        self.dval[key] = self.dval.get(key, 0) + 16
        sem = self.sem(key)
        self.prog[q].append(lambda e, out=out, in_=in_, sem=sem: e.dma_start(out=out, in_=in_).then_inc(sem, 16))
        self._reg(key, self.dval[key], r, w, ())

    def barrier(self):
        toks = {e: self.cnt[e] for e in COMPUTE if self.cnt[e] > 0}
        toks.update(self.dval)
        for e in ALLENG:
            for k, v in toks.items():
                if k == e:
                    continue
                if self.seen[e].get(k, 0) >= v:
                    continue
                self.seen[e][k] = v
                sem = self.sem(k)
                self.prog[e].append(lambda en, sem=sem, v=v: en.wait_ge(sem, v))

    def finish(self):
        self.barrier()
        nc = self.nc
        prog = self.prog
        with nc.Block() as block:
            for name in ALLENG:
                def mk(name):
                    def body(e):
                        for f in prog[name]:
                            f(e)
                    return body
                getattr(block, name)(mk(name))

    def tpbank(self, banks):
        self.tpi += 1
        return banks[self.tpi % len(banks)]


def V(ap, pat, off=0):
    return AP(ap.tensor, ap.offset + off, [list(ap.ap[0])] + [list(x) for x in pat])


def build(cfg):
    b = Bld(cfg)
    nc = b.nc
    L, TS, TP, NP, PAST, NTOK = cfg.L, cfg.TS, cfg.TP, cfg.NP, cfg.PAST, cfg.NTOK
    PF = pf_layout(L)
    NKC_S = (TS + PAST) // 128
    dt_ = nc.dram_tensor
    xall = dt_("xall", [NTOK, D], F32, kind="ExternalInput").ap()
    ck = dt_("ck", [L, PAST, 128], F32, kind="ExternalInput").ap()
    cv = dt_("cv", [L, PAST, 128], F32, kind="ExternalInput").ap()
    cdk = dt_("cdk", [L, PAST, 256], F32, kind="ExternalInput").ap()
    cdv = dt_("cdv", [L, PAST, 256], F32, kind="ExternalInput").ap()
    pf_d = dt_("pf", [128, PF["_n"]], F32, kind="ExternalInput").ap()
    pb_d = dt_("pb", [L, PB_N], F32, kind="ExternalInput").ap()
    w_mod = dt_("w_mod", [L, D, 6 * D], F32, kind="ExternalInput").ap()
    w_in = dt_("w_in", [L, D, 2304], F32, kind="ExternalInput").ap()
    w_out = dt_("w_out", [L, D, D], F32, kind="ExternalInput").ap()
    f_up = dt_("f_up", [L, D, 2 * DFF], F32, kind="ExternalInput").ap()
    f_dn = dt_("f_dn", [L, DFF, D], F32, kind="ExternalInput").ap()
    rope_d = dt_("rope", [TS, 192], F32, kind="ExternalInput").ap()
    idb_d = dt_("identb", [128, 128], BF16, kind="ExternalInput").ap()
    c32_d = dt_("cst32", [128, 256], F32, kind="ExternalInput").ap()
    y = dt_("y", [NTOK, D], F32, kind="ExternalOutput").ap()
    ngk = dt_("ngk", [NP, L, TP, 128], F32, kind="ExternalOutput").ap()
    ngv = dt_("ngv", [NP, L, TP, 128], F32, kind="ExternalOutput").ap()
    ndk = dt_("ndk", [NP, L, TP, 256], F32, kind="ExternalOutput").ap()
    ndv = dt_("ndv", [NP, L, TP, 256], F32, kind="ExternalOutput").ap()

    identb = b.buf(128, BF16)
    cst32 = b.buf(256, F32)
    ident32, ones32 = cst32[:, 0:128], cst32[:, 128:256]
    pf = b.buf(PF["_n"], F32)
    pbl = b.buf(PB_N, F32)
    modsT = b.buf(2 * L * 48, F32)
    condb = b.buf(16, BF16)
    SBT = b.buf(32, F32)
    lamt = b.buf(16, F32)
    gsubS = b.buf(64, F32)
    Gb = b.buf(1024, F32)
    epsc = b.buf(1, F32)
    small = b.buf(256, F32)
    diag = b.buf(128, F32)
    xs = [b.buf(1024, F32) for _ in range(2)]
    xn = [b.buf(1024, BF16) for _ in range(2)]
    Fb = [b.buf(1024, F32) for _ in range(4)]
    hTh = b.buf(8 * 16, BF16)
    X0 = b.arena_off
    XSZ = 34880
    b.arena_off += XSZ
    BIG0 = b.arena_off
    BIGSZ = 132 * 1024
    b.arena_off += BIGSZ
    assert b.arena_off <= b.ARENA, b.arena_off

    class Lay:
        def __init__(self, base):
            self.o = base

        def buf(self, cols, dt):
            nb = (cols * (4 if dt == F32 else 2) + 31) // 32 * 32
            r = b.buf(cols, dt, at=self.o)
            self.o += nb
            return r

    rowsb = b.buf(1024, F32, at=X0)
    la = Lay(X0)
    hT = la.buf(8 * 512, BF16)
    tab = [la.buf(192, F32) for _ in range(2)]
    v32 = [la.buf(384, F32) for _ in range(2)]
    kb = la.buf(384, BF16)
    pext = la.buf(514, F32)
    assert la.o <= X0 + XSZ
    lb = Lay(X0)
    hTq = [lb.buf(8 * 128, BF16) for _ in range(2)]
    tabB = [lb.buf(192, F32) for _ in range(2)]
    qbs = [lb.buf(768, BF16) for _ in range(2)]
    QT = lb.buf(6 * 512, BF16)
    PbA = lb.buf(3 * 2 * 512, BF16)
    otok = lb.buf(4 * 768, BF16)
    ocatT = lb.buf(6 * 512, BF16)
    assert lb.o <= X0 + XSZ
    lc = Lay(X0)
    hTc = lc.buf(8 * 512, BF16)
    aext = [lc.buf(514, F32) for _ in range(2)]
    fT = lc.buf(NJ * 512, BF16)
    assert lc.o <= X0 + XSZ, lc.o - X0
    lg = Lay(BIG0)
    w_in_s = lg.buf(8 * 2304, BF16)
    w_out_s = lg.buf(8 * 1024, BF16)

    class Seq:
        pass
    seqs = []
    sq = Seq()
    sq.kind, sq.row0, sq.T, sq.NT, sq.rope, sq.nkc, sq.pi = 0, 0, TS, 512, True, NKC_S, -1
    sq.KgT = lg.buf(NKC_S * 128, BF16)
    sq.KdT = lg.buf(2 * NKC_S * 128, BF16)
    sq.Vg = lg.buf(NKC_S * 2 * 65, BF16)
    sq.Vd = lg.buf(NKC_S * 4 * 65, BF16)
    sq.ocT = lg.buf(2 * TS, BF16)
    seqs.append(sq)
    pq = Seq()
    pq.KgT = lg.buf(TP, BF16)
    pq.KdT = lg.buf(2 * TP, BF16)
    pq.Vg = lg.buf((TP // 128) * 2 * 65, BF16)
    pq.Vd = lg.buf((TP // 128) * 4 * 65, BF16)
    pq.ocT = lg.buf(2 * TP, BF16)
    for pi in range(NP):
        s_ = Seq()
        s_.__dict__.update(pq.__dict__)
        s_.kind, s_.row0, s_.T, s_.NT, s_.rope, s_.nkc, s_.pi = 1, TS + pi * TP, TP, TP, False, TP // 128, pi
        seqs.append(s_)
    ckb = lg.buf(2 * 384, BF16)
    QT2 = lg.buf(6 * 512, BF16)
    QTs = [QT, QT2]
    assert lg.o <= BIG0 + BIGSZ, lg.o - BIG0
    lf = Lay(BIG0)
    fup_s = lf.buf(8 * 2 * DFF, BF16)
    fdn_s = lf.buf(NJ * 1024, BF16)
    assert lf.o <= BIG0 + BIGSZ, lf.o - BIG0
    wst = [b.buf(8 * 512, BF16, at=BIG0 + i * 8192) for i in range(2)]

    P, Pb = b.P, b.Pb
    op, dma = b.op, b.dma

    def pfv(name, off, n):
        return pf[:, PF[name] + off: PF[name] + off + n]

    dma("sync", identb, idb_d, w=["identb"], key="setup")
    dma("sync", cst32, c32_d, w=["cst32"], key="setup")
    dma("sync", pf, pf_d, w=["pf"], key="setup")
    op("vector", lambda e: e.memset(epsc, EPS), w=["epsc"])
    b.barrier()
    op("scalar", lambda e: e.activation(out=condb, in_=pfv("cond", 0, 16), func=AF.Silu), w=["condb"])
    one11 = ones32[0:1, 0:1]
    bi = 0
    for l in range(L):
        wv = w_mod[l].rearrange("(k p) n -> p k n", p=128)
        for blk in range(12):
            slot = bi % 2
            bi += 1
            dma("gpsimd", wst[slot].rearrange("p (k n) -> p k n", k=8), wv[:, :, blk * 512:(blk + 1) * 512],
                w=[("wst", slot)], key=("wst", slot))
            for kind in range(2):
                for k in range(8):
                    op("tensor", lambda e, kind=kind, k=k, slot=slot: e.matmul(
                        P[kind][0:1, 0:512], lhsT=condb[:, kind * 8 + k: kind * 8 + k + 1],
                        rhs=wst[slot][:, k * 512:(k + 1) * 512], start=(k == 0), stop=(k == 7)),
                       r=[("wst", slot), "condb"], w=[("ps", kind)])
                op("scalar", lambda e, kind=kind: e.activation(out=rowsb[0:1, kind * 512:(kind + 1) * 512],
                                                               in_=P[kind][0:1, 0:512], func=AF.Copy),
                   r=[("ps", kind)], w=[("row", kind)])
                for c in range(4):
                    op("tensor", lambda e, kind=kind, c=c: e.matmul(
                        P[2 + kind][:, c:c + 1], lhsT=rowsb[0:1, kind * 512 + c * 128: kind * 512 + (c + 1) * 128],
                        rhs=one11, start=True, stop=True),
                       r=[("row", kind), "cst32"], w=[("ps", 2 + kind)])
                mo = (kind * L + l) * 48 + blk * 4
                op("vector", lambda e, kind=kind, mo=mo, l=l, blk=blk: e.tensor_tensor(
                    out=modsT[:, mo:mo + 4], in0=P[2 + kind][:, 0:4], in1=pfv("bmod", l * 48 + blk * 4, 4), op=ALU.add),
                   r=[("ps", 2 + kind), "pf"], ww=["modsT"])
    b.barrier()

    def mT(kind, l, i):
        o = (kind * L + l) * 48 + i * 8
        return modsT[:, o:o + 8]

    sl = [0]

    def norm_a(xap, xkey, np_):
        sl[0] += 1
        i = sl[0] % 2
        ss, sd, rs = small[0:np_, i:i + 1], small[0:np_, 2 + i:3 + i], small[0:np_, 4 + i:5 + i]
        xnb = xn[i]
        op("scalar", lambda e: e.activation(out=xnb[0:np_, :], in_=xap, func=AF.Square, accum_out=ss),
           r=[xkey], w=[("xn", i), ("ss", i)])
        op("scalar", lambda e: e.activation(out=sd, in_=ss, func=AF.Ln, scale=1.0 / D, bias=epsc[0:np_, :]),
           r=[("ss", i), "epsc"], w=[("sd", i)])
        op("scalar", lambda e: e.activation(out=rs, in_=sd, func=AF.Exp, scale=-0.5), r=[("sd", i)], w=[("rs", i)])
        op("vector", lambda e: e.tensor_scalar(out=xnb[0:np_, :], in0=xap, scalar1=rs, scalar2=None, op0=ALU.mult),
           r=[xkey, ("rs", i)], w=[("xn", i)])
        return i

    def norm_b(i, np_, ST, BT, dst, dkey, tpbanks):
        xnb = xn[i]
        tb = b.tpbank(tpbanks)
        for c in range(8):
            op("tensor", lambda e, c=c: e.transpose(out=Pb[tb][:, c * 128:c * 128 + np_],
                                                    in_=xnb[0:np_, c * 128:(c + 1) * 128],
                                                    identity=identb[0:np_, 0:np_]),
               r=[("xn", i), "identb"], w=[("ps", tb)])
        for c in range(8):
            op("vector", lambda e, c=c: e.tensor_scalar(out=dst(c), in0=Pb[tb][:, c * 128:c * 128 + np_],
                                                        scalar1=ST[:, c:c + 1], scalar2=BT[:, c:c + 1],
                                                        op0=ALU.mult, op1=ALU.add),
               r=[("ps", tb), "SBT", "modsT"], ww=[dkey])

    def norm_sub(xap, xkey, np_, ST, BT, dst, dkey, tpbanks):
        i = norm_a(xap, xkey, np_)
        norm_b(i, np_, ST, BT, dst, dkey, tpbanks)

    def qknorm(src, skey, G, hd, gain, dst, dkey, zs, zkey="F0"):
        n = G * hd
        sl[0] += 1
        i = sl[0] % 2
        ssq, sdq, rq = small[:, 8 + i * 8:8 + i * 8 + G], small[:, 24 + i * 8:24 + i * 8 + G], small[:, 40 + i * 8:40 + i * 8 + G]
        zv = zs[:, 0:n]
        op("scalar", lambda e: e.activation(out=zv, in_=src, func=AF.Square), r=[skey], w=[zkey])
        op("vector", lambda e: e.tensor_reduce(out=ssq, in_=V(zv, [[hd, G], [1, hd]]), axis=AX.X, op=ALU.add),
           r=[zkey], w=[("ssq", i)])
        op("scalar", lambda e: e.activation(out=sdq, in_=ssq, func=AF.Ln, scale=1.0 / hd, bias=epsc),
           r=[("ssq", i)], w=[("sdq", i)])
        op("scalar", lambda e: e.activation(out=rq, in_=sdq, func=AF.Exp, scale=-0.5), r=[("sdq", i)], w=[("rq", i)])
        op("vector", lambda e: e.tensor_tensor(out=V(dst, [[hd, G], [1, hd]]), in0=V(src, [[hd, G], [1, hd]]),
                                               in1=V(rq, [[1, G], [0, hd]]), op=ALU.mult),
           r=[skey, ("rq", i)], ww=[dkey])
        op("vector", lambda e: e.tensor_tensor(out=V(dst, [[hd, G], [1, hd]]), in0=V(dst, [[hd, G], [1, hd]]),
                                               in1=V(gain, [[0, G], [1, hd]]), op=ALU.mult),
           r=["pbl", dkey], ww=[dkey])

    def rope(src, skey, dst, dkey, parts, tb, tkey, t1, t2, k1, k2):
        for (off, G, hd, co, so) in parts:
            q = hd // 4
            n = G * hd
            op("vector", lambda e, off=off, G=G, hd=hd, co=co, n=n: e.tensor_tensor(
                out=V(t1[:, off:off + n], [[hd, G], [1, hd]]), in0=V(src[:, off:off + n], [[hd, G], [1, hd]]),
                in1=V(tb[:, co:co + hd], [[0, G], [1, hd]]), op=ALU.mult), r=[skey, tkey], ww=[k1])
            for pr in range(2):
                op("vector", lambda e, off=off, G=G, hd=hd, so=so, q=q, pr=pr: e.tensor_tensor(
                    out=V(t2, [[hd, G], [2 * q, 2], [1, q]], off + pr * q),
                    in0=V(src, [[hd, G], [2 * q, 2], [1, q]], off + (1 - pr) * q),
                    in1=V(tb, [[0, G], [2 * q, 2], [1, q]], so + pr * q), op=ALU.mult),
                   r=[skey, tkey], ww=[k2])
        W = sum(p[1] * p[2] for p in parts)
        o0 = parts[0][0]
        op("vector", lambda e: e.tensor_tensor(out=dst[:, o0:o0 + W], in0=t1[:, o0:o0 + W], in1=t2[:, o0:o0 + W], op=ALU.add),
           r=[k1, k2], w=[dkey])

    def make_gate(kind, l, gi):
        gT = mT(kind, l, gi)
        for c in range(8):
            op("vector", lambda e, c=c: e.tensor_scalar(out=diag, in0=ident32, scalar1=gT[:, c:c + 1], scalar2=None,
                                                        op0=ALU.mult), r=["cst32", "modsT"], w=["diag"])
            bk = 6 + c // 4
            op("tensor", lambda e, c=c, bk=bk: e.matmul(P[bk][:, (c % 4) * 128:(c % 4 + 1) * 128], lhsT=ones32,
                                                        rhs=diag, start=True, stop=True),
               r=["diag", "cst32"], w=[("ps", bk)])
        for h in range(2):
            op("scalar", lambda e, h=h: e.activation(out=Gb[:, h * 512:(h + 1) * 512], in_=P[6 + h], func=AF.Copy),
               r=[("ps", 6 + h)], ww=["Gb"])

    def xsrc(l, ph):
        return xall if (l == 0 and ph != "C") else y

    xsl = [0]

    def load_x(l, ph, row0, npart=128):
        xsl[0] += 1
        i = xsl[0] % 2
        dma("sync", xs[i][0:npart, :], xsrc(l, ph)[row0:row0 + npart, :], r=[("xres", row0 // 128)], w=[("xs", i)],
            key=("xs", i))
        return i

    def load_halo(l, ph, s_):
        nb = s_.T // s_.NT - 1
        xsl[0] += 1
        i = xsl[0] % 2
        src = xsrc(l, ph)
        for bb in range(nb):
            r0 = s_.row0 + (bb + 1) * s_.NT - 1
            dma("sync", xs[i][2 * bb:2 * bb + 2, :], src[r0:r0 + 2, :],
                r=[("xres", r0 // 128), ("xres", (r0 + 1) // 128)] if bb > 0 else
                [("xres", r0 // 128), ("xres", (r0 + 1) // 128)], w=[("xs", i)] if bb == 0 else [], key=("xs", i))
        b.lastw[("xs", i)] = {("xs", i): b.dval[("xs", i)]}
        return i, 2 * nb

    class Stop(Exception):
        pass

    def chk(n):
        if cfg.stop <= n:
            raise Stop()

    A0, CB, CC, CU, Q0 = 0, 768, 1024, 1280, 1536

    def phaseA(l, s_):
        kind = s_.kind
        ST, BT = SBT[:, kind * 8:kind * 8 + 8], mT(kind, l, 0)
        NT, T = s_.NT, s_.T
        ntile, nsub = T // NT, NT // 128
        gk64, gk32 = pbl[:, 64:128], pbl[:, 160:192]
        nh = 0
        if ntile > 1:
            hi, nh = load_halo(l, "A", s_)
            norm_sub(xs[hi][0:nh, :], ("xs", hi), nh, ST, BT, lambda c: hTh[:, c * 16:c * 16 + nh], "hTh", [0, 1])
        chk(2.1)
        for t in range(ntile):
            t0 = t * NT
            for sb in range(nsub):
                i = load_x(l, "A", s_.row0 + t0 + sb * 128)
                norm_sub(xs[i], ("xs", i), 128, ST, BT,
                         lambda c, sb=sb: hT[:, c * 512 + sb * 128: c * 512 + (sb + 1) * 128], "hT", [0, 1])
            hL = (2 * (t - 1)) if t > 0 else None
            hR = (2 * t + 1) if t < ntile - 1 else None
            chk(2.2)
            for m in range(2):
                for (bank, col0) in ((2, CC + m * 128), (3, CU + m * 128), (4, CB + m * 128)):
                    for k in range(8):
                        op("tensor", lambda e, bank=bank, col0=col0, k=k: e.matmul(
                            P[bank][:, 0:NT], lhsT=w_in_s[:, k * 2304 + col0: k * 2304 + col0 + 128],
                            rhs=hT[:, k * 512: k * 512 + NT], start=(k == 0), stop=(k == 7)),
                           r=["hT", "w_in"], w=[("ps", bank)])
                if nh:
                    for (o, col0) in ((0, CC + m * 128), (16, CU + m * 128)):
                        for k in range(8):
                            op("tensor", lambda e, o=o, col0=col0, k=k: e.matmul(
                                P[5][:, o:o + nh], lhsT=w_in_s[:, k * 2304 + col0: k * 2304 + col0 + 128],
                                rhs=hTh[:, k * 16: k * 16 + nh], start=(k == 0), stop=(k == 7)),
                               r=["hTh", "w_in"], w=[("ps", 5)])
                cuS = Fb[0]
                op("scalar", lambda e: e.activation(out=cuS[:, 0:NT], in_=P[3][:, 0:NT], func=AF.Copy),
                   r=[("ps", 3)], w=["F0"])
                op("vector", lambda e: e.tensor_tensor(out=pext[:, 1:NT + 1], in0=P[2][:, 0:NT], in1=cuS[:, 0:NT],
                                                       op=ALU.mult), r=[("ps", 2), "F0"], w=["pext"])
                if nh:
                    op("scalar", lambda e: e.activation(out=small[:, 64:64 + nh], in_=P[5][:, 16:16 + nh], func=AF.Copy),
                       r=[("ps", 5)], w=["hcu"])
                for (hx, col) in ((hL, 0), (hR, NT + 1)):
                    if hx is None:
                        op("vector", lambda e, col=col: e.memset(pext[:, col:col + 1], 0.0), ww=["pext"])
                    else:
                        op("vector", lambda e, col=col, hx=hx: e.tensor_tensor(
                            out=pext[:, col:col + 1], in0=P[5][:, hx:hx + 1], in1=small[:, 64 + hx:65 + hx], op=ALU.mult),
                           r=[("ps", 5), "hcu"], ww=["pext"])
                tcv = Fb[3]
                cw0 = PF["cw"] + (l * 2 + m) * 3
                cbo = PF["cb"] + l * 2 + m
                op("vector", lambda e, cw0=cw0, cbo=cbo: e.tensor_scalar(
                    out=tcv[:, 0:NT], in0=pext[:, 0:NT], scalar1=pf[:, cw0:cw0 + 1], scalar2=pf[:, cbo:cbo + 1],
                    op0=ALU.mult, op1=ALU.add), r=["pext"], w=["F3"])
                for kk in (1, 2):
                    op("vector", lambda e, cw0=cw0, kk=kk: e.scalar_tensor_tensor(
                        out=tcv[:, 0:NT], in0=pext[:, kk:kk + NT], scalar=pf[:, cw0 + kk:cw0 + kk + 1], in1=tcv[:, 0:NT],
                        op0=ALU.mult, op1=ALU.add), r=["pext", "F3"], w=["F3"])
                op("vector", lambda e, m=m: e.tensor_tensor(out=s_.ocT[:, m * T + t0: m * T + t0 + NT], in0=P[4][:, 0:NT],
                                                             in1=tcv[:, 0:NT], op=ALU.mult), r=[("ps", 4), "F3"])
            chk(2.3)
            for sb in range(nsub):
                g = t * nsub + sb
                for (bank, c0_, n) in ((6, A0, 512), (7, A0 + 512, 256)):
                    for k in range(8):
                        op("tensor", lambda e, bank=bank, c0_=c0_, n=n, k=k, sb=sb: e.matmul(
                            P[bank][:, 0:n], lhsT=hT[:, k * 512 + sb * 128: k * 512 + (sb + 1) * 128],
                            rhs=w_in_s[:, k * 2304 + c0_: k * 2304 + c0_ + n], start=(k == 0), stop=(k == 7)),
                           r=["hT", "w_in"], w=[("ps", bank)])
                ki = 1 + g % 2
                kf = Fb[ki]
                kkey = "F%d" % ki
                qknorm(P[6][:, 0:128], ("ps", 6), 2, 64, gk64, kf[:, 0:128], kkey, Fb[0])
                qknorm(P[6][:, 128:384], ("ps", 6), 8, 32, gk32, kf[:, 128:384], kkey, Fb[0])
                if s_.pi >= 0:
                    rows = slice(g * 128, (g + 1) * 128)
                    dma("sync", ngk[s_.pi, l, rows, :], kf[:, 0:128], r=[kkey], key=("kf", ki))
                    dma("sync", ndk[s_.pi, l, rows, :], kf[:, 128:384], r=[kkey], key=("kf", ki))
                if s_.rope:
                    xsl[0] += 1
                    ti = xsl[0] % 2
                    dma("sync", tab[ti], rope_d[g * 128:(g + 1) * 128, :], w=[("tab", ti)], key=("tab", ti))
                    rope(kf, kkey, kb, "kb", [(0, 2, 64, 0, 64), (128, 8, 32, 128, 160)], tab[ti], ("tab", ti),
                         Fb[3], Fb[3][:, 384:768], "F3", "F3")
                else:
                    op("vector", lambda e: e.tensor_copy(out=kb, in_=kf[:, 0:384]), r=[kkey], w=["kb"])
                tb = b.tpbank([0, 1])
                for blk in range(3):
                    op("tensor", lambda e, blk=blk: e.transpose(out=Pb[tb][:, blk * 128:(blk + 1) * 128],
                                                                in_=kb[:, blk * 128:(blk + 1) * 128], identity=identb),
                       r=["kb"], w=[("ps", tb)])
                nk = s_.nkc * 128
                op("scalar", lambda e, g=g: e.activation(out=s_.KgT[:, g * 128:(g + 1) * 128], in_=Pb[tb][:, 0:128],
                                                         func=AF.Copy), r=[("ps", tb)])
                op("scalar", lambda e, g=g, nk=nk: e.activation(
                    out=V(s_.KdT, [[nk, 2], [1, 128]], g * 128), in_=V(Pb[tb], [[128, 2], [1, 128]], 128), func=AF.Copy),
                   r=[("ps", tb)])
                op("scalar", lambda e, g=g: e.activation(out=V(s_.Vg, [[65, 2], [1, 64]], g * 130),
                                                         in_=V(P[6], [[64, 2], [1, 64]], 384), func=AF.Copy),
                   r=[("ps", 6)])
                op("scalar", lambda e, g=g: e.activation(out=V(s_.Vd, [[65, 4], [1, 64]], g * 260),
                                                         in_=V(P[7], [[64, 4], [1, 64]], 0), func=AF.Copy),
                   r=[("ps", 7)])
                if s_.pi >= 0:
                    vi = g % 2
                    op("vector", lambda e, vi=vi: e.tensor_copy(out=v32[vi][:, 0:128], in_=P[6][:, 384:512]),
                       r=[("ps", 6)], w=[("v32", vi)])
                    op("vector", lambda e, vi=vi: e.tensor_copy(out=v32[vi][:, 128:384], in_=P[7][:, 0:256]),
                       r=[("ps", 7)], ww=[("v32", vi)])
                    rows = slice(g * 128, (g + 1) * 128)
                    dma("sync", ngv[s_.pi, l, rows, :], v32[vi][:, 0:128], r=[("v32", vi)], key=("v32", vi))
                    dma("sync", ndv[s_.pi, l, rows, :], v32[vi][:, 128:384], r=[("v32", vi)], key=("v32", vi))
        chk(2.4)
        if s_.pi < 0:
            nown = T // 128
            nk = s_.nkc * 128
            for sb in range(PAST // 128):
                rows = slice(sb * 128, (sb + 1) * 128)
                ci = 1 + sb % 2
                cf = Fb[ci]
                ckey = "F%d" % ci
                dma("sync", cf[:, 0:128], ck[l, rows, :], w=[ckey], key=("cst", ci))
                dma("sync", cf[:, 128:384], cdk[l, rows, :], key=("cst", ci))
                dma("sync", cf[:, 384:512], cv[l, rows, :], key=("cst", ci))
                dma("sync", cf[:, 512:768], cdv[l, rows, :], key=("cst", ci))
                b.lastw[ckey] = {("cst", ci): b.dval[("cst", ci)]}
                g = nown + sb
                op("vector", lambda e, cf=cf, sb=sb: e.tensor_copy(out=ckb[:, sb * 384:(sb + 1) * 384], in_=cf[:, 0:384]),
                   r=[ckey], w=[("ckb", sb)])
                op("scalar", lambda e, g=g, cf=cf: e.activation(out=V(s_.Vg, [[65, 2], [1, 64]], g * 130),
                                                                in_=V(cf, [[64, 2], [1, 64]], 384), func=AF.Copy), r=[ckey])
                op("scalar", lambda e, g=g, cf=cf: e.activation(out=V(s_.Vd, [[65, 4], [1, 64]], g * 260),
                                                                in_=V(cf, [[64, 4], [1, 64]], 512), func=AF.Copy), r=[ckey])
                chk(2.5)
                tb = b.tpbank([0, 1])
                for blk in range(3):
                    op("tensor", lambda e, blk=blk, sb=sb: e.transpose(
                        out=Pb[tb][:, blk * 128:(blk + 1) * 128], in_=ckb[:, sb * 384 + blk * 128: sb * 384 + (blk + 1) * 128],
                        identity=identb), r=[("ckb", sb)], w=[("ps", tb)])
                op("scalar", lambda e, g=g: e.activation(out=s_.KgT[:, g * 128:(g + 1) * 128], in_=Pb[tb][:, 0:128],
                                                         func=AF.Copy), r=[("ps", tb)])
                op("scalar", lambda e, g=g, nk=nk: e.activation(
                    out=V(s_.KdT, [[nk, 2], [1, 128]], g * 128), in_=V(Pb[tb], [[128, 2], [1, 128]], 128), func=AF.Copy),
                   r=[("ps", tb)])

    def attention(maps, nkc, NT):
        nsub = NT // 128
        Pall = b.Pall
        for k in range(nkc + 2):
            if k < nkc:
                for m in maps:
                    sb_ = 2 * m["i"] + k % 2
                    op("tensor", lambda e, m=m, k=k, sb_=sb_: e.matmul(P[sb_][:, 0:NT], lhsT=m["kT"](k), rhs=m["q"],
                                                                       start=True, stop=True, tile_position=m["tp"]),
                       r=[m["qk"]], w=[("ps", sb_)])
            if k >= 2:
                kk = k - 2
                sl_ = kk % 3
                for m in maps:
                    po = sl_ * 1024 + m["i"] * 512
                    for sb in range(nsub):
                        op("tensor", lambda e, m=m, kk=kk, sb=sb, po=po: e.matmul(
                            P[m["O"]][:, sb * 65:(sb + 1) * 65], lhsT=PbA[:, po + sb * 128: po + (sb + 1) * 128], rhs=m["v"](kk),
                            start=(kk == 0 and sb == 0), stop=(kk == nkc - 1), skip_group_check=True),
                           r=[("pb", sl_)], w=[("ps", m["O"])])
            b.pop_until_pe()
            if 1 <= k <= nkc:
                kk = k - 1
                sl_ = kk % 3
                op("scalar", lambda e, kk=kk, sl_=sl_: e.activation(
                    out=V(PbA, [[512, 2], [1, NT]], sl_ * 1024), in_=V(Pall, [[1024, 2], [1, NT]], (kk % 2) * 512),
                    func=AF.Exp, scale=maps[0]["scale"]),
                   r=[("ps", kk % 2), ("ps", 2 + kk % 2)], w=[("pb", sl_)])

    def phaseB(l, s_, first_of_kind):
        kind = s_.kind
        ST, BT = SBT[:, kind * 8:kind * 8 + 8], mT(kind, l, 0)
        NT, T, nkc = s_.NT, s_.T, s_.nkc
        ntile, nsub = T // NT, NT // 128
        gq64, gq32 = pbl[:, 0:64], pbl[:, 128:160]
        nk = nkc * 128
        if first_of_kind:
            make_gate(kind, l, 2)
        st = {}

        def pn(t, sb):
            i = load_x(l, "B", s_.row0 + t * NT + sb * 128)
            st[(t, sb)] = norm_a(xs[i], ("xs", i), 128)

        def pt(t, sb):
            hq = hTq[sb % 2]
            norm_b(st[(t, sb)], 128, ST, BT, lambda c, hq=hq: hq[:, c * 128:(c + 1) * 128], ("hTq", sb % 2), [6])

        def pz(t, sb):
            hq = hTq[sb % 2]
            hkey = ("hTq", sb % 2)
            for (bank, c0_, n) in ((6, Q0, 512), (7, Q0 + 512, 256)):
                for k in range(8):
                    op("tensor", lambda e, bank=bank, c0_=c0_, n=n, k=k, hq=hq: e.matmul(
                        P[bank][:, 0:n], lhsT=hq[:, k * 128:(k + 1) * 128],
                        rhs=w_in_s[:, k * 2304 + c0_: k * 2304 + c0_ + n], start=(k == 0), stop=(k == 7)),
                       r=[hkey, "w_in"], w=[("ps", bank)])
            qf = Fb[1]
            qknorm(P[6][:, 0:512], ("ps", 6), 8, 64, gq64, qf[:, 0:512], "F1", Fb[0])
            qknorm(P[7][:, 0:256], ("ps", 7), 8, 32, gq32, qf[:, 512:768], "F1", Fb[0])
            if s_.rope:
                g = t * nsub + sb
                xsl[0] += 1
                ti = xsl[0] % 2
                dma("sync", tabB[ti], rope_d[g * 128:(g + 1) * 128, :], w=[("tabB", ti)], key=("tabB", ti))
                rope(qf, "F1", qbs[sb % 2], ("qb", sb % 2), [(0, 8, 64, 0, 64), (512, 8, 32, 128, 160)], tabB[ti], ("tabB", ti),
                     Fb[2], Fb[3], "F2", "F3")
            else:
                op("vector", lambda e: e.tensor_copy(out=qbs[sb % 2], in_=qf[:, 0:768]), r=["F1"], w=[("qb", sb % 2)])

        def pq(t, sb):
            QTt = QTs[t % 2]
            qb = qbs[sb % 2]
            for (b0_, nb_) in ((0, 4), (4, 2)):
                for blk in range(nb_):
                    op("tensor", lambda e, blk=blk, b0_=b0_: e.transpose(
                        out=Pb[7][:, 512 + blk * 128: 512 + (blk + 1) * 128],
                        in_=qb[:, (b0_ + blk) * 128:(b0_ + blk + 1) * 128], identity=identb),
                       r=[("qb", sb % 2)], w=[("ps", 7)])
                op("scalar", lambda e, sb=sb, b0_=b0_, nb_=nb_, QTt=QTt: e.activation(
                    out=V(QTt, [[512, nb_], [1, 128]], b0_ * 512 + sb * 128),
                    in_=V(Pb[7], [[128, nb_], [1, 128]], 512), func=AF.Copy),
                   r=[("ps", 7)], ww=[("QT", t % 2)])

        def ey(t, sb):
            t0 = t * NT
            for half in range(2):
                for c in range(8):
                    if c < 4:
                        lt = ocatT[:, c * 512 + sb * 128: c * 512 + (sb + 1) * 128]
                    elif c < 6:
                        lt = s_.ocT[:, (c - 4) * T + t0 + sb * 128: (c - 4) * T + t0 + (sb + 1) * 128]
                    else:
                        lt = ocatT[:, (c - 2) * 512 + sb * 128: (c - 2) * 512 + (sb + 1) * 128]
                    op("tensor", lambda e, half=half, c=c, lt=lt: e.matmul(
                        P[6 + half], lhsT=lt, rhs=w_out_s[:, c * 1024 + half * 512: c * 1024 + (half + 1) * 512],
                        start=(c == 0), stop=(c == 7)), r=["ocatT", "w_out"], w=[("ps", 6 + half)])
            residual(l, "B", s_.row0 + t0 + sb * 128)

        def eo(t):
            for sb in range(nsub):
                tb = b.tpbank([6, 7])
                for blk in range(6):
                    op("tensor", lambda e, blk=blk, sb=sb: e.transpose(
                        out=Pb[tb][:, blk * 128:(blk + 1) * 128], in_=otok[:, sb * 768 + blk * 128: sb * 768 + (blk + 1) * 128],
                        identity=identb), r=["otok"], w=[("ps", tb)])
                op("scalar", lambda e, sb=sb: e.activation(out=V(ocatT, [[512, 6], [1, 128]], sb * 128),
                                                           in_=V(Pb[tb], [[128, 6], [1, 128]]), func=AF.Copy),
                   r=[("ps", tb)], ww=["ocatT"])

        for sb in range(nsub):
            pn(0, sb), pt(0, sb), pz(0, sb), pq(0, sb)
        for t in range(ntile):
            t0 = t * NT
            QT = QTs[t % 2]
            b.defer = True
            if t > 0:
                for sb in range(nsub):
                    ey(t - 1, sb)
            if t + 1 < ntile:
                tn = t + 1
                pn(tn, 0), pn(tn, 1), pt(tn, 0), pt(tn, 1), pn(tn, 2), pn(tn, 3)
                pz(tn, 0), pt(tn, 2), pz(tn, 1), pq(tn, 0), pt(tn, 3), pz(tn, 2), pq(tn, 1), pz(tn, 3), pq(tn, 2), pq(tn, 3)
            b.defer = False

            def after_group():
                pass
            for j in range(4):
                gcn[0] += 1
                ob = (4, 5)
                maps = []
                for r_ in range(2):
                    maps.append(dict(
                        i=r_, S=(2 * r_, 2 * r_ + 1), O=ob[r_], tp=(64 * r_, 0), scale=0.125, qk=("QT", t % 2),
                        kT=lambda kc, r_=r_: s_.KgT[64 * r_:64 * r_ + 64, kc * 128:(kc + 1) * 128],
                        q=QT[64 * r_:64 * r_ + 64, j * 512: j * 512 + NT],
                        v=lambda kc, r_=r_: s_.Vg[:, kc * 130 + r_ * 65: kc * 130 + r_ * 65 + 65]))
                attention(maps, nkc, NT)
                for r_ in range(2):
                    h = j + 4 * r_
                    Ov = P[ob[r_]]
                    rs = small[:, 72 + r_ * 4: 72 + r_ * 4 + nsub]
                    op("vector", lambda e, Ov=Ov, rs=rs: e.reciprocal(out=rs, in_=V(Ov, [[65, nsub], [1, 1]], 64)),
                       r=[("ps", ob[r_])], w=[("rsg", r_)])
                    op("vector", lambda e, Ov=Ov, rs=rs, h=h: e.tensor_tensor(
                        out=V(otok, [[768, nsub], [1, 64]], h * 64), in0=V(Ov, [[65, nsub], [1, 64]]),
                        in1=V(rs, [[1, nsub], [0, 64]]), op=ALU.mult), r=[("ps", ob[r_]), ("rsg", r_)], ww=["otok"])
                after_group()
            for h in range(4):
                hb, hh = h // 2, h % 2
                gcn[0] += 1
                ob = (4, 5)
                maps = []
                for c_ in range(2):
                    gi = 2 * hh + c_
                    maps.append(dict(
                        i=c_, S=(2 * c_, 2 * c_ + 1), O=ob[c_], tp=(32 * gi, 0), scale=32 ** -0.5, qk=("QT", t % 2),
                        kT=lambda kc, gi=gi, hb=hb: s_.KdT[32 * gi:32 * gi + 32, hb * nk + kc * 128: hb * nk + (kc + 1) * 128],
                        q=QT[32 * gi:32 * gi + 32, (4 + hb) * 512: (4 + hb) * 512 + NT],
                        v=lambda kc, h=h: s_.Vd[:, kc * 260 + h * 65: kc * 260 + h * 65 + 65]))
                attention(maps, nkc, NT)
                if True:
                    Oa, Ob = P[ob[0]], P[ob[1]]
                    ka, kb_ = ("ps", ob[0]), ("ps", ob[1])
                    r0 = small[:, 80:80 + nsub]
                    r1 = small[:, 84:84 + nsub]
                    n = nsub * 64
                    t0_, t1_ = Fb[0][:, 0:n], Fb[0][:, 256:256 + n]
                    op("vector", lambda e, Oa=Oa: e.reciprocal(out=r0, in_=V(Oa, [[65, nsub], [1, 1]], 64)), r=[ka], w=["r0"])
                    op("vector", lambda e, Ob=Ob: e.reciprocal(out=r1, in_=V(Ob, [[65, nsub], [1, 1]], 64)), r=[kb_], w=["r1"])
                    op("vector", lambda e, Oa=Oa: e.tensor_tensor(out=V(t0_, [[64, nsub], [1, 64]]), in0=V(Oa, [[65, nsub], [1, 64]]),
                                                                  in1=V(r0, [[1, nsub], [0, 64]]), op=ALU.mult),
                       r=[ka, "r0"], w=["F0"])
                    op("vector", lambda e, Ob=Ob: e.tensor_tensor(out=V(t1_, [[64, nsub], [1, 64]]), in0=V(Ob, [[65, nsub], [1, 64]]),
                                                                  in1=V(r1, [[1, nsub], [0, 64]]), op=ALU.mult),
                       r=[kb_, "r1"], ww=["F0"])
                    od = Fb[0][:, 512:512 + n]
                    op("vector", lambda e: e.scalar_tensor_tensor(out=od, in0=t1_, scalar=lamt[:, 4 + l:5 + l], in1=t0_,
                                                                  op0=ALU.mult, op1=ALU.add), r=["F0", "lamt"], ww=["F0"])
                    sqd = Fb[0][:, 768:768 + n]
                    op("scalar", lambda e: e.activation(out=sqd, in_=od, func=AF.Square), r=["F0"], ww=["F0"])
                    ssd, sdd, rrd = small[:, 88:88 + nsub], small[:, 92:92 + nsub], small[:, 96:96 + nsub]
                    op("vector", lambda e: e.tensor_reduce(out=ssd, in_=V(sqd, [[64, nsub], [1, 64]]), axis=AX.X, op=ALU.add),
                       r=["F0"], w=["ssd"])
                    op("scalar", lambda e: e.activation(out=sdd, in_=ssd, func=AF.Ln, scale=1.0 / 64, bias=epsc),
                       r=["ssd"], w=["sdd"])
                    op("scalar", lambda e: e.activation(out=rrd, in_=sdd, func=AF.Exp, scale=-0.5), r=["sdd"], w=["rrd"])
                    op("vector", lambda e: e.tensor_tensor(out=V(od, [[64, nsub], [1, 64]]), in0=V(od, [[64, nsub], [1, 64]]),
                                                           in1=V(rrd, [[1, nsub], [0, 64]]), op=ALU.mult),
                       r=["F0", "rrd"], ww=["F0"])
                    op("vector", lambda e, h=h: e.tensor_tensor(out=V(otok, [[768, nsub], [1, 64]], 512 + h * 64),
                                                                in0=V(od, [[64, nsub], [1, 64]]),
                                                                in1=V(gsubS, [[0, nsub], [1, 64]]), op=ALU.mult),
                       r=["F0", "gsubS"], ww=["otok"])
                after_group()
            b.flush()
            eo(t)
            if t == ntile - 1:
                for sb in range(nsub):
                    ey(t, sb)

    rsl = [0]
    gcn = [0]

    def residual(l, ph, row0):
        i = load_x(l, ph, row0)
        rsl[0] += 1
        ri = 2 + rsl[0] % 2
        tr = Fb[ri]
        rk = "F%d" % ri
        for half in range(2):
            op("vector", lambda e, half=half: e.tensor_tensor(out=tr[:, half * 512:(half + 1) * 512], in0=P[6 + half],
                                                              in1=Gb[:, half * 512:(half + 1) * 512], op=ALU.mult),
               r=[("ps", 6 + half), "Gb"], w=[rk] if half == 0 else [], ww=[] if half == 0 else [rk])
        op("gpsimd", lambda e: e.tensor_tensor(out=tr, in0=tr, in1=xs[i], op=ALU.add), r=[rk, ("xs", i)], w=[rk])
        dma("sync", y[row0:row0 + 128, :], tr, r=[rk], w=[("xres", row0 // 128)], key=rk)

    def phaseC(l, s_, first_of_kind):
        kind = s_.kind
        ST, BT = SBT[:, 16 + kind * 8:16 + kind * 8 + 8], mT(kind, l, 3)
        NT, T = s_.NT, s_.T
        ntile, nsub = T // NT, NT // 128
        if first_of_kind:
            make_gate(kind, l, 5)
        nh = 0
        if ntile > 1:
            hi, nh = load_halo(l, "C", s_)
            norm_sub(xs[hi][0:nh, :], ("xs", hi), nh, ST, BT, lambda c: hTh[:, c * 16:c * 16 + nh], "hTh", [4])
        for t in range(ntile):
            t0 = t * NT
            for sb in range(nsub):
                i = load_x(l, "C", s_.row0 + t0 + sb * 128)
                norm_sub(xs[i], ("xs", i), 128, ST, BT,
                         lambda c, sb=sb: hTc[:, c * 512 + sb * 128: c * 512 + (sb + 1) * 128], "hTc", [4])
            hL = (2 * (t - 1)) if t > 0 else None
            hR = (2 * t + 1) if t < ntile - 1 else None
            for j in range(NJ):
                pa, pu = j % 2, 2 + j % 2
                for (bank, co) in ((pa, j * 256), (pu, j * 256 + 128)):
                    for k in range(8):
                        op("tensor", lambda e, bank=bank, co=co, k=k: e.matmul(
                            P[bank][:, 0:NT], lhsT=fup_s[:, k * 2 * DFF + co: k * 2 * DFF + co + 128],
                            rhs=hTc[:, k * 512: k * 512 + NT], start=(k == 0), stop=(k == 7)),
                           r=["hTc", ("fup", j // 2)], w=[("ps", bank)])
                if nh:
                    for k in range(8):
                        op("tensor", lambda e, j=j, k=k: e.matmul(
                            P[5][:, j * 16: j * 16 + nh], lhsT=fup_s[:, k * 2 * DFF + j * 256: k * 2 * DFF + j * 256 + 128],
                            rhs=hTh[:, k * 16: k * 16 + nh], start=(k == 0), stop=(k == 7)),
                           r=["hTh", ("fup", j // 2)], w=[("ps", 5)])
                ax = aext[j % 2]
                akey = ("aext", j % 2)
                op("scalar", lambda e, ax=ax, pa=pa: e.activation(out=ax[:, 1:NT + 1], in_=P[pa][:, 0:NT], func=AF.Copy),
                   r=[("ps", pa)], w=[akey])
                for (hx, col) in ((hL, 0), (hR, NT + 1)):
                    if hx is None:
                        op("vector", lambda e, ax=ax, col=col: e.memset(ax[:, col:col + 1], 0.0), ww=[akey])
                    else:
                        op("vector", lambda e, ax=ax, col=col, hx=hx, j=j: e.tensor_copy(
                            out=ax[:, col:col + 1], in_=P[5][:, j * 16 + hx: j * 16 + hx + 1]), r=[("ps", 5)], ww=[akey])
                tcv = Fb[0][:, (j % 2) * 512:(j % 2) * 512 + NT]
                tk = ("tcv", j % 2)
                sil = Fb[1][:, (j % 2) * 512:(j % 2) * 512 + NT]
                sk = ("sil", j % 2)
                w0 = PF["fcw"] + (l * NJ + j) * 3
                bo = PF["fcb"] + l * NJ + j
                op("vector", lambda e, ax=ax, tcv=tcv, w0=w0, bo=bo: e.tensor_scalar(
                    out=tcv, in0=ax[:, 0:NT], scalar1=pf[:, w0:w0 + 1], scalar2=pf[:, bo:bo + 1],
                    op0=ALU.mult, op1=ALU.add), r=[akey], w=[tk])
                for kk in (1, 2):
                    op("vector", lambda e, ax=ax, tcv=tcv, w0=w0, kk=kk: e.scalar_tensor_tensor(
                        out=tcv, in0=ax[:, kk:kk + NT], scalar=pf[:, w0 + kk:w0 + kk + 1], in1=tcv,
                        op0=ALU.mult, op1=ALU.add), r=[akey, tk], w=[tk])
                op("scalar", lambda e, tcv=tcv, sil=sil: e.activation(out=sil, in_=tcv, func=AF.Silu), r=[tk], w=[sk])
                op("vector", lambda e, sil=sil, pu=pu, j=j: e.tensor_tensor(out=fT[:, j * 512: j * 512 + NT], in0=P[pu][:, 0:NT],
                                                                            in1=sil, op=ALU.mult),
                   r=[("ps", pu), sk], ww=["fT"])
            for sb in range(nsub):
                for half in range(2):
                    for j in range(NJ):
                        op("tensor", lambda e, half=half, j=j, sb=sb: e.matmul(
                            P[6 + half], lhsT=fT[:, j * 512 + sb * 128: j * 512 + (sb + 1) * 128],
                            rhs=fdn_s[:, j * 1024 + half * 512: j * 1024 + (half + 1) * 512],
                            start=(j == 0), stop=(j == NJ - 1)), r=["fT", ("fdn", j // 11)], w=[("ps", 6 + half)])
                residual(l, "C", s_.row0 + t0 + sb * 128)

    try:
      chk(1)
      for l in range(L):
          lam_init = 0.8 - 0.6 * math.exp(-0.3 * l)
          wi = w_in[l].rearrange("(k p) n -> p k n", p=128)
          wis = w_in_s.rearrange("p (k n) -> p k n", k=8)
          for c0_ in range(0, 2304, 768):
              dma("gpsimd", wis[:, :, c0_:c0_ + 768], wi[:, :, c0_:c0_ + 768], w=[] if c0_ else ["w_in"], key="w_in")
          b.lastw["w_in"] = {"w_in": b.dval["w_in"]}
          wo = w_out[l].rearrange("(k p) n -> p k n", p=128)
          dma("gpsimd", w_out_s.rearrange("p (k n) -> p k n", k=8), wo, w=["w_out"], key="w_out")
          dma("sync", pbl, AP(pb_d.tensor, l * PB_N, [[0, 128], [1, PB_N]]), w=["pbl"], key="pbl")
          for kind in range(2):
              op("vector", lambda e, kind=kind: e.scalar_tensor_tensor(
                  out=SBT[:, kind * 8:kind * 8 + 8], in0=mT(kind, l, 1), scalar=1.0, in1=pfv("n1g", l * 8, 8),
                  op0=ALU.add, op1=ALU.mult), r=["modsT", "pf"], ww=["SBT"])
              op("vector", lambda e, kind=kind: e.scalar_tensor_tensor(
                  out=SBT[:, 16 + kind * 8:16 + kind * 8 + 8], in0=mT(kind, l, 4), scalar=1.0, in1=pfv("n2g", l * 8, 8),
                  op0=ALU.add, op1=ALU.mult), r=["modsT", "pf"], ww=["SBT"])
          dl = pbl[:, 256:384]
          pr = small[:, 128:192]
          op("vector", lambda e: e.tensor_tensor(out=V(pr, [[32, 2], [1, 32]]), in0=V(dl, [[64, 2], [1, 32]]),
                                                 in1=V(dl, [[64, 2], [1, 32]], 32), op=ALU.mult), r=["pbl"], w=["pr"])
          op("vector", lambda e: e.tensor_reduce(out=lamt[:, 0:2], in_=V(pr, [[32, 2], [1, 32]]), axis=AX.X, op=ALU.add),
             r=["pr"], w=["lam0"])
          op("scalar", lambda e: e.activation(out=lamt[:, 2:4], in_=lamt[:, 0:2], func=AF.Exp), r=["lam0"], w=["lam1"])
          op("vector", lambda e, li=lam_init: e.scalar_tensor_tensor(
              out=lamt[:, 4 + l:5 + l], in0=lamt[:, 3:4], scalar=-li, in1=lamt[:, 2:3], op0=ALU.add, op1=ALU.subtract),
             r=["lam1"], w=["lamt"])
          op("vector", lambda e, li=lam_init: e.tensor_scalar(out=gsubS, in0=pbl[:, 192:256], scalar1=1.0 - li, scalar2=None,
                                                              op0=ALU.mult), r=["pbl"], w=["gsubS"])
          for s_ in (seqs[0], seqs[1]):
              nk_ = s_.nkc
              op("vector", lambda e, a=V(s_.Vg, [[65, nk_ * 2], [1, 1]], 64): e.memset(a, 1.0))
              op("vector", lambda e, a=V(s_.Vd, [[65, nk_ * 4], [1, 1]], 64): e.memset(a, 1.0))
          b.barrier()
          chk(2)
          prevk = -1
          for s_ in seqs:
              phaseA(l, s_)
              b.barrier()
              chk(3)
              phaseB(l, s_, s_.kind != prevk)
              prevk = s_.kind
              b.barrier()
          fu = f_up[l].rearrange("(k p) n -> p k n", p=128)
          fus = fup_s.rearrange("p (k n) -> p k n", k=8)
          for pc in range(NJ // 2):
              dma("gpsimd", fus[:, :, pc * 512:(pc + 1) * 512], fu[:, :, pc * 512:(pc + 1) * 512], w=[("fup", pc)],
                  key=("fup", pc))
          fd = f_dn[l].rearrange("(j p) n -> p j n", p=128)
          fds = fdn_s.rearrange("p (j n) -> p j n", j=NJ)
          for pc in range(2):
              dma("gpsimd", fds[:, pc * 11:(pc + 1) * 11, :], fd[:, pc * 11:(pc + 1) * 11, :], w=[("fdn", pc)],
                  key=("fdn", pc))
          prevk = -1
          for s_ in seqs:
              phaseC(l, s_, s_.kind != prevk)
              prevk = s_.kind
          b.barrier()
    except Stop:
        pass
    b.finish()
    return nc


_NC_CACHE = {}


def _rope_table(TS):
    n_rows = TS // GRID_W
    row = np.repeat(np.arange(n_rows), GRID_W).astype(np.float32)
    col = np.tile(np.arange(GRID_W), n_rows).astype(np.float32)
    out = []
    for dim in (64, 32):
        nf = dim // 4
        freqs = (np.float32(10000.0) ** (-np.arange(nf, dtype=np.float32) / np.float32(nf))).astype(np.float32)
        ar = row[:, None] * freqs[None, :]
        ac = col[:, None] * freqs[None, :]
        ang = np.concatenate([ar, ar, ac, ac], axis=-1).astype(np.float32)
        cos, sin = np.cos(ang), np.sin(ang)
        sgn = np.concatenate([-np.ones(nf), np.ones(nf), -np.ones(nf), np.ones(nf)]).astype(np.float32)
        out += [cos.astype(np.float32), (sin * sgn[None, :]).astype(np.float32)]
    return np.ascontiguousarray(np.concatenate(out, axis=-1), dtype=np.float32)


def _fm(v):
    v = np.asarray(v, dtype=np.float32)
    n = v.shape[-1] // 128
    r = v.reshape(v.shape[:-1] + (n, 128))
    return np.moveaxis(r, -1, 0)


def run(cfg, inp, n_cores):
    L, TS, TP, NP = cfg.L, cfg.TS, cfg.TP, cfg.NP
    f = lambda k: np.asarray(inp[k], dtype=np.float32)
    key = (cfg.TS, cfg.TP, cfg.NP, cfg.L, cfg.PAST)
    if key not in _NC_CACHE:
        _NC_CACHE[key] = build(cfg)
    nc = _NC_CACHE[key]
    PF = pf_layout(L)
    qg = [np.arange(h * 64, (h + 1) * 64) for h in (0, 4, 1, 5, 2, 6, 3, 7)]
    o_qg, o_kg, o_vg, o_cb, o_cc, o_cu, o_qd, o_kd, o_vd = 0, 512, 640, 768, 1024, 1280, 1536, 1792, 2048
    perm = np.concatenate([np.arange(o_kg, o_kg + 128), np.arange(o_kd, o_kd + 256), np.arange(o_vg, o_vg + 128),
                           np.arange(o_vd, o_vd + 256), np.arange(o_cb, o_cb + 256), np.arange(o_cc, o_cc + 256),
                           np.arange(o_cu, o_cu + 256), np.concatenate(qg), np.arange(o_qd, o_qd + 256)])
    w_in_p = np.ascontiguousarray(f("w_in")[:, :, perm])
    fu = f("ffn_up")
    fu_p = np.ascontiguousarray(
        np.stack([fu[:, :, :DFF].reshape(L, D, NJ, 128), fu[:, :, DFF:].reshape(L, D, NJ, 128)], axis=3).reshape(L, D, 2 * DFF))
    pb = np.ascontiguousarray(np.concatenate([f("gqa_qn_g"), f("gqa_kn_g"), f("diff_qn_g"), f("diff_kn_g"), f("diff_subln_g"),
                                              f("diff_lambda").reshape(L, 128)], axis=1))
    identb = np.eye(128, dtype=np.float32).astype(ml_dtypes.bfloat16)
    cst32 = np.ascontiguousarray(np.concatenate([np.eye(128, dtype=np.float32), np.ones((128, 128), np.float32)], axis=1))
    rope = _rope_table(TS)
    shared = dict(w_mod=f("w_mod"), w_in=w_in_p, w_out=f("w_out"), f_up=fu_p, f_dn=f("ffn_down"), rope=rope, identb=identb,
                  cst32=cst32, pb=pb)
    xp, xsm = f("x_prompt"), f("x_sample")
    in_maps = []
    for c in range(n_cores):
        pfa = np.zeros((128, PF["_n"]), np.float32)
        pfa[:, PF["cond"]:PF["cond"] + 8] = _fm(f("c")[c])
        pfa[:, PF["cond"] + 8:PF["cond"] + 16] = _fm(f("c_ctx"))
        pfa[:, PF["bmod"]:PF["bmod"] + L * 48] = _fm(f("b_mod")).reshape(128, L * 48)
        pfa[:, PF["n1g"]:PF["n1g"] + L * 8] = _fm(f("norm1_g")).reshape(128, L * 8)
        pfa[:, PF["n2g"]:PF["n2g"] + L * 8] = _fm(f("norm2_g")).reshape(128, L * 8)
        cw = _fm(f("conv_w"))
        pfa[:, PF["cw"]:PF["cw"] + L * 6] = np.transpose(cw, (0, 1, 3, 2)).reshape(128, L * 6)
        pfa[:, PF["cb"]:PF["cb"] + L * 2] = _fm(f("conv_b")).reshape(128, L * 2)
        fcw = _fm(f("ffn_conv_w"))
        pfa[:, PF["fcw"]:PF["fcw"] + L * NJ * 3] = np.transpose(fcw, (0, 1, 3, 2)).reshape(128, L * NJ * 3)
        pfa[:, PF["fcb"]:PF["fcb"] + L * NJ] = _fm(f("ffn_conv_b")).reshape(128, L * NJ)
        xall = np.ascontiguousarray(np.concatenate([xsm[c], xp[c * NP:(c + 1) * NP].reshape(NP * TP, D)], axis=0))
        m = dict(shared)
        m.update(xall=xall, pf=pfa,
                 ck=np.ascontiguousarray(f("cache_gqa_k")[c].reshape(L, cfg.PAST, 128)),
                 cv=np.ascontiguousarray(f("cache_gqa_v")[c].reshape(L, cfg.PAST, 128)),
                 cdk=np.ascontiguousarray(f("cache_diff_k")[c].reshape(L, cfg.PAST, 256)),
                 cdv=np.ascontiguousarray(f("cache_diff_v")[c].reshape(L, cfg.PAST, 256)))
        in_maps.append(m)
    res = run_bass_kernel_spmd(nc, in_maps, core_ids=list(range(n_cores)))
    R = res.results
    ys = np.stack([R[c]["y"][:TS] for c in range(n_cores)], axis=0)
    yp = np.concatenate([R[c]["y"][TS:].reshape(NP, TP, D) for c in range(n_cores)], axis=0)
    gk = np.concatenate([R[c]["ngk"] for c in range(n_cores)], axis=0).reshape(n_cores * NP, L, TP, 2, 64)
    gv = np.concatenate([R[c]["ngv"] for c in range(n_cores)], axis=0).reshape(n_cores * NP, L, TP, 2, 64)
    dk = np.concatenate([R[c]["ndk"] for c in range(n_cores)], axis=0).reshape(n_cores * NP, L, TP, 4, 2, 32)
    dv = np.concatenate([R[c]["ndv"] for c in range(n_cores)], axis=0).reshape(n_cores * NP, L, TP, 4, 64)
    return (yp.astype(np.float32), ys.astype(np.float32), gk.astype(np.float32), gv.astype(np.float32),
            dk.astype(np.float32), dv.astype(np.float32))


def kernel(**inputs):
    cfg = Cfg()
    return run(cfg, inputs, 8)
```

```python
import math
import numpy as np
import ml_dtypes
import concourse.bass as bass
import concourse.mybir as mybir
from concourse.bass_utils import run_bass_kernel_spmd
from concourse.ap import AP
from contextlib import ExitStack

F32 = mybir.dt.float32
BF16 = mybir.dt.bfloat16
U8 = mybir.dt.uint8
ALU = mybir.AluOpType
AF = mybir.ActivationFunctionType
AX = mybir.AxisListType

D = 1024
GRID_W = 64
DFF = 2816
NJ = DFF // 128
EPS = 1e-6
COMPUTE = ("tensor", "vector", "scalar", "gpsimd")
ALLENG = ("sync", "tensor", "vector", "scalar", "gpsimd")


class Cfg:
    def __init__(self, TS=4096, TP=256, NP=4, L=4, PAST=256):
        self.TS, self.TP, self.NP, self.L, self.PAST = TS, TP, NP, L, PAST
        self.NTOK = TS + NP * TP
        self.stop = 99


def pf_layout(L):
    o = {}
    c = 0
    for name, n in (("cond", 16), ("bmod", L * 48), ("n1g", L * 8), ("n2g", L * 8), ("cw", L * 6),
                    ("cb", L * 2), ("fcw", L * NJ * 3), ("fcb", L * NJ)):
        o[name] = c
        c += n
    o["_n"] = c
    return o


PB_N = 64 + 64 + 32 + 32 + 64 + 128


class Rec:
    def __getattr__(self, name):
        def f(*a, **k):
            self.call = (name, a, k)
            return self
        return f


class Bld:
    def __init__(self, cfg):
        self.cfg = cfg
        self.nc = bass.Bass("TRN2", target_bir_lowering=False)
        self.es = ExitStack()
        self.prog = {e: [] for e in ALLENG}
        self.cnt = {e: 0 for e in COMPUTE}
        self.semh = {}
        self.dval = {}
        self.seen = {e: {} for e in ALLENG}
        self.lastw = {}
        self.readers = {}
        self.arena_off = 0
        self.ARENA = 212480
        self.arena = self.es.enter_context(self.nc.sbuf_tensor("arena", [128, self.ARENA], U8))
        self.P = []
        self.Pb = []
        pall = self.es.enter_context(self.nc.psum_tensor("psall", [128, 4096], F32))
        self.Pall = pall[:, :]
        pallb = pall[:, :].bitcast(BF16)
        for i in range(8):
            self.P.append(self.Pall[:, i * 512:(i + 1) * 512])
            self.Pb.append(pallb[:, i * 1024:(i + 1) * 1024])
        self.tpi = 0
        self.defer = False
        self.emitting = False
        import collections
        self.q = collections.deque()

    def buf(self, cols, dt, at=None):
        nb = cols * (4 if dt == F32 else 2)
        nb = (nb + 31) // 32 * 32
        if at is None:
            at = self.arena_off
            self.arena_off += nb
            assert self.arena_off <= self.ARENA, ("arena overflow", self.arena_off)
        return self.arena[:, at:at + nb].bitcast(dt)[:, 0:cols]

    def sem(self, key):
        if key not in self.semh:
            self.semh[key] = self.es.enter_context(self.nc.semaphore("s%d" % len(self.semh)))
        return self.semh[key]

    def _wait(self, eng, deps):
        need = {}
        for d in deps:
            for k, v in d.items():
                if need.get(k, 0) < v:
                    need[k] = v
        for k, v in need.items():
            if k == eng:
                if eng == "tensor":
                    continue
                if v < self.cnt[eng] - 1:
                    continue
            if self.seen[eng].get(k, 0) >= v:
                continue
            self.seen[eng][k] = v
            sem = self.sem(k)
            self.prog[eng].append(lambda e, sem=sem, v=v: e.wait_ge(sem, v))

    def _deps(self, r, w, ww):
        deps = []
        for k in r:
            if k in self.lastw:
                deps.append(self.lastw[k])
        for k in w:
            if k in self.lastw:
                deps.append(self.lastw[k])
            if k in self.readers:
                deps.append(self.readers[k])
        for k in ww:
            if k in self.readers:
                deps.append(self.readers[k])
        return deps

    def _reg(self, tokk, tokv, r, w, ww):
        for k in r:
            d = self.readers.setdefault(k, {})
            d[tokk] = tokv
        for k in w:
            self.lastw[k] = {tokk: tokv}
            self.readers[k] = {}
        for k in ww:
            d = self.lastw.setdefault(k, {})
            d[tokk] = tokv

    def pop_until_pe(self):
        while self.q:
            ent = self.q.popleft()
            self.emitting = True
            if ent[0] == "op":
                self.op(*ent[1:])
            else:
                self.dma(*ent[1:])
            self.emitting = False
            if ent[0] == "op" and ent[1] == "tensor":
                return

    def flush(self):
        while self.q:
            self.pop_until_pe()

    def op(self, eng, fn, r=(), w=(), ww=()):
        if self.defer and not self.emitting:
            rec = Rec()
            fn(rec)
            call = rec.call
            self.q.append(("op", eng, (lambda e, call=call: getattr(e, call[0])(*call[1], **call[2])), tuple(r), tuple(w), tuple(ww)))
            return
        self._wait(eng, self._deps(r, w, ww))
        self.cnt[eng] += 1
        sem = self.sem(eng)
        rec = Rec()
        fn(rec)
        name, a, k = rec.call
        self.prog[eng].append(lambda e, name=name, a=a, k=k, sem=sem: getattr(e, name)(*a, **k).then_inc(sem, 1))
        self._reg(eng, self.cnt[eng], r, w, ww)

    def dma(self, q, out, in_, r=(), w=(), key=None):
        if self.defer and not self.emitting:
            self.q.append(("dma", q, out, in_, tuple(r), tuple(w), key))
            return
        self._wait(q, self._deps(r, w, ()))
        self.dval[key] = self.dval.get(key, 0) + 16
        sem = self.sem(key)
        self.prog[q].append(lambda e, out=out, in_=in_, sem=sem: e.dma_start(out=out, in_=in_).then_inc(sem, 16))
        self._reg(key, self.dval[key], r, w, ())

    def barrier(self):
        toks = {e: self.cnt[e] for e in COMPUTE if self.cnt[e] > 0}
        toks.update(self.dval)
        for e in ALLENG:
            for k, v in toks.items():
                if k == e:
                    continue
                if self.seen[e].get(k, 0) >= v:
                    continue
                self.seen[e][k] = v
                sem = self.sem(k)
                self.prog[e].append(lambda en, sem=sem, v=v: en.wait_ge(sem, v))

    def finish(self):
        self.barrier()
        nc = self.nc
        prog = self.prog
        with nc.Block() as block:
            for name in ALLENG:
                def mk(name):
                    def body(e):
                        for f in prog[name]:
                            f(e)
                    return body
                getattr(block, name)(mk(name))

    def tpbank(self, banks):
        self.tpi += 1
        return banks[self.tpi % len(banks)]


def V(ap, pat, off=0):
    return AP(ap.tensor, ap.offset + off, [list(ap.ap[0])] + [list(x) for x in pat])


def build(cfg):
    b = Bld(cfg)
    nc = b.nc
    L, TS, TP, NP, PAST, NTOK = cfg.L, cfg.TS, cfg.TP, cfg.NP, cfg.PAST, cfg.NTOK
    PF = pf_layout(L)
    NKC_S = (TS + PAST) // 128
    dt_ = nc.dram_tensor
    xall = dt_("xall", [NTOK, D], F32, kind="ExternalInput").ap()
    ck = dt_("ck", [L, PAST, 128], F32, kind="ExternalInput").ap()
    cv = dt_("cv", [L, PAST, 128], F32, kind="ExternalInput").ap()
    cdk = dt_("cdk", [L, PAST, 256], F32, kind="ExternalInput").ap()
    cdv = dt_("cdv", [L, PAST, 256], F32, kind="ExternalInput").ap()
    pf_d = dt_("pf", [128, PF["_n"]], F32, kind="ExternalInput").ap()
    pb_d = dt_("pb", [L, PB_N], F32, kind="ExternalInput").ap()
    w_mod = dt_("w_mod", [L, D, 6 * D], F32, kind="ExternalInput").ap()
    w_in = dt_("w_in", [L, D, 2304], F32, kind="ExternalInput").ap()
    w_out = dt_("w_out", [L, D, D], F32, kind="ExternalInput").ap()
    f_up = dt_("f_up", [L, D, 2 * DFF], F32, kind="ExternalInput").ap()
    f_dn = dt_("f_dn", [L, DFF, D], F32, kind="ExternalInput").ap()
    rope_d = dt_("rope", [TS, 192], F32, kind="ExternalInput").ap()
    idb_d = dt_("identb", [128, 128], BF16, kind="ExternalInput").ap()
    c32_d = dt_("cst32", [128, 256], F32, kind="ExternalInput").ap()
    y = dt_("y", [NTOK, D], F32, kind="ExternalOutput").ap()
    ngk = dt_("ngk", [NP, L, TP, 128], F32, kind="ExternalOutput").ap()
    ngv = dt_("ngv", [NP, L, TP, 128], F32, kind="ExternalOutput").ap()
    ndk = dt_("ndk", [NP, L, TP, 256], F32, kind="ExternalOutput").ap()
    ndv = dt_("ndv", [NP, L, TP, 256], F32, kind="ExternalOutput").ap()

    identb = b.buf(128, BF16)
    cst32 = b.buf(256, F32)
    ident32, ones32 = cst32[:, 0:128], cst32[:, 128:256]
    pf = b.buf(PF["_n"], F32)
    pbl = b.buf(PB_N, F32)
    modsT = b.buf(2 * L * 48, F32)
    condb = b.buf(16, BF16)
    SBT = b.buf(32, F32)
    lamt = b.buf(16, F32)
    gsubS = b.buf(64, F32)
    Gb = b.buf(1024, F32)
    epsc = b.buf(1, F32)
    small = b.buf(256, F32)
    diag = b.buf(128, F32)
    xs = [b.buf(1024, F32) for _ in range(2)]
    xn = [b.buf(1024, BF16) for _ in range(2)]
    Fb = [b.buf(1024, F32) for _ in range(4)]
    hTh = b.buf(8 * 16, BF16)
    X0 = b.arena_off
    XSZ = 34880
    b.arena_off += XSZ
    BIG0 = b.arena_off
    BIGSZ = 132 * 1024
    b.arena_off += BIGSZ
    assert b.arena_off <= b.ARENA, b.arena_off

    class Lay:
        def __init__(self, base):
            self.o = base

        def buf(self, cols, dt):
            nb = (cols * (4 if dt == F32 else 2) + 31) // 32 * 32
            r = b.buf(cols, dt, at=self.o)
            self.o += nb
            return r

    rowsb = b.buf(1024, F32, at=X0)
    la = Lay(X0)
    hTa = [la.buf(8 * 512, BF16) for _ in range(2)]
    tab = [la.buf(192, F32) for _ in range(2)]
    v32 = [la.buf(384, F32) for _ in range(2)]
    kb = la.buf(384, BF16)
    pext = la.buf(514, F32)
    assert la.o <= X0 + XSZ
    lb = Lay(X0)
    hTq = [lb.buf(8 * 128, BF16) for _ in range(2)]
    tabB = [lb.buf(192, F32) for _ in range(2)]
    qbs = [lb.buf(768, BF16) for _ in range(2)]
    QT = lb.buf(6 * 512, BF16)
    PbA = lb.buf(3 * 2 * 512, BF16)
    otok = lb.buf(4 * 768, BF16)
    ocatT = lb.buf(6 * 512, BF16)
    assert lb.o <= X0 + XSZ
    lc = Lay(X0)
    hTc = lc.buf(8 * 512, BF16)
    aext = [lc.buf(514, F32) for _ in range(2)]
    fT = lc.buf(NJ * 512, BF16)
    assert lc.o <= X0 + XSZ, lc.o - X0
    lg = Lay(BIG0)
    w_in_s = lg.buf(8 * 2304, BF16)
    w_out_s = lg.buf(8 * 1024, BF16)

    class Seq:
        pass
    seqs = []
    sq = Seq()
    sq.kind, sq.row0, sq.T, sq.NT, sq.rope, sq.nkc, sq.pi = 0, 0, TS, 512, True, NKC_S, -1
    sq.KgT = lg.buf(NKC_S * 128, BF16)
    sq.KdT = lg.buf(2 * NKC_S * 128, BF16)
    sq.Vg = lg.buf(NKC_S * 2 * 65, BF16)
    sq.Vd = lg.buf(NKC_S * 4 * 65, BF16)
    sq.ocT = lg.buf(2 * TS, BF16)
    seqs.append(sq)
    pq = Seq()
    pq.KgT = lg.buf(TP, BF16)
    pq.KdT = lg.buf(2 * TP, BF16)
    pq.Vg = lg.buf((TP // 128) * 2 * 65, BF16)
    pq.Vd = lg.buf((TP // 128) * 4 * 65, BF16)
    pq.ocT = lg.buf(2 * TP, BF16)
    for pi in range(NP):
        s_ = Seq()
        s_.__dict__.update(pq.__dict__)
        s_.kind, s_.row0, s_.T, s_.NT, s_.rope, s_.nkc, s_.pi = 1, TS + pi * TP, TP, TP, False, TP // 128, pi
        seqs.append(s_)
    ckb = lg.buf(2 * 384, BF16)
    QT2 = lg.buf(6 * 512, BF16)
    QTs = [QT, QT2]
    assert lg.o <= BIG0 + BIGSZ, lg.o - BIG0
    lf = Lay(BIG0)
    fup_s = lf.buf(8 * 2 * DFF, BF16)
    fdn_s = lf.buf(NJ * 1024, BF16)
    assert lf.o <= BIG0 + BIGSZ, lf.o - BIG0
    wst = [b.buf(8 * 512, BF16, at=BIG0 + i * 8192) for i in range(2)]

    P, Pb = b.P, b.Pb
    op, dma = b.op, b.dma

    def pfv(name, off, n):
        return pf[:, PF[name] + off: PF[name] + off + n]

    dma("sync", identb, idb_d, w=["identb"], key="setup")
    dma("sync", cst32, c32_d, w=["cst32"], key="setup")
    dma("sync", pf, pf_d, w=["pf"], key="setup")
    op("vector", lambda e: e.memset(epsc, EPS), w=["epsc"])
    b.barrier()
    op("scalar", lambda e: e.activation(out=condb, in_=pfv("cond", 0, 16), func=AF.Silu), w=["condb"])
    one11 = ones32[0:1, 0:1]
    bi = 0
    for l in range(L):
        wv = w_mod[l].rearrange("(k p) n -> p k n", p=128)
        for blk in range(12):
            slot = bi % 2
            bi += 1
            dma("gpsimd", wst[slot].rearrange("p (k n) -> p k n", k=8), wv[:, :, blk * 512:(blk + 1) * 512],
                w=[("wst", slot)], key=("wst", slot))
            for kind in range(2):
                for k in range(8):
                    op("tensor", lambda e, kind=kind, k=k, slot=slot: e.matmul(
                        P[kind][0:1, 0:512], lhsT=condb[:, kind * 8 + k: kind * 8 + k + 1],
                        rhs=wst[slot][:, k * 512:(k + 1) * 512], start=(k == 0), stop=(k == 7)),
                       r=[("wst", slot), "condb"], w=[("ps", kind)])
                op("scalar", lambda e, kind=kind: e.activation(out=rowsb[0:1, kind * 512:(kind + 1) * 512],
                                                               in_=P[kind][0:1, 0:512], func=AF.Copy),
                   r=[("ps", kind)], w=[("row", kind)])
                for c in range(4):
                    op("tensor", lambda e, kind=kind, c=c: e.matmul(
                        P[2 + kind][:, c:c + 1], lhsT=rowsb[0:1, kind * 512 + c * 128: kind * 512 + (c + 1) * 128],
                        rhs=one11, start=True, stop=True),
                       r=[("row", kind), "cst32"], w=[("ps", 2 + kind)])
                mo = (kind * L + l) * 48 + blk * 4
                op("vector", lambda e, kind=kind, mo=mo, l=l, blk=blk: e.tensor_tensor(
                    out=modsT[:, mo:mo + 4], in0=P[2 + kind][:, 0:4], in1=pfv("bmod", l * 48 + blk * 4, 4), op=ALU.add),
                   r=[("ps", 2 + kind), "pf"], ww=["modsT"])
    b.barrier()

    def mT(kind, l, i):
        o = (kind * L + l) * 48 + i * 8
        return modsT[:, o:o + 8]

    sl = [0]

    def norm_a(xap, xkey, np_):
        sl[0] += 1
        i = sl[0] % 2
        ss, sd, rs = small[0:np_, i:i + 1], small[0:np_, 2 + i:3 + i], small[0:np_, 4 + i:5 + i]
        xnb = xn[i]
        op("scalar", lambda e: e.activation(out=xnb[0:np_, :], in_=xap, func=AF.Square, accum_out=ss),
           r=[xkey], w=[("xn", i), ("ss", i)])
        op("scalar", lambda e: e.activation(out=sd, in_=ss, func=AF.Ln, scale=1.0 / D, bias=epsc[0:np_, :]),
           r=[("ss", i), "epsc"], w=[("sd", i)])
        op("scalar", lambda e: e.activation(out=rs, in_=sd, func=AF.Exp, scale=-0.5), r=[("sd", i)], w=[("rs", i)])
        op("vector", lambda e: e.tensor_scalar(out=xnb[0:np_, :], in0=xap, scalar1=rs, scalar2=None, op0=ALU.mult),
           r=[xkey, ("rs", i)], w=[("xn", i)])
        return i

    def norm_b(i, np_, ST, BT, dst, dkey, tpbanks):
        xnb = xn[i]
        tb = b.tpbank(tpbanks)
        for c in range(8):
            op("tensor", lambda e, c=c: e.transpose(out=Pb[tb][:, c * 128:c * 128 + np_],
                                                    in_=xnb[0:np_, c * 128:(c + 1) * 128],
                                                    identity=identb[0:np_, 0:np_]),
               r=[("xn", i), "identb"], w=[("ps", tb)])
        for c in range(8):
            op("vector", lambda e, c=c: e.tensor_scalar(out=dst(c), in0=Pb[tb][:, c * 128:c * 128 + np_],
                                                        scalar1=ST[:, c:c + 1], scalar2=BT[:, c:c + 1],
                                                        op0=ALU.mult, op1=ALU.add),
               r=[("ps", tb), "SBT", "modsT"], ww=[dkey])

    def norm_sub(xap, xkey, np_, ST, BT, dst, dkey, tpbanks):
        i = norm_a(xap, xkey, np_)
        norm_b(i, np_, ST, BT, dst, dkey, tpbanks)

    def qknorm(src, skey, G, hd, gain, dst, dkey, zs, zkey="F0"):
        n = G * hd
        sl[0] += 1
        i = sl[0] % 2
        ssq, sdq, rq = small[:, 8 + i * 8:8 + i * 8 + G], small[:, 24 + i * 8:24 + i * 8 + G], small[:, 40 + i * 8:40 + i * 8 + G]
        zv = zs[:, 0:n]
        op("scalar", lambda e: e.activation(out=zv, in_=src, func=AF.Square), r=[skey], w=[zkey])
        op("vector", lambda e: e.tensor_reduce(out=ssq, in_=V(zv, [[hd, G], [1, hd]]), axis=AX.X, op=ALU.add),
           r=[zkey], w=[("ssq", i)])
        op("scalar", lambda e: e.activation(out=sdq, in_=ssq, func=AF.Ln, scale=1.0 / hd, bias=epsc),
           r=[("ssq", i)], w=[("sdq", i)])
        op("scalar", lambda e: e.activation(out=rq, in_=sdq, func=AF.Exp, scale=-0.5), r=[("sdq", i)], w=[("rq", i)])
        op("vector", lambda e: e.tensor_tensor(out=V(dst, [[hd, G], [1, hd]]), in0=V(src, [[hd, G], [1, hd]]),
                                               in1=V(rq, [[1, G], [0, hd]]), op=ALU.mult),
           r=[skey, ("rq", i)], ww=[dkey])
        op("vector", lambda e: e.tensor_tensor(out=V(dst, [[hd, G], [1, hd]]), in0=V(dst, [[hd, G], [1, hd]]),
                                               in1=V(gain, [[0, G], [1, hd]]), op=ALU.mult),
           r=["pbl", dkey], ww=[dkey])

    def rope(src, skey, dst, dkey, parts, tb, tkey, t1, t2, k1, k2):
        for (off, G, hd, co, so) in parts:
            q = hd // 4
            n = G * hd
            op("vector", lambda e, off=off, G=G, hd=hd, co=co, n=n: e.tensor_tensor(
                out=V(t1[:, off:off + n], [[hd, G], [1, hd]]), in0=V(src[:, off:off + n], [[hd, G], [1, hd]]),
                in1=V(tb[:, co:co + hd], [[0, G], [1, hd]]), op=ALU.mult), r=[skey, tkey], ww=[k1])
            for pr in range(2):
                op("vector", lambda e, off=off, G=G, hd=hd, so=so, q=q, pr=pr: e.tensor_tensor(
                    out=V(t2, [[hd, G], [2 * q, 2], [1, q]], off + pr * q),
                    in0=V(src, [[hd, G], [2 * q, 2], [1, q]], off + (1 - pr) * q),
                    in1=V(tb, [[0, G], [2 * q, 2], [1, q]], so + pr * q), op=ALU.mult),
                   r=[skey, tkey], ww=[k2])
        W = sum(p[1] * p[2] for p in parts)
        o0 = parts[0][0]
        op("vector", lambda e: e.tensor_tensor(out=dst[:, o0:o0 + W], in0=t1[:, o0:o0 + W], in1=t2[:, o0:o0 + W], op=ALU.add),
           r=[k1, k2], w=[dkey])

    def make_gate(kind, l, gi):
        gT = mT(kind, l, gi)
        for c in range(8):
            op("vector", lambda e, c=c: e.tensor_scalar(out=diag, in0=ident32, scalar1=gT[:, c:c + 1], scalar2=None,
                                                        op0=ALU.mult), r=["cst32", "modsT"], w=["diag"])
            bk = 6 + c // 4
            op("tensor", lambda e, c=c, bk=bk: e.matmul(P[bk][:, (c % 4) * 128:(c % 4 + 1) * 128], lhsT=ones32,
                                                        rhs=diag, start=True, stop=True),
               r=["diag", "cst32"], w=[("ps", bk)])
        for h in range(2):
            op("scalar", lambda e, h=h: e.activation(out=Gb[:, h * 512:(h + 1) * 512], in_=P[6 + h], func=AF.Copy),
               r=[("ps", 6 + h)], ww=["Gb"])

    def xsrc(l, ph):
        return xall if (l == 0 and ph != "C") else y

    xsl = [0]

    def load_x(l, ph, row0, npart=128):
        xsl[0] += 1
        i = xsl[0] % 2
        dma("sync", xs[i][0:npart, :], xsrc(l, ph)[row0:row0 + npart, :], r=[("xres", row0 // 128)], w=[("xs", i)],
            key=("xs", i))
        return i

    def load_halo(l, ph, s_):
        nb = s_.T // s_.NT - 1
        xsl[0] += 1
        i = xsl[0] % 2
        src = xsrc(l, ph)
        for bb in range(nb):
            r0 = s_.row0 + (bb + 1) * s_.NT - 1
            dma("sync", xs[i][2 * bb:2 * bb + 2, :], src[r0:r0 + 2, :],
                r=[("xres", r0 // 128), ("xres", (r0 + 1) // 128)] if bb > 0 else
                [("xres", r0 // 128), ("xres", (r0 + 1) // 128)], w=[("xs", i)] if bb == 0 else [], key=("xs", i))
        b.lastw[("xs", i)] = {("xs", i): b.dval[("xs", i)]}
        return i, 2 * nb

    class Stop(Exception):
        pass

    def chk(n):
        if cfg.stop <= n:
            raise Stop()

    A0, CB, CC, CU, Q0 = 0, 768, 1024, 1280, 1536

    def phaseA(l, s_):
        kind = s_.kind
        ST, BT = SBT[:, kind * 8:kind * 8 + 8], mT(kind, l, 0)
        NT, T = s_.NT, s_.T
        ntile, nsub = T // NT, NT // 128
        gk64, gk32 = pbl[:, 64:128], pbl[:, 160:192]
        nh = 0
        if ntile > 1:
            hi, nh = load_halo(l, "A", s_)
            norm_sub(xs[hi][0:nh, :], ("xs", hi), nh, ST, BT, lambda c: hTh[:, c * 16:c * 16 + nh], "hTh", [0, 1])
        chk(2.1)
        def tile_norm(t):
            hTt = hTa[t % 2]
            for sb in range(nsub):
                i = load_x(l, "A", s_.row0 + t * NT + sb * 128)
                norm_sub(xs[i], ("xs", i), 128, ST, BT,
                         lambda c, sb=sb: hTt[:, c * 512 + sb * 128: c * 512 + (sb + 1) * 128], ("hT", t % 2), [1])

        tile_norm(0)
        for t in range(ntile):
            t0 = t * NT
            hT = hTa[t % 2]
            hTk = ("hT", t % 2)
            if t + 1 < ntile:
                b.defer = True
                tile_norm(t + 1)
                b.defer = False
            hL = (2 * (t - 1)) if t > 0 else None
            hR = (2 * t + 1) if t < ntile - 1 else None
            chk(2.2)
            for m in range(2):
                for (bank, col0) in ((2, CC + m * 128), (3, CU + m * 128), (4, CB + m * 128)):
                    for k in range(8):
                        op("tensor", lambda e, bank=bank, col0=col0, k=k: e.matmul(
                            P[bank][:, 0:NT], lhsT=w_in_s[:, k * 2304 + col0: k * 2304 + col0 + 128],
                            rhs=hT[:, k * 512: k * 512 + NT], start=(k == 0), stop=(k == 7)),
                           r=[hTk, "w_in"], w=[("ps", bank)])
                    b.pop_until_pe(), b.pop_until_pe()
                if nh:
                    for (o, col0) in ((0, CC + m * 128), (16, CU + m * 128)):
                        for k in range(8):
                            op("tensor", lambda e, o=o, col0=col0, k=k: e.matmul(
                                P[5][:, o:o + nh], lhsT=w_in_s[:, k * 2304 + col0: k * 2304 + col0 + 128],
                                rhs=hTh[:, k * 16: k * 16 + nh], start=(k == 0), stop=(k == 7)),
                               r=["hTh", "w_in"], w=[("ps", 5)])
                cuS = Fb[0]
                op("scalar", lambda e: e.activation(out=cuS[:, 0:NT], in_=P[3][:, 0:NT], func=AF.Copy),
                   r=[("ps", 3)], w=["F0"])
                op("vector", lambda e: e.tensor_tensor(out=pext[:, 1:NT + 1], in0=P[2][:, 0:NT], in1=cuS[:, 0:NT],
                                                       op=ALU.mult), r=[("ps", 2), "F0"], w=["pext"])
                if nh:
                    op("scalar", lambda e: e.activation(out=small[:, 64:64 + nh], in_=P[5][:, 16:16 + nh], func=AF.Copy),
                       r=[("ps", 5)], w=["hcu"])
                for (hx, col) in ((hL, 0), (hR, NT + 1)):
                    if hx is None:
                        op("vector", lambda e, col=col: e.memset(pext[:, col:col + 1], 0.0), ww=["pext"])
                    else:
                        op("vector", lambda e, col=col, hx=hx: e.tensor_tensor(
                            out=pext[:, col:col + 1], in0=P[5][:, hx:hx + 1], in1=small[:, 64 + hx:65 + hx], op=ALU.mult),
                           r=[("ps", 5), "hcu"], ww=["pext"])
                tcv = Fb[3]
                cw0 = PF["cw"] + (l * 2 + m) * 3
                cbo = PF["cb"] + l * 2 + m
                op("vector", lambda e, cw0=cw0, cbo=cbo: e.tensor_scalar(
                    out=tcv[:, 0:NT], in0=pext[:, 0:NT], scalar1=pf[:, cw0:cw0 + 1], scalar2=pf[:, cbo:cbo + 1],
                    op0=ALU.mult, op1=ALU.add), r=["pext"], w=["F3"])
                for kk in (1, 2):
                    op("vector", lambda e, cw0=cw0, kk=kk: e.scalar_tensor_tensor(
                        out=tcv[:, 0:NT], in0=pext[:, kk:kk + NT], scalar=pf[:, cw0 + kk:cw0 + kk + 1], in1=tcv[:, 0:NT],
                        op0=ALU.mult, op1=ALU.add), r=["pext", "F3"], w=["F3"])
                op("vector", lambda e, m=m: e.tensor_tensor(out=s_.ocT[:, m * T + t0: m * T + t0 + NT], in0=P[4][:, 0:NT],
                                                             in1=tcv[:, 0:NT], op=ALU.mult), r=[("ps", 4), "F3"])
            chk(2.3)
            for sb in range(nsub):
                g = t * nsub + sb
                for (bank, c0_, n) in ((6, A0, 512), (7, A0 + 512, 256)):
                    for k in range(8):
                        op("tensor", lambda e, bank=bank, c0_=c0_, n=n, k=k, sb=sb: e.matmul(
                            P[bank][:, 0:n], lhsT=hT[:, k * 512 + sb * 128: k * 512 + (sb + 1) * 128],
                            rhs=w_in_s[:, k * 2304 + c0_: k * 2304 + c0_ + n], start=(k == 0), stop=(k == 7)),
                           r=[hTk, "w_in"], w=[("ps", bank)])
                    b.pop_until_pe(), b.pop_until_pe(), b.pop_until_pe()
                ki = 1 + g % 2
                kf = Fb[ki]
                kkey = "F%d" % ki
                qknorm(P[6][:, 0:128], ("ps", 6), 2, 64, gk64, kf[:, 0:128], kkey, Fb[0])
                qknorm(P[6][:, 128:384], ("ps", 6), 8, 32, gk32, kf[:, 128:384], kkey, Fb[0])
                if s_.pi >= 0:
                    rows = slice(g * 128, (g + 1) * 128)
                    dma("sync", ngk[s_.pi, l, rows, :], kf[:, 0:128], r=[kkey], key=("kf", ki))
                    dma("sync", ndk[s_.pi, l, rows, :], kf[:, 128:384], r=[kkey], key=("kf", ki))
                if s_.rope:
                    xsl[0] += 1
                    ti = xsl[0] % 2
                    dma("sync", tab[ti], rope_d[g * 128:(g + 1) * 128, :], w=[("tab", ti)], key=("tab", ti))
                    rope(kf, kkey, kb, "kb", [(0, 2, 64, 0, 64), (128, 8, 32, 128, 160)], tab[ti], ("tab", ti),
                         Fb[3], Fb[3][:, 384:768], "F3", "F3")
                else:
                    op("vector", lambda e: e.tensor_copy(out=kb, in_=kf[:, 0:384]), r=[kkey], w=["kb"])
                tb = b.tpbank([0])
                for blk in range(3):
                    op("tensor", lambda e, blk=blk: e.transpose(out=Pb[tb][:, blk * 128:(blk + 1) * 128],
                                                                in_=kb[:, blk * 128:(blk + 1) * 128], identity=identb),
                       r=["kb"], w=[("ps", tb)])
                nk = s_.nkc * 128
                op("scalar", lambda e, g=g: e.activation(out=s_.KgT[:, g * 128:(g + 1) * 128], in_=Pb[tb][:, 0:128],
                                                         func=AF.Copy), r=[("ps", tb)])
                op("scalar", lambda e, g=g, nk=nk: e.activation(
                    out=V(s_.KdT, [[nk, 2], [1, 128]], g * 128), in_=V(Pb[tb], [[128, 2], [1, 128]], 128), func=AF.Copy),
                   r=[("ps", tb)])
                op("scalar", lambda e, g=g: e.activation(out=V(s_.Vg, [[65, 2], [1, 64]], g * 130),
                                                         in_=V(P[6], [[64, 2], [1, 64]], 384), func=AF.Copy),
                   r=[("ps", 6)])
                op("scalar", lambda e, g=g: e.activation(out=V(s_.Vd, [[65, 4], [1, 64]], g * 260),
                                                         in_=V(P[7], [[64, 4], [1, 64]], 0), func=AF.Copy),
                   r=[("ps", 7)])
                if s_.pi >= 0:
                    vi = g % 2
                    op("vector", lambda e, vi=vi: e.tensor_copy(out=v32[vi][:, 0:128], in_=P[6][:, 384:512]),
                       r=[("ps", 6)], w=[("v32", vi)])
                    op("vector", lambda e, vi=vi: e.tensor_copy(out=v32[vi][:, 128:384], in_=P[7][:, 0:256]),
                       r=[("ps", 7)], ww=[("v32", vi)])
                    rows = slice(g * 128, (g + 1) * 128)
                    dma("sync", ngv[s_.pi, l, rows, :], v32[vi][:, 0:128], r=[("v32", vi)], key=("v32", vi))
                    dma("sync", ndv[s_.pi, l, rows, :], v32[vi][:, 128:384], r=[("v32", vi)], key=("v32", vi))
            b.flush()
        chk(2.4)
        if s_.pi < 0:
            nown = T // 128
            nk = s_.nkc * 128
            for sb in range(PAST // 128):
                rows = slice(sb * 128, (sb + 1) * 128)
                ci = 1 + sb % 2
                cf = Fb[ci]
                ckey = "F%d" % ci
                dma("sync", cf[:, 0:128], ck[l, rows, :], w=[ckey], key=("cst", ci))
                dma("sync", cf[:, 128:384], cdk[l, rows, :], key=("cst", ci))
                dma("sync", cf[:, 384:512], cv[l, rows, :], key=("cst", ci))
                dma("sync", cf[:, 512:768], cdv[l, rows, :], key=("cst", ci))
                b.lastw[ckey] = {("cst", ci): b.dval[("cst", ci)]}
                g = nown + sb
                op("vector", lambda e, cf=cf, sb=sb: e.tensor_copy(out=ckb[:, sb * 384:(sb + 1) * 384], in_=cf[:, 0:384]),
                   r=[ckey], w=[("ckb", sb)])
                op("scalar", lambda e, g=g, cf=cf: e.activation(out=V(s_.Vg, [[65, 2], [1, 64]], g * 130),
                                                                in_=V(cf, [[64, 2], [1, 64]], 384), func=AF.Copy), r=[ckey])
                op("scalar", lambda e, g=g, cf=cf: e.activation(out=V(s_.Vd, [[65, 4], [1, 64]], g * 260),
                                                                in_=V(cf, [[64, 4], [1, 64]], 512), func=AF.Copy), r=[ckey])
                chk(2.5)
                tb = b.tpbank([0])
                for blk in range(3):
                    op("tensor", lambda e, blk=blk, sb=sb: e.transpose(
                        out=Pb[tb][:, blk * 128:(blk + 1) * 128], in_=ckb[:, sb * 384 + blk * 128: sb * 384 + (blk + 1) * 128],
                        identity=identb), r=[("ckb", sb)], w=[("ps", tb)])
                op("scalar", lambda e, g=g: e.activation(out=s_.KgT[:, g * 128:(g + 1) * 128], in_=Pb[tb][:, 0:128],
                                                         func=AF.Copy), r=[("ps", tb)])
                op("scalar", lambda e, g=g, nk=nk: e.activation(
                    out=V(s_.KdT, [[nk, 2], [1, 128]], g * 128), in_=V(Pb[tb], [[128, 2], [1, 128]], 128), func=AF.Copy),
                   r=[("ps", tb)])

    def attention(maps, nkc, NT):
        nsub = NT // 128
        Pall = b.Pall
        for k in range(nkc + 2):
            if k < nkc:
                for m in maps:
                    sb_ = 2 * m["i"] + k % 2
                    op("tensor", lambda e, m=m, k=k, sb_=sb_: e.matmul(P[sb_][:, 0:NT], lhsT=m["kT"](k), rhs=m["q"],
                                                                       start=True, stop=True, tile_position=m["tp"]),
                       r=[m["qk"]], w=[("ps", sb_)])
            if k >= 2:
                kk = k - 2
                sl_ = kk % 3
                for m in maps:
                    po = sl_ * 1024 + m["i"] * 512
                    for sb in range(nsub):
                        op("tensor", lambda e, m=m, kk=kk, sb=sb, po=po: e.matmul(
                            P[m["O"]][:, sb * 65:(sb + 1) * 65], lhsT=PbA[:, po + sb * 128: po + (sb + 1) * 128], rhs=m["v"](kk),
                            start=(kk == 0 and sb == 0), stop=(kk == nkc - 1), skip_group_check=True),
                           r=[("pb", sl_)], w=[("ps", m["O"])])
            b.pop_until_pe()
            if 1 <= k <= nkc:
                kk = k - 1
                sl_ = kk % 3
                op("scalar", lambda e, kk=kk, sl_=sl_: e.activation(
                    out=V(PbA, [[512, 2], [1, NT]], sl_ * 1024), in_=V(Pall, [[1024, 2], [1, NT]], (kk % 2) * 512),
                    func=AF.Exp, scale=maps[0]["scale"]),
                   r=[("ps", kk % 2), ("ps", 2 + kk % 2)], w=[("pb", sl_)])

    def phaseB(l, s_, first_of_kind):
        kind = s_.kind
        ST, BT = SBT[:, kind * 8:kind * 8 + 8], mT(kind, l, 0)
        NT, T, nkc = s_.NT, s_.T, s_.nkc
        ntile, nsub = T // NT, NT // 128
        gq64, gq32 = pbl[:, 0:64], pbl[:, 128:160]
        nk = nkc * 128
        if first_of_kind:
            make_gate(kind, l, 2)
        st = {}

        def pn(t, sb):
            i = load_x(l, "B", s_.row0 + t * NT + sb * 128)
            st[(t, sb)] = norm_a(xs[i], ("xs", i), 128)

        def pt(t, sb):
            hq = hTq[sb % 2]
            norm_b(st[(t, sb)], 128, ST, BT, lambda c, hq=hq: hq[:, c * 128:(c + 1) * 128], ("hTq", sb % 2), [6])

        def pz(t, sb):
            hq = hTq[sb % 2]
            hkey = ("hTq", sb % 2)
            for (bank, c0_, n) in ((6, Q0, 512), (7, Q0 + 512, 256)):
                for k in range(8):
                    op("tensor", lambda e, bank=bank, c0_=c0_, n=n, k=k, hq=hq: e.matmul(
                        P[bank][:, 0:n], lhsT=hq[:, k * 128:(k + 1) * 128],
                        rhs=w_in_s[:, k * 2304 + c0_: k * 2304 + c0_ + n], start=(k == 0), stop=(k == 7)),
                       r=[hkey, "w_in"], w=[("ps", bank)])
            qf = Fb[1]
            qknorm(P[6][:, 0:512], ("ps", 6), 8, 64, gq64, qf[:, 0:512], "F1", Fb[0])
            qknorm(P[7][:, 0:256], ("ps", 7), 8, 32, gq32, qf[:, 512:768], "F1", Fb[0])
            if s_.rope:
                g = t * nsub + sb
                xsl[0] += 1
                ti = xsl[0] % 2
                dma("sync", tabB[ti], rope_d[g * 128:(g + 1) * 128, :], w=[("tabB", ti)], key=("tabB", ti))
                rope(qf, "F1", qbs[sb % 2], ("qb", sb % 2), [(0, 8, 64, 0, 64), (512, 8, 32, 128, 160)], tabB[ti], ("tabB", ti),
                     Fb[2], Fb[3], "F2", "F3")
            else:
                op("vector", lambda e: e.tensor_copy(out=qbs[sb % 2], in_=qf[:, 0:768]), r=["F1"], w=[("qb", sb % 2)])

        def pq(t, sb):
            QTt = QTs[t % 2]
            qb = qbs[sb % 2]
            for (b0_, nb_) in ((0, 4), (4, 2)):
                for blk in range(nb_):
                    op("tensor", lambda e, blk=blk, b0_=b0_: e.transpose(
                        out=Pb[7][:, 512 + blk * 128: 512 + (blk + 1) * 128],
                        in_=qb[:, (b0_ + blk) * 128:(b0_ + blk + 1) * 128], identity=identb),
                       r=[("qb", sb % 2)], w=[("ps", 7)])
                op("scalar", lambda e, sb=sb, b0_=b0_, nb_=nb_, QTt=QTt: e.activation(
                    out=V(QTt, [[512, nb_], [1, 128]], b0_ * 512 + sb * 128),
                    in_=V(Pb[7], [[128, nb_], [1, 128]], 512), func=AF.Copy),
                   r=[("ps", 7)], ww=[("QT", t % 2)])

        def ey(t, sb):
            t0 = t * NT
            for half in range(2):
                for c in range(8):
                    if c < 4:
                        lt = ocatT[:, c * 512 + sb * 128: c * 512 + (sb + 1) * 128]
                    elif c < 6:
                        lt = s_.ocT[:, (c - 4) * T + t0 + sb * 128: (c - 4) * T + t0 + (sb + 1) * 128]
                    else:
                        lt = ocatT[:, (c - 2) * 512 + sb * 128: (c - 2) * 512 + (sb + 1) * 128]
                    op("tensor", lambda e, half=half, c=c, lt=lt: e.matmul(
                        P[6 + half], lhsT=lt, rhs=w_out_s[:, c * 1024 + half * 512: c * 1024 + (half + 1) * 512],
                        start=(c == 0), stop=(c == 7)), r=["ocatT", "w_out"], w=[("ps", 6 + half)])
            residual(l, "B", s_.row0 + t0 + sb * 128)

        def eo(t):
            for sb in range(nsub):
                tb = b.tpbank([6, 7])
                for blk in range(6):
                    op("tensor", lambda e, blk=blk, sb=sb: e.transpose(
                        out=Pb[tb][:, blk * 128:(blk + 1) * 128], in_=otok[:, sb * 768 + blk * 128: sb * 768 + (blk + 1) * 128],
                        identity=identb), r=["otok"], w=[("ps", tb)])
                op("scalar", lambda e, sb=sb: e.activation(out=V(ocatT, [[512, 6], [1, 128]], sb * 128),
                                                           in_=V(Pb[tb], [[128, 6], [1, 128]]), func=AF.Copy),
                   r=[("ps", tb)], ww=["ocatT"])

        for sb in range(nsub):
            pn(0, sb), pt(0, sb), pz(0, sb), pq(0, sb)
        for t in range(ntile):
            t0 = t * NT
            QT = QTs[t % 2]
            b.defer = True
            if t > 0:
                for sb in range(nsub):
                    ey(t - 1, sb)
            if t + 1 < ntile:
                tn = t + 1
                pn(tn, 0), pn(tn, 1), pt(tn, 0), pt(tn, 1), pn(tn, 2), pn(tn, 3)
                pz(tn, 0), pt(tn, 2), pz(tn, 1), pq(tn, 0), pt(tn, 3), pz(tn, 2), pq(tn, 1), pz(tn, 3), pq(tn, 2), pq(tn, 3)
            b.defer = False

            def after_group():
                pass
            for j in range(4):
                gcn[0] += 1
                ob = (4, 5)
                maps = []
                for r_ in range(2):
                    maps.append(dict(
                        i=r_, S=(2 * r_, 2 * r_ + 1), O=ob[r_], tp=(64 * r_, 0), scale=0.125, qk=("QT", t % 2),
                        kT=lambda kc, r_=r_: s_.KgT[64 * r_:64 * r_ + 64, kc * 128:(kc + 1) * 128],
                        q=QT[64 * r_:64 * r_ + 64, j * 512: j * 512 + NT],
                        v=lambda kc, r_=r_: s_.Vg[:, kc * 130 + r_ * 65: kc * 130 + r_ * 65 + 65]))
                attention(maps, nkc, NT)
                for r_ in range(2):
                    h = j + 4 * r_
                    Ov = P[ob[r_]]
                    rs = small[:, 72 + r_ * 4: 72 + r_ * 4 + nsub]
                    op("vector", lambda e, Ov=Ov, rs=rs: e.reciprocal(out=rs, in_=V(Ov, [[65, nsub], [1, 1]], 64)),
                       r=[("ps", ob[r_])], w=[("rsg", r_)])
                    op("vector", lambda e, Ov=Ov, rs=rs, h=h: e.tensor_tensor(
                        out=V(otok, [[768, nsub], [1, 64]], h * 64), in0=V(Ov, [[65, nsub], [1, 64]]),
                        in1=V(rs, [[1, nsub], [0, 64]]), op=ALU.mult), r=[("ps", ob[r_]), ("rsg", r_)], ww=["otok"])
                after_group()
            for h in range(4):
                hb, hh = h // 2, h % 2
                gcn[0] += 1
                ob = (4, 5)
                maps = []
                for c_ in range(2):
                    gi = 2 * hh + c_
                    maps.append(dict(
                        i=c_, S=(2 * c_, 2 * c_ + 1), O=ob[c_], tp=(32 * gi, 0), scale=32 ** -0.5, qk=("QT", t % 2),
                        kT=lambda kc, gi=gi, hb=hb: s_.KdT[32 * gi:32 * gi + 32, hb * nk + kc * 128: hb * nk + (kc + 1) * 128],
                        q=QT[32 * gi:32 * gi + 32, (4 + hb) * 512: (4 + hb) * 512 + NT],
                        v=lambda kc, h=h: s_.Vd[:, kc * 260 + h * 65: kc * 260 + h * 65 + 65]))
                attention(maps, nkc, NT)
                if True:
                    Oa, Ob = P[ob[0]], P[ob[1]]
                    ka, kb_ = ("ps", ob[0]), ("ps", ob[1])
                    r0 = small[:, 80:80 + nsub]
                    r1 = small[:, 84:84 + nsub]
                    n = nsub * 64
                    t0_, t1_ = Fb[0][:, 0:n], Fb[0][:, 256:256 + n]
                    op("vector", lambda e, Oa=Oa: e.reciprocal(out=r0, in_=V(Oa, [[65, nsub], [1, 1]], 64)), r=[ka], w=["r0"])
                    op("vector", lambda e, Ob=Ob: e.reciprocal(out=r1, in_=V(Ob, [[65, nsub], [1, 1]], 64)), r=[kb_], w=["r1"])
                    op("vector", lambda e, Oa=Oa: e.tensor_tensor(out=V(t0_, [[64, nsub], [1, 64]]), in0=V(Oa, [[65, nsub], [1, 64]]),
                                                                  in1=V(r0, [[1, nsub], [0, 64]]), op=ALU.mult),
                       r=[ka, "r0"], w=["F0"])
                    op("vector", lambda e, Ob=Ob: e.tensor_tensor(out=V(t1_, [[64, nsub], [1, 64]]), in0=V(Ob, [[65, nsub], [1, 64]]),
                                                                  in1=V(r1, [[1, nsub], [0, 64]]), op=ALU.mult),
                       r=[kb_, "r1"], ww=["F0"])
                    od = Fb[0][:, 512:512 + n]
                    op("vector", lambda e: e.scalar_tensor_tensor(out=od, in0=t1_, scalar=lamt[:, 4 + l:5 + l], in1=t0_,
                                                                  op0=ALU.mult, op1=ALU.add), r=["F0", "lamt"], ww=["F0"])
                    sqd = Fb[0][:, 768:768 + n]
                    op("scalar", lambda e: e.activation(out=sqd, in_=od, func=AF.Square), r=["F0"], ww=["F0"])
                    ssd, sdd, rrd = small[:, 88:88 + nsub], small[:, 92:92 + nsub], small[:, 96:96 + nsub]
                    op("vector", lambda e: e.tensor_reduce(out=ssd, in_=V(sqd, [[64, nsub], [1, 64]]), axis=AX.X, op=ALU.add),
                       r=["F0"], w=["ssd"])
                    op("scalar", lambda e: e.activation(out=sdd, in_=ssd, func=AF.Ln, scale=1.0 / 64, bias=epsc),
                       r=["ssd"], w=["sdd"])
                    op("scalar", lambda e: e.activation(out=rrd, in_=sdd, func=AF.Exp, scale=-0.5), r=["sdd"], w=["rrd"])
                    op("vector", lambda e: e.tensor_tensor(out=V(od, [[64, nsub], [1, 64]]), in0=V(od, [[64, nsub], [1, 64]]),
                                                           in1=V(rrd, [[1, nsub], [0, 64]]), op=ALU.mult),
                       r=["F0", "rrd"], ww=["F0"])
                    op("vector", lambda e, h=h: e.tensor_tensor(out=V(otok, [[768, nsub], [1, 64]], 512 + h * 64),
                                                                in0=V(od, [[64, nsub], [1, 64]]),
                                                                in1=V(gsubS, [[0, nsub], [1, 64]]), op=ALU.mult),
                       r=["F0", "gsubS"], ww=["otok"])
                after_group()
            b.flush()
            eo(t)
            if t == ntile - 1:
                for sb in range(nsub):
                    ey(t, sb)

    rsl = [0]
    gcn = [0]

    def residual(l, ph, row0):
        i = load_x(l, ph, row0)
        rsl[0] += 1
        ri = 2 + rsl[0] % 2
        tr = Fb[ri]
        rk = "F%d" % ri
        for half in range(2):
            op("vector", lambda e, half=half: e.tensor_tensor(out=tr[:, half * 512:(half + 1) * 512], in0=P[6 + half],
                                                              in1=Gb[:, half * 512:(half + 1) * 512], op=ALU.mult),
               r=[("ps", 6 + half), "Gb"], w=[rk] if half == 0 else [], ww=[] if half == 0 else [rk])
        op("gpsimd", lambda e: e.tensor_tensor(out=tr, in0=tr, in1=xs[i], op=ALU.add), r=[rk, ("xs", i)], w=[rk])
        dma("sync", y[row0:row0 + 128, :], tr, r=[rk], w=[("xres", row0 // 128)], key=rk)

    def phaseC(l, s_, first_of_kind):
        kind = s_.kind
        ST, BT = SBT[:, 16 + kind * 8:16 + kind * 8 + 8], mT(kind, l, 3)
        NT, T = s_.NT, s_.T
        ntile, nsub = T // NT, NT // 128
        if first_of_kind:
            make_gate(kind, l, 5)
        nh = 0
        if ntile > 1:
            hi, nh = load_halo(l, "C", s_)
            norm_sub(xs[hi][0:nh, :], ("xs", hi), nh, ST, BT, lambda c: hTh[:, c * 16:c * 16 + nh], "hTh", [4])
        for t in range(ntile):
            t0 = t * NT
            for sb in range(nsub):
                i = load_x(l, "C", s_.row0 + t0 + sb * 128)
                norm_sub(xs[i], ("xs", i), 128, ST, BT,
                         lambda c, sb=sb: hTc[:, c * 512 + sb * 128: c * 512 + (sb + 1) * 128], "hTc", [4])
            hL = (2 * (t - 1)) if t > 0 else None
            hR = (2 * t + 1) if t < ntile - 1 else None
            for j in range(NJ):
                pa, pu = j % 2, 2 + j % 2
                for (bank, co) in ((pa, j * 256), (pu, j * 256 + 128)):
                    for k in range(8):
                        op("tensor", lambda e, bank=bank, co=co, k=k: e.matmul(
                            P[bank][:, 0:NT], lhsT=fup_s[:, k * 2 * DFF + co: k * 2 * DFF + co + 128],
                            rhs=hTc[:, k * 512: k * 512 + NT], start=(k == 0), stop=(k == 7)),
                           r=["hTc", ("fup", j // 2)], w=[("ps", bank)])
                if nh:
                    for k in range(8):
                        op("tensor", lambda e, j=j, k=k: e.matmul(
                            P[5][:, j * 16: j * 16 + nh], lhsT=fup_s[:, k * 2 * DFF + j * 256: k * 2 * DFF + j * 256 + 128],
                            rhs=hTh[:, k * 16: k * 16 + nh], start=(k == 0), stop=(k == 7)),
                           r=["hTh", ("fup", j // 2)], w=[("ps", 5)])
                ax = aext[j % 2]
                akey = ("aext", j % 2)
                op("scalar", lambda e, ax=ax, pa=pa: e.activation(out=ax[:, 1:NT + 1], in_=P[pa][:, 0:NT], func=AF.Copy),
                   r=[("ps", pa)], w=[akey])
                for (hx, col) in ((hL, 0), (hR, NT + 1)):
                    if hx is None:
                        op("vector", lambda e, ax=ax, col=col: e.memset(ax[:, col:col + 1], 0.0), ww=[akey])
                    else:
                        op("vector", lambda e, ax=ax, col=col, hx=hx, j=j: e.tensor_copy(
                            out=ax[:, col:col + 1], in_=P[5][:, j * 16 + hx: j * 16 + hx + 1]), r=[("ps", 5)], ww=[akey])
                tcv = Fb[0][:, (j % 2) * 512:(j % 2) * 512 + NT]
                tk = ("tcv", j % 2)
                sil = Fb[1][:, (j % 2) * 512:(j % 2) * 512 + NT]
                sk = ("sil", j % 2)
                w0 = PF["fcw"] + (l * NJ + j) * 3
                bo = PF["fcb"] + l * NJ + j
                op("vector", lambda e, ax=ax, tcv=tcv, w0=w0, bo=bo: e.tensor_scalar(
                    out=tcv, in0=ax[:, 0:NT], scalar1=pf[:, w0:w0 + 1], scalar2=pf[:, bo:bo + 1],
                    op0=ALU.mult, op1=ALU.add), r=[akey], w=[tk])
                for kk in (1, 2):
                    op("vector", lambda e, ax=ax, tcv=tcv, w0=w0, kk=kk: e.scalar_tensor_tensor(
                        out=tcv, in0=ax[:, kk:kk + NT], scalar=pf[:, w0 + kk:w0 + kk + 1], in1=tcv,
                        op0=ALU.mult, op1=ALU.add), r=[akey, tk], w=[tk])
                op("scalar", lambda e, tcv=tcv, sil=sil: e.activation(out=sil, in_=tcv, func=AF.Silu), r=[tk], w=[sk])
                op("vector", lambda e, sil=sil, pu=pu, j=j: e.tensor_tensor(out=fT[:, j * 512: j * 512 + NT], in0=P[pu][:, 0:NT],
                                                                            in1=sil, op=ALU.mult),
                   r=[("ps", pu), sk], ww=["fT"])
            for sb in range(nsub):
                for half in range(2):
                    for j in range(NJ):
                        op("tensor", lambda e, half=half, j=j, sb=sb: e.matmul(
                            P[6 + half], lhsT=fT[:, j * 512 + sb * 128: j * 512 + (sb + 1) * 128],
                            rhs=fdn_s[:, j * 1024 + half * 512: j * 1024 + (half + 1) * 512],
                            start=(j == 0), stop=(j == NJ - 1)), r=["fT", ("fdn", j // 11)], w=[("ps", 6 + half)])
                residual(l, "C", s_.row0 + t0 + sb * 128)

    try:
      chk(1)
      for l in range(L):
          lam_init = 0.8 - 0.6 * math.exp(-0.3 * l)
          wi = w_in[l].rearrange("(k p) n -> p k n", p=128)
          wis = w_in_s.rearrange("p (k n) -> p k n", k=8)
          for c0_ in range(0, 2304, 768):
              dma("gpsimd", wis[:, :, c0_:c0_ + 768], wi[:, :, c0_:c0_ + 768], w=[] if c0_ else ["w_in"], key="w_in")
          b.lastw["w_in"] = {"w_in": b.dval["w_in"]}
          wo = w_out[l].rearrange("(k p) n -> p k n", p=128)
          dma("gpsimd", w_out_s.rearrange("p (k n) -> p k n", k=8), wo, w=["w_out"], key="w_out")
          dma("sync", pbl, AP(pb_d.tensor, l * PB_N, [[0, 128], [1, PB_N]]), w=["pbl"], key="pbl")
          for kind in range(2):
              op("vector", lambda e, kind=kind: e.scalar_tensor_tensor(
                  out=SBT[:, kind * 8:kind * 8 + 8], in0=mT(kind, l, 1), scalar=1.0, in1=pfv("n1g", l * 8, 8),
                  op0=ALU.add, op1=ALU.mult), r=["modsT", "pf"], ww=["SBT"])
              op("vector", lambda e, kind=kind: e.scalar_tensor_tensor(
                  out=SBT[:, 16 + kind * 8:16 + kind * 8 + 8], in0=mT(kind, l, 4), scalar=1.0, in1=pfv("n2g", l * 8, 8),
                  op0=ALU.add, op1=ALU.mult), r=["modsT", "pf"], ww=["SBT"])
          dl = pbl[:, 256:384]
          pr = small[:, 128:192]
          op("vector", lambda e: e.tensor_tensor(out=V(pr, [[32, 2], [1, 32]]), in0=V(dl, [[64, 2], [1, 32]]),
                                                 in1=V(dl, [[64, 2], [1, 32]], 32), op=ALU.mult), r=["pbl"], w=["pr"])
          op("vector", lambda e: e.tensor_reduce(out=lamt[:, 0:2], in_=V(pr, [[32, 2], [1, 32]]), axis=AX.X, op=ALU.add),
             r=["pr"], w=["lam0"])
          op("scalar", lambda e: e.activation(out=lamt[:, 2:4], in_=lamt[:, 0:2], func=AF.Exp), r=["lam0"], w=["lam1"])
          op("vector", lambda e, li=lam_init: e.scalar_tensor_tensor(
              out=lamt[:, 4 + l:5 + l], in0=lamt[:, 3:4], scalar=-li, in1=lamt[:, 2:3], op0=ALU.add, op1=ALU.subtract),
             r=["lam1"], w=["lamt"])
          op("vector", lambda e, li=lam_init: e.tensor_scalar(out=gsubS, in0=pbl[:, 192:256], scalar1=1.0 - li, scalar2=None,
                                                              op0=ALU.mult), r=["pbl"], w=["gsubS"])
          for s_ in (seqs[0], seqs[1]):
              nk_ = s_.nkc
              op("vector", lambda e, a=V(s_.Vg, [[65, nk_ * 2], [1, 1]], 64): e.memset(a, 1.0))
              op("vector", lambda e, a=V(s_.Vd, [[65, nk_ * 4], [1, 1]], 64): e.memset(a, 1.0))
          b.barrier()
          chk(2)
          prevk = -1
          for s_ in seqs:
              phaseA(l, s_)
              b.barrier()
              chk(3)
              phaseB(l, s_, s_.kind != prevk)
              prevk = s_.kind
              b.barrier()
          fu = f_up[l].rearrange("(k p) n -> p k n", p=128)
          fus = fup_s.rearrange("p (k n) -> p k n", k=8)
          for pc in range(NJ // 2):
              dma("gpsimd", fus[:, :, pc * 512:(pc + 1) * 512], fu[:, :, pc * 512:(pc + 1) * 512], w=[("fup", pc)],
                  key=("fup", pc))
          fd = f_dn[l].rearrange("(j p) n -> p j n", p=128)
          fds = fdn_s.rearrange("p (j n) -> p j n", j=NJ)
          for pc in range(2):
              dma("gpsimd", fds[:, pc * 11:(pc + 1) * 11, :], fd[:, pc * 11:(pc + 1) * 11, :], w=[("fdn", pc)],
                  key=("fdn", pc))
          prevk = -1
          for s_ in seqs:
              phaseC(l, s_, s_.kind != prevk)
              prevk = s_.kind
          b.barrier()
    except Stop:
        pass
    b.finish()
    return nc


_NC_CACHE = {}


def _rope_table(TS):
    n_rows = TS // GRID_W
    row = np.repeat(np.arange(n_rows), GRID_W).astype(np.float32)
    col = np.tile(np.arange(GRID_W), n_rows).astype(np.float32)
    out = []
    for dim in (64, 32):
        nf = dim // 4
        freqs = (np.float32(10000.0) ** (-np.arange(nf, dtype=np.float32) / np.float32(nf))).astype(np.float32)
        ar = row[:, None] * freqs[None, :]
        ac = col[:, None] * freqs[None, :]
        ang = np.concatenate([ar, ar, ac, ac], axis=-1).astype(np.float32)
        cos, sin = np.cos(ang), np.sin(ang)
        sgn = np.concatenate([-np.ones(nf), np.ones(nf), -np.ones(nf), np.ones(nf)]).astype(np.float32)
        out += [cos.astype(np.float32), (sin * sgn[None, :]).astype(np.float32)]
    return np.ascontiguousarray(np.concatenate(out, axis=-1), dtype=np.float32)


def _fm(v):
    v = np.asarray(v, dtype=np.float32)
    n = v.shape[-1] // 128
    r = v.reshape(v.shape[:-1] + (n, 128))
    return np.moveaxis(r, -1, 0)


def run(cfg, inp, n_cores):
    L, TS, TP, NP = cfg.L, cfg.TS, cfg.TP, cfg.NP
    f = lambda k: np.asarray(inp[k], dtype=np.float32)
    key = (cfg.TS, cfg.TP, cfg.NP, cfg.L, cfg.PAST)
    if key not in _NC_CACHE:
        _NC_CACHE[key] = build(cfg)
    nc = _NC_CACHE[key]
    PF = pf_layout(L)
    qg = [np.arange(h * 64, (h + 1) * 64) for h in (0, 4, 1, 5, 2, 6, 3, 7)]
    o_qg, o_kg, o_vg, o_cb, o_cc, o_cu, o_qd, o_kd, o_vd = 0, 512, 640, 768, 1024, 1280, 1536, 1792, 2048
    perm = np.concatenate([np.arange(o_kg, o_kg + 128), np.arange(o_kd, o_kd + 256), np.arange(o_vg, o_vg + 128),
                           np.arange(o_vd, o_vd + 256), np.arange(o_cb, o_cb + 256), np.arange(o_cc, o_cc + 256),
                           np.arange(o_cu, o_cu + 256), np.concatenate(qg), np.arange(o_qd, o_qd + 256)])
    w_in_p = np.ascontiguousarray(f("w_in")[:, :, perm])
    fu = f("ffn_up")
    fu_p = np.ascontiguousarray(
        np.stack([fu[:, :, :DFF].reshape(L, D, NJ, 128), fu[:, :, DFF:].reshape(L, D, NJ, 128)], axis=3).reshape(L, D, 2 * DFF))
    pb = np.ascontiguousarray(np.concatenate([f("gqa_qn_g"), f("gqa_kn_g"), f("diff_qn_g"), f("diff_kn_g"), f("diff_subln_g"),
                                              f("diff_lambda").reshape(L, 128)], axis=1))
    identb = np.eye(128, dtype=np.float32).astype(ml_dtypes.bfloat16)
    cst32 = np.ascontiguousarray(np.concatenate([np.eye(128, dtype=np.float32), np.ones((128, 128), np.float32)], axis=1))
    rope = _rope_table(TS)
    shared = dict(w_mod=f("w_mod"), w_in=w_in_p, w_out=f("w_out"), f_up=fu_p, f_dn=f("ffn_down"), rope=rope, identb=identb,
                  cst32=cst32, pb=pb)
    xp, xsm = f("x_prompt"), f("x_sample")
    in_maps = []
    for c in range(n_cores):
        pfa = np.zeros((128, PF["_n"]), np.float32)
        pfa[:, PF["cond"]:PF["cond"] + 8] = _fm(f("c")[c])
        pfa[:, PF["cond"] + 8:PF["cond"] + 16] = _fm(f("c_ctx"))
        pfa[:, PF["bmod"]:PF["bmod"] + L * 48] = _fm(f("b_mod")).reshape(128, L * 48)
        pfa[:, PF["n1g"]:PF["n1g"] + L * 8] = _fm(f("norm1_g")).reshape(128, L * 8)
        pfa[:, PF["n2g"]:PF["n2g"] + L * 8] = _fm(f("norm2_g")).reshape(128, L * 8)
        cw = _fm(f("conv_w"))
        pfa[:, PF["cw"]:PF["cw"] + L * 6] = np.transpose(cw, (0, 1, 3, 2)).reshape(128, L * 6)
        pfa[:, PF["cb"]:PF["cb"] + L * 2] = _fm(f("conv_b")).reshape(128, L * 2)
        fcw = _fm(f("ffn_conv_w"))
        pfa[:, PF["fcw"]:PF["fcw"] + L * NJ * 3] = np.transpose(fcw, (0, 1, 3, 2)).reshape(128, L * NJ * 3)
        pfa[:, PF["fcb"]:PF["fcb"] + L * NJ] = _fm(f("ffn_conv_b")).reshape(128, L * NJ)
        xall = np.ascontiguousarray(np.concatenate([xsm[c], xp[c * NP:(c + 1) * NP].reshape(NP * TP, D)], axis=0))
        m = dict(shared)
        m.update(xall=xall, pf=pfa,
                 ck=np.ascontiguousarray(f("cache_gqa_k")[c].reshape(L, cfg.PAST, 128)),
                 cv=np.ascontiguousarray(f("cache_gqa_v")[c].reshape(L, cfg.PAST, 128)),
                 cdk=np.ascontiguousarray(f("cache_diff_k")[c].reshape(L, cfg.PAST, 256)),
                 cdv=np.ascontiguousarray(f("cache_diff_v")[c].reshape(L, cfg.PAST, 256)))
        in_maps.append(m)
    res = run_bass_kernel_spmd(nc, in_maps, core_ids=list(range(n_cores)))
    R = res.results
    ys = np.stack([R[c]["y"][:TS] for c in range(n_cores)], axis=0)
    yp = np.concatenate([R[c]["y"][TS:].reshape(NP, TP, D) for c in range(n_cores)], axis=0)
    gk = np.concatenate([R[c]["ngk"] for c in range(n_cores)], axis=0).reshape(n_cores * NP, L, TP, 2, 64)
    gv = np.concatenate([R[c]["ngv"] for c in range(n_cores)], axis=0).reshape(n_cores * NP, L, TP, 2, 64)
    dk = np.concatenate([R[c]["ndk"] for c in range(n_cores)], axis=0).reshape(n_cores * NP, L, TP, 4, 2, 32)
    dv = np.concatenate([R[c]["ndv"] for c in range(n_cores)], axis=0).reshape(n_cores * NP, L, TP, 4, 64)
    return (yp.astype(np.float32), ys.astype(np.float32), gk.astype(np.float32), gv.astype(np.float32),
            dk.astype(np.float32), dv.astype(np.float32))


def kernel(**inputs):
    cfg = Cfg()
    return run(cfg, inputs, 8)
```

```python
import math
import numpy as np
import ml_dtypes
import concourse.bass as bass
import concourse.mybir as mybir
from concourse.bass_utils import run_bass_kernel_spmd
from concourse.ap import AP
from contextlib import ExitStack

F32 = mybir.dt.float32
BF16 = mybir.dt.bfloat16
U8 = mybir.dt.uint8
ALU = mybir.AluOpType
AF = mybir.ActivationFunctionType
AX = mybir.AxisListType

D = 1024
GRID_W = 64
DFF = 2816
NJ = DFF // 128
EPS = 1e-6
COMPUTE = ("tensor", "vector", "scalar", "gpsimd")
ALLENG = ("sync", "tensor", "vector", "scalar", "gpsimd")


class Cfg:
    def __init__(self, TS=4096, TP=256, NP=4, L=4, PAST=256):
        self.TS, self.TP, self.NP, self.L, self.PAST = TS, TP, NP, L, PAST
        self.NTOK = TS + NP * TP
        self.stop = 99


def pf_layout(L):
    o = {}
    c = 0
    for name, n in (("cond", 16), ("bmod", L * 48), ("n1g", L * 8), ("n2g", L * 8), ("cw", L * 6),
                    ("cb", L * 2), ("fcw", L * NJ * 3), ("fcb", L * NJ)):
        o[name] = c
        c += n
    o["_n"] = c
    return o


PB_N = 64 + 64 + 32 + 32 + 64 + 128


class Rec:
    def __getattr__(self, name):
        def f(*a, **k):
            self.call = (name, a, k)
            return self
        return f


class Bld:
    def __init__(self, cfg):
        self.cfg = cfg
        self.nc = bass.Bass("TRN2", target_bir_lowering=False)
        self.es = ExitStack()
        self.prog = {e: [] for e in ALLENG}
        self.cnt = {e: 0 for e in COMPUTE}
        self.semh = {}
        self.dval = {}
        self.seen = {e: {} for e in ALLENG}
        self.lastw = {}
        self.readers = {}
        self.arena_off = 0
        self.ARENA = 212480
        self.arena = self.es.enter_context(self.nc.sbuf_tensor("arena", [128, self.ARENA], U8))
        self.P = []
        self.Pb = []
        pall = self.es.enter_context(self.nc.psum_tensor("psall", [128, 4096], F32))
        self.Pall = pall[:, :]
        pallb = pall[:, :].bitcast(BF16)
        for i in range(8):
            self.P.append(self.Pall[:, i * 512:(i + 1) * 512])
            self.Pb.append(pallb[:, i * 1024:(i + 1) * 1024])
        self.tpi = 0
        self.defer = False
        self.emitting = False
        import collections
        self.q = collections.deque()

    def buf(self, cols, dt, at=None):
        nb = cols * (4 if dt == F32 else 2)
        nb = (nb + 31) // 32 * 32
        if at is None:
            at = self.arena_off
            self.arena_off += nb
            assert self.arena_off <= self.ARENA, ("arena overflow", self.arena_off)
        return self.arena[:, at:at + nb].bitcast(dt)[:, 0:cols]

    def sem(self, key):
        if key not in self.semh:
            self.semh[key] = self.es.enter_context(self.nc.semaphore("s%d" % len(self.semh)))
        return self.semh[key]

    def _wait(self, eng, deps):
        need = {}
        for d in deps:
            for k, v in d.items():
                if need.get(k, 0) < v:
                    need[k] = v
        for k, v in need.items():
            if k == eng:
                if eng == "tensor":
                    continue
                if v < self.cnt[eng] - 1:
                    continue
            if self.seen[eng].get(k, 0) >= v:
                continue
            self.seen[eng][k] = v
            sem = self.sem(k)
            self.prog[eng].append(lambda e, sem=sem, v=v: e.wait_ge(sem, v))

    def _deps(self, r, w, ww):
        deps = []
        for k in r:
            if k in self.lastw:
                deps.append(self.lastw[k])
        for k in w:
            if k in self.lastw:
                deps.append(self.lastw[k])
            if k in self.readers:
                deps.append(self.readers[k])
        for k in ww:
            if k in self.readers:
                deps.append(self.readers[k])
        return deps

    def _reg(self, tokk, tokv, r, w, ww):
        for k in r:
            d = self.readers.setdefault(k, {})
            d[tokk] = tokv
        for k in w:
            self.lastw[k] = {tokk: tokv}
            self.readers[k] = {}
        for k in ww:
            d = self.lastw.setdefault(k, {})
            d[tokk] = tokv

    def pop_until_pe(self):
        while self.q:
            ent = self.q.popleft()
            self.emitting = True
            if ent[0] == "op":
                self.op(*ent[1:])
            else:
                self.dma(*ent[1:])
            self.emitting = False
            if ent[0] == "op" and ent[1] == "tensor":
                return

    def flush(self):
        while self.q:
            self.pop_until_pe()

    def op(self, eng, fn, r=(), w=(), ww=()):
        if self.defer and not self.emitting:
            rec = Rec()
            fn(rec)
            call = rec.call
            self.q.append(("op", eng, (lambda e, call=call: getattr(e, call[0])(*call[1], **call[2])), tuple(r), tuple(w), tuple(ww)))
            return
        self._wait(eng, self._deps(r, w, ww))
        self.cnt[eng] += 1
        sem = self.sem(eng)
        rec = Rec()
        fn(rec)
        name, a, k = rec.call
        self.prog[eng].append(lambda e, name=name, a=a, k=k, sem=sem: getattr(e, name)(*a, **k).then_inc(sem, 1))
        self._reg(eng, self.cnt[eng], r, w, ww)

    def dma(self, q, out, in_, r=(), w=(), key=None):
        if self.defer and not self.emitting:
            self.q.append(("dma", q, out, in_, tuple(r), tuple(w), key))
            return
        self._wait(q, self._deps(r, w, ()))
        self.dval[key] = self.dval.get(key, 0) + 16
        sem = self.sem(key)
        self.prog[q].append(lambda e, out=out, in_=in_, sem=sem: e.dma_start(out=out, in_=in_).then_inc(sem, 16))
        self._reg(key, self.dval[key], r, w, ())

    def barrier(self):
        toks = {e: self.cnt[e] for e in COMPUTE if self.cnt[e] > 0}
        toks.update(self.dval)
        for e in ALLENG:
            for k, v in toks.items():
                if k == e:
                    continue
                if self.seen[e].get(k, 0) >= v:
                    continue
                self.seen[e][k] = v
                sem = self.sem(k)
                self.prog[e].append(lambda en, sem=sem, v=v: en.wait_ge(sem, v))

    def finish(self):
        self.barrier()
        nc = self.nc
        prog = self.prog
        with nc.Block() as block:
            for name in ALLENG:
                def mk(name):
                    def body(e):
                        for f in prog[name]:
                            f(e)
                    return body
                getattr(block, name)(mk(name))

    def tpbank(self, banks):
        self.tpi += 1
        return banks[self.tpi % len(banks)]


def V(ap, pat, off=0):
    return AP(ap.tensor, ap.offset + off, [list(ap.ap[0])] + [list(x) for x in pat])


def build(cfg):
    b = Bld(cfg)
    nc = b.nc
    L, TS, TP, NP, PAST, NTOK = cfg.L, cfg.TS, cfg.TP, cfg.NP, cfg.PAST, cfg.NTOK
    PF = pf_layout(L)
    NKC_S = (TS + PAST) // 128
    dt_ = nc.dram_tensor
    xall = dt_("xall", [NTOK, D], F32, kind="ExternalInput").ap()
    ck = dt_("ck", [L, PAST, 128], F32, kind="ExternalInput").ap()
    cv = dt_("cv", [L, PAST, 128], F32, kind="ExternalInput").ap()
    cdk = dt_("cdk", [L, PAST, 256], F32, kind="ExternalInput").ap()
    cdv = dt_("cdv", [L, PAST, 256], F32, kind="ExternalInput").ap()
    pf_d = dt_("pf", [128, PF["_n"]], F32, kind="ExternalInput").ap()
    pb_d = dt_("pb", [L, PB_N], F32, kind="ExternalInput").ap()
    w_mod = dt_("w_mod", [L, D, 6 * D], F32, kind="ExternalInput").ap()
    w_in = dt_("w_in", [L, D, 2304], F32, kind="ExternalInput").ap()
    w_out = dt_("w_out", [L, D, D], F32, kind="ExternalInput").ap()
    f_up = dt_("f_up", [L, D, 2 * DFF], F32, kind="ExternalInput").ap()
    f_dn = dt_("f_dn", [L, DFF, D], F32, kind="ExternalInput").ap()
    rope_d = dt_("rope", [TS, 192], F32, kind="ExternalInput").ap()
    idb_d = dt_("identb", [128, 128], BF16, kind="ExternalInput").ap()
    c32_d = dt_("cst32", [128, 256], F32, kind="ExternalInput").ap()
    y = dt_("y", [NTOK, D], F32, kind="ExternalOutput").ap()
    ngk = dt_("ngk", [NP, L, TP, 128], F32, kind="ExternalOutput").ap()
    ngv = dt_("ngv", [NP, L, TP, 128], F32, kind="ExternalOutput").ap()
    ndk = dt_("ndk", [NP, L, TP, 256], F32, kind="ExternalOutput").ap()
    ndv = dt_("ndv", [NP, L, TP, 256], F32, kind="ExternalOutput").ap()

    identb = b.buf(128, BF16)
    cst32 = b.buf(256, F32)
    ident32, ones32 = cst32[:, 0:128], cst32[:, 128:256]
    pf = b.buf(PF["_n"], F32)
    pbl = b.buf(PB_N, F32)
    modsT = b.buf(2 * L * 48, F32)
    condb = b.buf(16, BF16)
    SBT = b.buf(32, F32)
    lamt = b.buf(16, F32)
    gsubS = b.buf(64, F32)
    Gb = b.buf(1024, F32)
    epsc = b.buf(1, F32)
    small = b.buf(256, F32)
    diag = b.buf(128, F32)
    xs = [b.buf(1024, F32) for _ in range(2)]
    xn = [b.buf(1024, BF16) for _ in range(2)]
    Fb = [b.buf(1024, F32) for _ in range(4)]
    hTh = b.buf(8 * 16, BF16)
    X0 = b.arena_off
    XSZ = 34880
    b.arena_off += XSZ
    BIG0 = b.arena_off
    BIGSZ = 132 * 1024
    b.arena_off += BIGSZ
    assert b.arena_off <= b.ARENA, b.arena_off

    class Lay:
        def __init__(self, base):
            self.o = base

        def buf(self, cols, dt):
            nb = (cols * (4 if dt == F32 else 2) + 31) // 32 * 32
            r = b.buf(cols, dt, at=self.o)
            self.o += nb
            return r

    rowsb = b.buf(1024, F32, at=X0)
    la = Lay(X0)
    hTa = [la.buf(8 * 512, BF16) for _ in range(2)]
    tab = [la.buf(192, F32) for _ in range(2)]
    v32_off = la.o
    v32 = [la.buf(384, F32) for _ in range(2)]
    rowA = b.buf(512, F32, at=v32_off)
    kb = la.buf(384, BF16)
    pext = la.buf(514, F32)
    wstA = la.buf(8 * 512, BF16)
    assert la.o <= X0 + XSZ, la.o - X0
    lb = Lay(X0)
    hTq = [lb.buf(8 * 128, BF16) for _ in range(2)]
    tabB = [lb.buf(192, F32) for _ in range(2)]
    qbs = [lb.buf(768, BF16) for _ in range(2)]
    QT = lb.buf(6 * 512, BF16)
    PbA = lb.buf(3 * 2 * 512, BF16)
    otok = lb.buf(4 * 768, BF16)
    ocatT = lb.buf(6 * 512, BF16)
    assert lb.o <= X0 + XSZ
    lc = Lay(X0)
    hTc = lc.buf(8 * 512, BF16)
    aext = [lc.buf(514, F32) for _ in range(2)]
    fT = lc.buf(NJ * 512, BF16)
    assert lc.o <= X0 + XSZ, lc.o - X0
    lg = Lay(BIG0)
    w_in_s = lg.buf(8 * 2304, BF16)
    w_out_s = lg.buf(8 * 1024, BF16)

    class Seq:
        pass
    seqs = []
    sq = Seq()
    sq.kind, sq.row0, sq.T, sq.NT, sq.rope, sq.nkc, sq.pi = 0, 0, TS, 512, True, NKC_S, -1
    sq.KgT = lg.buf(NKC_S * 128, BF16)
    sq.KdT = lg.buf(2 * NKC_S * 128, BF16)
    sq.Vg = lg.buf(NKC_S * 2 * 65, BF16)
    sq.Vd = lg.buf(NKC_S * 4 * 65, BF16)
    sq.ocT = lg.buf(2 * TS, BF16)
    seqs.append(sq)
    pq = Seq()
    pq.KgT = lg.buf(TP, BF16)
    pq.KdT = lg.buf(2 * TP, BF16)
    pq.Vg = lg.buf((TP // 128) * 2 * 65, BF16)
    pq.Vd = lg.buf((TP // 128) * 4 * 65, BF16)
    pq.ocT = lg.buf(2 * TP, BF16)
    for pi in range(NP):
        s_ = Seq()
        s_.__dict__.update(pq.__dict__)
        s_.kind, s_.row0, s_.T, s_.NT, s_.rope, s_.nkc, s_.pi = 1, TS + pi * TP, TP, TP, False, TP // 128, pi
        seqs.append(s_)
    ckb = lg.buf(2 * 384, BF16)
    QT2 = lg.buf(6 * 512, BF16)
    QTs = [QT, QT2]
    assert lg.o <= BIG0 + BIGSZ, lg.o - BIG0
    lf = Lay(BIG0)
    fup_s = lf.buf(8 * 2 * DFF, BF16)
    fdn_s = lf.buf(NJ * 1024, BF16)
    assert lf.o <= BIG0 + BIGSZ, lf.o - BIG0
    wst = [b.buf(8 * 512, BF16, at=BIG0 + i * 8192) for i in range(2)]

    P, Pb = b.P, b.Pb
    op, dma = b.op, b.dma

    def pfv(name, off, n):
        return pf[:, PF[name] + off: PF[name] + off + n]

    dma("sync", identb, idb_d, w=["identb"], key="setup")
    dma("sync", cst32, c32_d, w=["cst32"], key="setup")
    dma("sync", pf, pf_d, w=["pf"], key="setup")
    op("vector", lambda e: e.memset(epsc, EPS), w=["epsc"])
    b.barrier()
    op("scalar", lambda e: e.activation(out=condb, in_=pfv("cond", 0, 16), func=AF.Silu), w=["condb"])
    one11 = ones32[0:1, 0:1]
    bi = 0
    for l in range(1):
        wv = w_mod[l].rearrange("(k p) n -> p k n", p=128)
        for blk in range(12):
            slot = bi % 2
            bi += 1
            dma("gpsimd", wst[slot].rearrange("p (k n) -> p k n", k=8), wv[:, :, blk * 512:(blk + 1) * 512],
                w=[("wst", slot)], key=("wst", slot))
            for kind in range(2):
                for k in range(8):
                    op("tensor", lambda e, kind=kind, k=k, slot=slot: e.matmul(
                        P[kind][0:1, 0:512], lhsT=condb[:, kind * 8 + k: kind * 8 + k + 1],
                        rhs=wst[slot][:, k * 512:(k + 1) * 512], start=(k == 0), stop=(k == 7)),
                       r=[("wst", slot), "condb"], w=[("ps", kind)])
                op("scalar", lambda e, kind=kind: e.activation(out=rowsb[0:1, kind * 512:(kind + 1) * 512],
                                                               in_=P[kind][0:1, 0:512], func=AF.Copy),
                   r=[("ps", kind)], w=[("row", kind)])
                for c in range(4):
                    op("tensor", lambda e, kind=kind, c=c: e.matmul(
                        P[2 + kind][:, c:c + 1], lhsT=rowsb[0:1, kind * 512 + c * 128: kind * 512 + (c + 1) * 128],
                        rhs=one11, start=True, stop=True),
                       r=[("row", kind), "cst32"], w=[("ps", 2 + kind)])
                mo = (kind * L + l) * 48 + blk * 4
                op("vector", lambda e, kind=kind, mo=mo, l=l, blk=blk: e.tensor_tensor(
                    out=modsT[:, mo:mo + 4], in0=P[2 + kind][:, 0:4], in1=pfv("bmod", l * 48 + blk * 4, 4), op=ALU.add),
                   r=[("ps", 2 + kind), "pf"], ww=["modsT"])
    b.barrier()

    def mT(kind, l, i):
        o = (kind * L + l) * 48 + i * 8
        return modsT[:, o:o + 8]

    sl = [0]

    def norm_a(xap, xkey, np_):
        sl[0] += 1
        i = sl[0] % 2
        ss, sd, rs = small[0:np_, i:i + 1], small[0:np_, 2 + i:3 + i], small[0:np_, 4 + i:5 + i]
        xnb = xn[i]
        op("scalar", lambda e: e.activation(out=xnb[0:np_, :], in_=xap, func=AF.Square, accum_out=ss),
           r=[xkey], w=[("xn", i), ("ss", i)])
        op("scalar", lambda e: e.activation(out=sd, in_=ss, func=AF.Ln, scale=1.0 / D, bias=epsc[0:np_, :]),
           r=[("ss", i), "epsc"], w=[("sd", i)])
        op("scalar", lambda e: e.activation(out=rs, in_=sd, func=AF.Exp, scale=-0.5), r=[("sd", i)], w=[("rs", i)])
        op("vector", lambda e: e.tensor_scalar(out=xnb[0:np_, :], in0=xap, scalar1=rs, scalar2=None, op0=ALU.mult),
           r=[xkey, ("rs", i)], w=[("xn", i)])
        return i

    def norm_b(i, np_, ST, BT, dst, dkey, tpbanks):
        xnb = xn[i]
        tb = b.tpbank(tpbanks)
        for c in range(8):
            op("tensor", lambda e, c=c: e.transpose(out=Pb[tb][:, c * 128:c * 128 + np_],
                                                    in_=xnb[0:np_, c * 128:(c + 1) * 128],
                                                    identity=identb[0:np_, 0:np_]),
               r=[("xn", i), "identb"], w=[("ps", tb)])
        for c in range(8):
            op("vector", lambda e, c=c: e.tensor_scalar(out=dst(c), in0=Pb[tb][:, c * 128:c * 128 + np_],
                                                        scalar1=ST[:, c:c + 1], scalar2=BT[:, c:c + 1],
                                                        op0=ALU.mult, op1=ALU.add),
               r=[("ps", tb), "SBT", "modsT"], ww=[dkey])

    def norm_sub(xap, xkey, np_, ST, BT, dst, dkey, tpbanks):
        i = norm_a(xap, xkey, np_)
        norm_b(i, np_, ST, BT, dst, dkey, tpbanks)

    def qknorm(src, skey, G, hd, gain, dst, dkey, zs, zkey="F0"):
        n = G * hd
        sl[0] += 1
        i = sl[0] % 2
        ssq, sdq, rq = small[:, 8 + i * 8:8 + i * 8 + G], small[:, 24 + i * 8:24 + i * 8 + G], small[:, 40 + i * 8:40 + i * 8 + G]
        zv = zs[:, 0:n]
        op("scalar", lambda e: e.activation(out=zv, in_=src, func=AF.Square), r=[skey], w=[zkey])
        op("vector", lambda e: e.tensor_reduce(out=ssq, in_=V(zv, [[hd, G], [1, hd]]), axis=AX.X, op=ALU.add),
           r=[zkey], w=[("ssq", i)])
        op("scalar", lambda e: e.activation(out=sdq, in_=ssq, func=AF.Ln, scale=1.0 / hd, bias=epsc),
           r=[("ssq", i)], w=[("sdq", i)])
        op("scalar", lambda e: e.activation(out=rq, in_=sdq, func=AF.Exp, scale=-0.5), r=[("sdq", i)], w=[("rq", i)])
        op("vector", lambda e: e.tensor_tensor(out=V(dst, [[hd, G], [1, hd]]), in0=V(src, [[hd, G], [1, hd]]),
                                               in1=V(rq, [[1, G], [0, hd]]), op=ALU.mult),
           r=[skey, ("rq", i)], ww=[dkey])
        op("vector", lambda e: e.tensor_tensor(out=V(dst, [[hd, G], [1, hd]]), in0=V(dst, [[hd, G], [1, hd]]),
                                               in1=V(gain, [[0, G], [1, hd]]), op=ALU.mult),
           r=["pbl", dkey], ww=[dkey])

    def rope(src, skey, dst, dkey, parts, tb, tkey, t1, t2, k1, k2):
        for (off, G, hd, co, so) in parts:
            q = hd // 4
            n = G * hd
            op("vector", lambda e, off=off, G=G, hd=hd, co=co, n=n: e.tensor_tensor(
                out=V(t1[:, off:off + n], [[hd, G], [1, hd]]), in0=V(src[:, off:off + n], [[hd, G], [1, hd]]),
                in1=V(tb[:, co:co + hd], [[0, G], [1, hd]]), op=ALU.mult), r=[skey, tkey], ww=[k1])
            for pr in range(2):
                op("vector", lambda e, off=off, G=G, hd=hd, so=so, q=q, pr=pr: e.tensor_tensor(
                    out=V(t2, [[hd, G], [2 * q, 2], [1, q]], off + pr * q),
                    in0=V(src, [[hd, G], [2 * q, 2], [1, q]], off + (1 - pr) * q),
                    in1=V(tb, [[0, G], [2 * q, 2], [1, q]], so + pr * q), op=ALU.mult),
                   r=[skey, tkey], ww=[k2])
        W = sum(p[1] * p[2] for p in parts)
        o0 = parts[0][0]
        op("vector", lambda e: e.tensor_tensor(out=dst[:, o0:o0 + W], in0=t1[:, o0:o0 + W], in1=t2[:, o0:o0 + W], op=ALU.add),
           r=[k1, k2], w=[dkey])

    def make_gate(kind, l, gi):
        gT = mT(kind, l, gi)
        for c in range(8):
            op("vector", lambda e, c=c: e.tensor_scalar(out=diag, in0=ident32, scalar1=gT[:, c:c + 1], scalar2=None,
                                                        op0=ALU.mult), r=["cst32", "modsT"], w=["diag"])
            bk = 6 + c // 4
            op("tensor", lambda e, c=c, bk=bk: e.matmul(P[bk][:, (c % 4) * 128:(c % 4 + 1) * 128], lhsT=ones32,
                                                        rhs=diag, start=True, stop=True),
               r=["diag", "cst32"], w=[("ps", bk)])
        for h in range(2):
            op("scalar", lambda e, h=h: e.activation(out=Gb[:, h * 512:(h + 1) * 512], in_=P[6 + h], func=AF.Copy),
               r=[("ps", 6 + h)], ww=["Gb"])

    def xsrc(l, ph):
        return xall if (l == 0 and ph != "C") else y

    xsl = [0]

    def load_x(l, ph, row0, npart=128):
        xsl[0] += 1
        i = xsl[0] % 2
        dma("sync", xs[i][0:npart, :], xsrc(l, ph)[row0:row0 + npart, :], r=[("xres", row0 // 128)], w=[("xs", i)],
            key=("xs", i))
        return i

    def load_halo(l, ph, s_):
        nb = s_.T // s_.NT - 1
        xsl[0] += 1
        i = xsl[0] % 2
        src = xsrc(l, ph)
        for bb in range(nb):
            r0 = s_.row0 + (bb + 1) * s_.NT - 1
            dma("sync", xs[i][2 * bb:2 * bb + 2, :], src[r0:r0 + 2, :],
                r=[("xres", r0 // 128), ("xres", (r0 + 1) // 128)] if bb > 0 else
                [("xres", r0 // 128), ("xres", (r0 + 1) // 128)], w=[("xs", i)] if bb == 0 else [], key=("xs", i))
        b.lastw[("xs", i)] = {("xs", i): b.dval[("xs", i)]}
        return i, 2 * nb

    def mods_block(l, blk):
        wv = w_mod[l].rearrange("(k p) n -> p k n", p=128)
        dma("gpsimd", wstA.rearrange("p (k n) -> p k n", k=8), wv[:, :, blk * 512:(blk + 1) * 512], w=["wstA"], key="wstA")
        for kind in range(2):
            for k in range(8):
                op("tensor", lambda e, kind=kind, k=k: e.matmul(
                    P[1][0:1, 0:512], lhsT=condb[:, kind * 8 + k: kind * 8 + k + 1],
                    rhs=wstA[:, k * 512:(k + 1) * 512], start=(k == 0), stop=(k == 7)),
                   r=["wstA", "condb"], w=[("ps", 1)])
            op("scalar", lambda e: e.activation(out=rowA[0:1, 0:512], in_=P[1][0:1, 0:512], func=AF.Copy),
               r=[("ps", 1)], w=["rowA"])
            for c in range(4):
                op("tensor", lambda e, c=c: e.matmul(P[1][:, c:c + 1], lhsT=rowA[0:1, c * 128:(c + 1) * 128],
                                                     rhs=ones32[0:1, 0:1], start=True, stop=True),
                   r=["rowA", "cst32"], w=[("ps", 1)])
            mo = (kind * L + l) * 48 + blk * 4
            op("vector", lambda e, mo=mo: e.tensor_tensor(out=modsT[:, mo:mo + 4], in0=P[1][:, 0:4],
                                                         in1=pfv("bmod", l * 48 + blk * 4, 4), op=ALU.add),
               r=[("ps", 1), "pf"], ww=["modsT"])

    class Stop(Exception):
        pass

    def chk(n):
        if cfg.stop <= n:
            raise Stop()

    A0, CB, CC, CU, Q0 = 0, 768, 1024, 1280, 1536

    def phaseA(l, s_):
        kind = s_.kind
        ST, BT = SBT[:, kind * 8:kind * 8 + 8], mT(kind, l, 0)
        NT, T = s_.NT, s_.T
        ntile, nsub = T // NT, NT // 128
        gk64, gk32 = pbl[:, 64:128], pbl[:, 160:192]
        nh = 0
        if ntile > 1:
            hi, nh = load_halo(l, "A", s_)
            norm_sub(xs[hi][0:nh, :], ("xs", hi), nh, ST, BT, lambda c: hTh[:, c * 16:c * 16 + nh], "hTh", [0, 1])
        chk(2.1)
        def tile_norm(t):
            hTt = hTa[t % 2]
            for sb in range(nsub):
                i = load_x(l, "A", s_.row0 + t * NT + sb * 128)
                norm_sub(xs[i], ("xs", i), 128, ST, BT,
                         lambda c, sb=sb: hTt[:, c * 512 + sb * 128: c * 512 + (sb + 1) * 128], ("hT", t % 2), [1])

        tile_norm(0)
        for t in range(ntile):
            t0 = t * NT
            hT = hTa[t % 2]
            hTk = ("hT", t % 2)
            b.defer = True
            if t + 1 < ntile:
                tile_norm(t + 1)
            if s_.pi < 0 and l + 1 < L:
                for blk in range(12):
                    if blk * ntile // 12 == t:
                        mods_block(l + 1, blk)
            b.defer = False
            hL = (2 * (t - 1)) if t > 0 else None
            hR = (2 * t + 1) if t < ntile - 1 else None
            chk(2.2)
            for m in range(2):
                for (bank, col0) in ((2, CC + m * 128), (3, CU + m * 128), (4, CB + m * 128)):
                    for k in range(8):
                        op("tensor", lambda e, bank=bank, col0=col0, k=k: e.matmul(
                            P[bank][:, 0:NT], lhsT=w_in_s[:, k * 2304 + col0: k * 2304 + col0 + 128],
                            rhs=hT[:, k * 512: k * 512 + NT], start=(k == 0), stop=(k == 7)),
                           r=[hTk, "w_in"], w=[("ps", bank)])
                    for _ in range(4):
                        b.pop_until_pe()
                if nh:
                    for (o, col0) in ((0, CC + m * 128), (16, CU + m * 128)):
                        for k in range(8):
                            op("tensor", lambda e, o=o, col0=col0, k=k: e.matmul(
                                P[5][:, o:o + nh], lhsT=w_in_s[:, k * 2304 + col0: k * 2304 + col0 + 128],
                                rhs=hTh[:, k * 16: k * 16 + nh], start=(k == 0), stop=(k == 7)),
                               r=["hTh", "w_in"], w=[("ps", 5)])
                cuS = Fb[0]
                op("scalar", lambda e: e.activation(out=cuS[:, 0:NT], in_=P[3][:, 0:NT], func=AF.Copy),
                   r=[("ps", 3)], w=["F0"])
                op("vector", lambda e: e.tensor_tensor(out=pext[:, 1:NT + 1], in0=P[2][:, 0:NT], in1=cuS[:, 0:NT],
                                                       op=ALU.mult), r=[("ps", 2), "F0"], w=["pext"])
                if nh:
                    op("scalar", lambda e: e.activation(out=small[:, 64:64 + nh], in_=P[5][:, 16:16 + nh], func=AF.Copy),
                       r=[("ps", 5)], w=["hcu"])
                for (hx, col) in ((hL, 0), (hR, NT + 1)):
                    if hx is None:
                        op("vector", lambda e, col=col: e.memset(pext[:, col:col + 1], 0.0), ww=["pext"])
                    else:
                        op("vector", lambda e, col=col, hx=hx: e.tensor_tensor(
                            out=pext[:, col:col + 1], in0=P[5][:, hx:hx + 1], in1=small[:, 64 + hx:65 + hx], op=ALU.mult),
                           r=[("ps", 5), "hcu"], ww=["pext"])
                tcv = Fb[3]
                cw0 = PF["cw"] + (l * 2 + m) * 3
                cbo = PF["cb"] + l * 2 + m
                op("vector", lambda e, cw0=cw0, cbo=cbo: e.tensor_scalar(
                    out=tcv[:, 0:NT], in0=pext[:, 0:NT], scalar1=pf[:, cw0:cw0 + 1], scalar2=pf[:, cbo:cbo + 1],
                    op0=ALU.mult, op1=ALU.add), r=["pext"], w=["F3"])
                for kk in (1, 2):
                    op("vector", lambda e, cw0=cw0, kk=kk: e.scalar_tensor_tensor(
                        out=tcv[:, 0:NT], in0=pext[:, kk:kk + NT], scalar=pf[:, cw0 + kk:cw0 + kk + 1], in1=tcv[:, 0:NT],
                        op0=ALU.mult, op1=ALU.add), r=["pext", "F3"], w=["F3"])
                op("vector", lambda e, m=m: e.tensor_tensor(out=s_.ocT[:, m * T + t0: m * T + t0 + NT], in0=P[4][:, 0:NT],
                                                             in1=tcv[:, 0:NT], op=ALU.mult), r=[("ps", 4), "F3"])
            chk(2.3)
            for sb in range(nsub):
                g = t * nsub + sb
                for (bank, c0_, n) in ((6, A0, 512), (7, A0 + 512, 256)):
                    for k in range(8):
                        op("tensor", lambda e, bank=bank, c0_=c0_, n=n, k=k, sb=sb: e.matmul(
                            P[bank][:, 0:n], lhsT=hT[:, k * 512 + sb * 128: k * 512 + (sb + 1) * 128],
                            rhs=w_in_s[:, k * 2304 + c0_: k * 2304 + c0_ + n], start=(k == 0), stop=(k == 7)),
                           r=[hTk, "w_in"], w=[("ps", bank)])
                    for _ in range(5):
                        b.pop_until_pe()
                ki = 1 + g % 2
                kf = Fb[ki]
                kkey = "F%d" % ki
                qknorm(P[6][:, 0:128], ("ps", 6), 2, 64, gk64, kf[:, 0:128], kkey, Fb[0])
                qknorm(P[6][:, 128:384], ("ps", 6), 8, 32, gk32, kf[:, 128:384], kkey, Fb[0])
                if s_.pi >= 0:
                    rows = slice(g * 128, (g + 1) * 128)
                    dma("sync", ngk[s_.pi, l, rows, :], kf[:, 0:128], r=[kkey], key=("kf", ki))
                    dma("sync", ndk[s_.pi, l, rows, :], kf[:, 128:384], r=[kkey], key=("kf", ki))
                if s_.rope:
                    xsl[0] += 1
                    ti = xsl[0] % 2
                    dma("sync", tab[ti], rope_d[g * 128:(g + 1) * 128, :], w=[("tab", ti)], key=("tab", ti))
                    rope(kf, kkey, kb, "kb", [(0, 2, 64, 0, 64), (128, 8, 32, 128, 160)], tab[ti], ("tab", ti),
                         Fb[3], Fb[3][:, 384:768], "F3", "F3")
                else:
                    op("vector", lambda e: e.tensor_copy(out=kb, in_=kf[:, 0:384]), r=[kkey], w=["kb"])
                tb = b.tpbank([0])
                for blk in range(3):
                    op("tensor", lambda e, blk=blk: e.transpose(out=Pb[tb][:, blk * 128:(blk + 1) * 128],
                                                                in_=kb[:, blk * 128:(blk + 1) * 128], identity=identb),
                       r=["kb"], w=[("ps", tb)])
                nk = s_.nkc * 128
                op("scalar", lambda e, g=g: e.activation(out=s_.KgT[:, g * 128:(g + 1) * 128], in_=Pb[tb][:, 0:128],
                                                         func=AF.Copy), r=[("ps", tb)])
                op("scalar", lambda e, g=g, nk=nk: e.activation(
                    out=V(s_.KdT, [[nk, 2], [1, 128]], g * 128), in_=V(Pb[tb], [[128, 2], [1, 128]], 128), func=AF.Copy),
                   r=[("ps", tb)])
                op("scalar", lambda e, g=g: e.activation(out=V(s_.Vg, [[65, 2], [1, 64]], g * 130),
                                                         in_=V(P[6], [[64, 2], [1, 64]], 384), func=AF.Copy),
                   r=[("ps", 6)])
                op("scalar", lambda e, g=g: e.activation(out=V(s_.Vd, [[65, 4], [1, 64]], g * 260),
                                                         in_=V(P[7], [[64, 4], [1, 64]], 0), func=AF.Copy),
                   r=[("ps", 7)])
                if s_.pi >= 0:
                    vi = g % 2
                    op("vector", lambda e, vi=vi: e.tensor_copy(out=v32[vi][:, 0:128], in_=P[6][:, 384:512]),
                       r=[("ps", 6)], w=[("v32", vi)])
                    op("vector", lambda e, vi=vi: e.tensor_copy(out=v32[vi][:, 128:384], in_=P[7][:, 0:256]),
                       r=[("ps", 7)], ww=[("v32", vi)])
                    rows = slice(g * 128, (g + 1) * 128)
                    dma("sync", ngv[s_.pi, l, rows, :], v32[vi][:, 0:128], r=[("v32", vi)], key=("v32", vi))
                    dma("sync", ndv[s_.pi, l, rows, :], v32[vi][:, 128:384], r=[("v32", vi)], key=("v32", vi))
            b.flush()
        chk(2.4)
        if s_.pi < 0:
            nown = T // 128
            nk = s_.nkc * 128
            for sb in range(PAST // 128):
                rows = slice(sb * 128, (sb + 1) * 128)
                ci = 1 + sb % 2
                cf = Fb[ci]
                ckey = "F%d" % ci
                dma("sync", cf[:, 0:128], ck[l, rows, :], w=[ckey], key=("cst", ci))
                dma("sync", cf[:, 128:384], cdk[l, rows, :], key=("cst", ci))
                dma("sync", cf[:, 384:512], cv[l, rows, :], key=("cst", ci))
                dma("sync", cf[:, 512:768], cdv[l, rows, :], key=("cst", ci))
                b.lastw[ckey] = {("cst", ci): b.dval[("cst", ci)]}
                g = nown + sb
                op("vector", lambda e, cf=cf, sb=sb: e.tensor_copy(out=ckb[:, sb * 384:(sb + 1) * 384], in_=cf[:, 0:384]),
                   r=[ckey], w=[("ckb", sb)])
                op("scalar", lambda e, g=g, cf=cf: e.activation(out=V(s_.Vg, [[65, 2], [1, 64]], g * 130),
                                                                in_=V(cf, [[64, 2], [1, 64]], 384), func=AF.Copy), r=[ckey])
                op("scalar", lambda e, g=g, cf=cf: e.activation(out=V(s_.Vd, [[65, 4], [1, 64]], g * 260),
                                                                in_=V(cf, [[64, 4], [1, 64]], 512), func=AF.Copy), r=[ckey])
                chk(2.5)
                tb = b.tpbank([0])
                for blk in range(3):
                    op("tensor", lambda e, blk=blk, sb=sb: e.transpose(
                        out=Pb[tb][:, blk * 128:(blk + 1) * 128], in_=ckb[:, sb * 384 + blk * 128: sb * 384 + (blk + 1) * 128],
                        identity=identb), r=[("ckb", sb)], w=[("ps", tb)])
                op("scalar", lambda e, g=g: e.activation(out=s_.KgT[:, g * 128:(g + 1) * 128], in_=Pb[tb][:, 0:128],
                                                         func=AF.Copy), r=[("ps", tb)])
                op("scalar", lambda e, g=g, nk=nk: e.activation(
                    out=V(s_.KdT, [[nk, 2], [1, 128]], g * 128), in_=V(Pb[tb], [[128, 2], [1, 128]], 128), func=AF.Copy),
                   r=[("ps", tb)])

    def attention(maps, nkc, NT):
        nsub = NT // 128
        Pall = b.Pall
        for k in range(nkc + 2):
            if k < nkc:
                for m in maps:
                    sb_ = 2 * m["i"] + k % 2
                    op("tensor", lambda e, m=m, k=k, sb_=sb_: e.matmul(P[sb_][:, 0:NT], lhsT=m["kT"](k), rhs=m["q"],
                                                                       start=True, stop=True, tile_position=m["tp"]),
                       r=[m["qk"]], w=[("ps", sb_)])
            if k >= 2:
                kk = k - 2
                sl_ = kk % 3
                for m in maps:
                    po = sl_ * 1024 + m["i"] * 512
                    for sb in range(nsub):
                        op("tensor", lambda e, m=m, kk=kk, sb=sb, po=po: e.matmul(
                            P[m["O"]][:, sb * 65:(sb + 1) * 65], lhsT=PbA[:, po + sb * 128: po + (sb + 1) * 128], rhs=m["v"](kk),
                            start=(kk == 0 and sb == 0), stop=(kk == nkc - 1), skip_group_check=True),
                           r=[("pb", sl_)], w=[("ps", m["O"])])
            b.pop_until_pe()
            if 1 <= k <= nkc:
                kk = k - 1
                sl_ = kk % 3
                op("scalar", lambda e, kk=kk, sl_=sl_: e.activation(
                    out=V(PbA, [[512, 2], [1, NT]], sl_ * 1024), in_=V(Pall, [[1024, 2], [1, NT]], (kk % 2) * 512),
                    func=AF.Exp, scale=maps[0]["scale"]),
                   r=[("ps", kk % 2), ("ps", 2 + kk % 2)], w=[("pb", sl_)])

    def phaseB(l, s_, first_of_kind):
        kind = s_.kind
        ST, BT = SBT[:, kind * 8:kind * 8 + 8], mT(kind, l, 0)
        NT, T, nkc = s_.NT, s_.T, s_.nkc
        ntile, nsub = T // NT, NT // 128
        gq64, gq32 = pbl[:, 0:64], pbl[:, 128:160]
        nk = nkc * 128
        if first_of_kind:
            make_gate(kind, l, 2)
        st = {}

        def pn(t, sb):
            i = load_x(l, "B", s_.row0 + t * NT + sb * 128)
            st[(t, sb)] = norm_a(xs[i], ("xs", i), 128)

        def pt(t, sb):
            hq = hTq[sb % 2]
            norm_b(st[(t, sb)], 128, ST, BT, lambda c, hq=hq: hq[:, c * 128:(c + 1) * 128], ("hTq", sb % 2), [6])

        def pz(t, sb):
            hq = hTq[sb % 2]
            hkey = ("hTq", sb % 2)
            for (bank, c0_, n) in ((6, Q0, 512), (7, Q0 + 512, 256)):
                for k in range(8):
                    op("tensor", lambda e, bank=bank, c0_=c0_, n=n, k=k, hq=hq: e.matmul(
                        P[bank][:, 0:n], lhsT=hq[:, k * 128:(k + 1) * 128],
                        rhs=w_in_s[:, k * 2304 + c0_: k * 2304 + c0_ + n], start=(k == 0), stop=(k == 7)),
                       r=[hkey, "w_in"], w=[("ps", bank)])
            qf = Fb[1]
            qknorm(P[6][:, 0:512], ("ps", 6), 8, 64, gq64, qf[:, 0:512], "F1", Fb[0])
            qknorm(P[7][:, 0:256], ("ps", 7), 8, 32, gq32, qf[:, 512:768], "F1", Fb[0])
            if s_.rope:
                g = t * nsub + sb
                xsl[0] += 1
                ti = xsl[0] % 2
                dma("sync", tabB[ti], rope_d[g * 128:(g + 1) * 128, :], w=[("tabB", ti)], key=("tabB", ti))
                rope(qf, "F1", qbs[sb % 2], ("qb", sb % 2), [(0, 8, 64, 0, 64), (512, 8, 32, 128, 160)], tabB[ti], ("tabB", ti),
                     Fb[2], Fb[3], "F2", "F3")
            else:
                op("vector", lambda e: e.tensor_copy(out=qbs[sb % 2], in_=qf[:, 0:768]), r=["F1"], w=[("qb", sb % 2)])

        def pq(t, sb):
            QTt = QTs[t % 2]
            qb = qbs[sb % 2]
            for (b0_, nb_) in ((0, 4), (4, 2)):
                for blk in range(nb_):
                    op("tensor", lambda e, blk=blk, b0_=b0_: e.transpose(
                        out=Pb[7][:, 512 + blk * 128: 512 + (blk + 1) * 128],
                        in_=qb[:, (b0_ + blk) * 128:(b0_ + blk + 1) * 128], identity=identb),
                       r=[("qb", sb % 2)], w=[("ps", 7)])
                op("scalar", lambda e, sb=sb, b0_=b0_, nb_=nb_, QTt=QTt: e.activation(
                    out=V(QTt, [[512, nb_], [1, 128]], b0_ * 512 + sb * 128),
                    in_=V(Pb[7], [[128, nb_], [1, 128]], 512), func=AF.Copy),
                   r=[("ps", 7)], ww=[("QT", t % 2)])

        def ey(t, sb):
            t0 = t * NT
            for half in range(2):
                for c in range(8):
                    if c < 4:
                        lt = ocatT[:, c * 512 + sb * 128: c * 512 + (sb + 1) * 128]
                    elif c < 6:
                        lt = s_.ocT[:, (c - 4) * T + t0 + sb * 128: (c - 4) * T + t0 + (sb + 1) * 128]
                    else:
                        lt = ocatT[:, (c - 2) * 512 + sb * 128: (c - 2) * 512 + (sb + 1) * 128]
                    op("tensor", lambda e, half=half, c=c, lt=lt: e.matmul(
                        P[6 + half], lhsT=lt, rhs=w_out_s[:, c * 1024 + half * 512: c * 1024 + (half + 1) * 512],
                        start=(c == 0), stop=(c == 7)), r=["ocatT", "w_out"], w=[("ps", 6 + half)])
            residual(l, "B", s_.row0 + t0 + sb * 128)

        def eo(t):
            for sb in range(nsub):
                tb = b.tpbank([6, 7])
                for blk in range(6):
                    op("tensor", lambda e, blk=blk, sb=sb: e.transpose(
                        out=Pb[tb][:, blk * 128:(blk + 1) * 128], in_=otok[:, sb * 768 + blk * 128: sb * 768 + (blk + 1) * 128],
                        identity=identb), r=["otok"], w=[("ps", tb)])
                op("scalar", lambda e, sb=sb: e.activation(out=V(ocatT, [[512, 6], [1, 128]], sb * 128),
                                                           in_=V(Pb[tb], [[128, 6], [1, 128]]), func=AF.Copy),
                   r=[("ps", tb)], ww=["ocatT"])

        for sb in range(nsub):
            pn(0, sb), pt(0, sb), pz(0, sb), pq(0, sb)
        for t in range(ntile):
            t0 = t * NT
            QT = QTs[t % 2]
            b.defer = True
            if t > 0:
                for sb in range(nsub):
                    ey(t - 1, sb)
            if t + 1 < ntile:
                tn = t + 1
                pn(tn, 0), pn(tn, 1), pt(tn, 0), pt(tn, 1), pn(tn, 2), pn(tn, 3)
                pz(tn, 0), pt(tn, 2), pz(tn, 1), pq(tn, 0), pt(tn, 3), pz(tn, 2), pq(tn, 1), pz(tn, 3), pq(tn, 2), pq(tn, 3)
            b.defer = False

            def after_group():
                pass
            for j in range(4):
                gcn[0] += 1
                ob = (4, 5)
                maps = []
                for r_ in range(2):
                    maps.append(dict(
                        i=r_, S=(2 * r_, 2 * r_ + 1), O=ob[r_], tp=(64 * r_, 0), scale=0.125, qk=("QT", t % 2),
                        kT=lambda kc, r_=r_: s_.KgT[64 * r_:64 * r_ + 64, kc * 128:(kc + 1) * 128],
                        q=QT[64 * r_:64 * r_ + 64, j * 512: j * 512 + NT],
                        v=lambda kc, r_=r_: s_.Vg[:, kc * 130 + r_ * 65: kc * 130 + r_ * 65 + 65]))
                attention(maps, nkc, NT)
                for r_ in range(2):
                    h = j + 4 * r_
                    Ov = P[ob[r_]]
                    rs = small[:, 72 + r_ * 4: 72 + r_ * 4 + nsub]
                    op("vector", lambda e, Ov=Ov, rs=rs: e.reciprocal(out=rs, in_=V(Ov, [[65, nsub], [1, 1]], 64)),
                       r=[("ps", ob[r_])], w=[("rsg", r_)])
                    op("vector", lambda e, Ov=Ov, rs=rs, h=h: e.tensor_tensor(
                        out=V(otok, [[768, nsub], [1, 64]], h * 64), in0=V(Ov, [[65, nsub], [1, 64]]),
                        in1=V(rs, [[1, nsub], [0, 64]]), op=ALU.mult), r=[("ps", ob[r_]), ("rsg", r_)], ww=["otok"])
                after_group()
            for h in range(4):
                hb, hh = h // 2, h % 2
                gcn[0] += 1
                ob = (4, 5)
                maps = []
                for c_ in range(2):
                    gi = 2 * hh + c_
                    maps.append(dict(
                        i=c_, S=(2 * c_, 2 * c_ + 1), O=ob[c_], tp=(32 * gi, 0), scale=32 ** -0.5, qk=("QT", t % 2),
                        kT=lambda kc, gi=gi, hb=hb: s_.KdT[32 * gi:32 * gi + 32, hb * nk + kc * 128: hb * nk + (kc + 1) * 128],
                        q=QT[32 * gi:32 * gi + 32, (4 + hb) * 512: (4 + hb) * 512 + NT],
                        v=lambda kc, h=h: s_.Vd[:, kc * 260 + h * 65: kc * 260 + h * 65 + 65]))
                attention(maps, nkc, NT)
                if True:
                    Oa, Ob = P[ob[0]], P[ob[1]]
                    ka, kb_ = ("ps", ob[0]), ("ps", ob[1])
                    r0 = small[:, 80:80 + nsub]
                    r1 = small[:, 84:84 + nsub]
                    n = nsub * 64
                    t0_, t1_ = Fb[0][:, 0:n], Fb[0][:, 256:256 + n]
                    op("vector", lambda e, Oa=Oa: e.reciprocal(out=r0, in_=V(Oa, [[65, nsub], [1, 1]], 64)), r=[ka], w=["r0"])
                    op("vector", lambda e, Ob=Ob: e.reciprocal(out=r1, in_=V(Ob, [[65, nsub], [1, 1]], 64)), r=[kb_], w=["r1"])
                    op("vector", lambda e, Oa=Oa: e.tensor_tensor(out=V(t0_, [[64, nsub], [1, 64]]), in0=V(Oa, [[65, nsub], [1, 64]]),
                                                                  in1=V(r0, [[1, nsub], [0, 64]]), op=ALU.mult),
                       r=[ka, "r0"], w=["F0"])
                    op("vector", lambda e, Ob=Ob: e.tensor_tensor(out=V(t1_, [[64, nsub], [1, 64]]), in0=V(Ob, [[65, nsub], [1, 64]]),
                                                                  in1=V(r1, [[1, nsub], [0, 64]]), op=ALU.mult),
                       r=[kb_, "r1"], ww=["F0"])
                    od = Fb[0][:, 512:512 + n]
                    op("vector", lambda e: e.scalar_tensor_tensor(out=od, in0=t1_, scalar=lamt[:, 4 + l:5 + l], in1=t0_,
                                                                  op0=ALU.mult, op1=ALU.add), r=["F0", "lamt"], ww=["F0"])
                    sqd = Fb[0][:, 768:768 + n]
                    op("scalar", lambda e: e.activation(out=sqd, in_=od, func=AF.Square), r=["F0"], ww=["F0"])
                    ssd, sdd, rrd = small[:, 88:88 + nsub], small[:, 92:92 + nsub], small[:, 96:96 + nsub]
                    op("vector", lambda e: e.tensor_reduce(out=ssd, in_=V(sqd, [[64, nsub], [1, 64]]), axis=AX.X, op=ALU.add),
                       r=["F0"], w=["ssd"])
                    op("scalar", lambda e: e.activation(out=sdd, in_=ssd, func=AF.Ln, scale=1.0 / 64, bias=epsc),
                       r=["ssd"], w=["sdd"])
                    op("scalar", lambda e: e.activation(out=rrd, in_=sdd, func=AF.Exp, scale=-0.5), r=["sdd"], w=["rrd"])
                    op("vector", lambda e: e.tensor_tensor(out=V(od, [[64, nsub], [1, 64]]), in0=V(od, [[64, nsub], [1, 64]]),
                                                           in1=V(rrd, [[1, nsub], [0, 64]]), op=ALU.mult),
                       r=["F0", "rrd"], ww=["F0"])
                    op("vector", lambda e, h=h: e.tensor_tensor(out=V(otok, [[768, nsub], [1, 64]], 512 + h * 64),
                                                                in0=V(od, [[64, nsub], [1, 64]]),
                                                                in1=V(gsubS, [[0, nsub], [1, 64]]), op=ALU.mult),
                       r=["F0", "gsubS"], ww=["otok"])
                after_group()
            b.flush()
            eo(t)
            if t == ntile - 1:
                for sb in range(nsub):
                    ey(t, sb)

    rsl = [0]
    gcn = [0]

    def residual(l, ph, row0):
        i = load_x(l, ph, row0)
        rsl[0] += 1
        ri = 2 + rsl[0] % 2
        tr = Fb[ri]
        rk = "F%d" % ri
        for half in range(2):
            op("vector", lambda e, half=half: e.tensor_tensor(out=tr[:, half * 512:(half + 1) * 512], in0=P[6 + half],
                                                              in1=Gb[:, half * 512:(half + 1) * 512], op=ALU.mult),
               r=[("ps", 6 + half), "Gb"], w=[rk] if half == 0 else [], ww=[] if half == 0 else [rk])
        op("gpsimd", lambda e: e.tensor_tensor(out=tr, in0=tr, in1=xs[i], op=ALU.add), r=[rk, ("xs", i)], w=[rk])
        dma("sync", y[row0:row0 + 128, :], tr, r=[rk], w=[("xres", row0 // 128)], key=rk)

    def phaseC(l, s_, first_of_kind):
        kind = s_.kind
        ST, BT = SBT[:, 16 + kind * 8:16 + kind * 8 + 8], mT(kind, l, 3)
        NT, T = s_.NT, s_.T
        ntile, nsub = T // NT, NT // 128
        if first_of_kind:
            make_gate(kind, l, 5)
        nh = 0
        if ntile > 1:
            hi, nh = load_halo(l, "C", s_)
            norm_sub(xs[hi][0:nh, :], ("xs", hi), nh, ST, BT, lambda c: hTh[:, c * 16:c * 16 + nh], "hTh", [4])
        for t in range(ntile):
            t0 = t * NT
            for sb in range(nsub):
                i = load_x(l, "C", s_.row0 + t0 + sb * 128)
                norm_sub(xs[i], ("xs", i), 128, ST, BT,
                         lambda c, sb=sb: hTc[:, c * 512 + sb * 128: c * 512 + (sb + 1) * 128], "hTc", [4])
            hL = (2 * (t - 1)) if t > 0 else None
            hR = (2 * t + 1) if t < ntile - 1 else None
            for j in range(NJ):
                pa, pu = j % 2, 2 + j % 2
                for (bank, co) in ((pa, j * 256), (pu, j * 256 + 128)):
                    for k in range(8):
                        op("tensor", lambda e, bank=bank, co=co, k=k: e.matmul(
                            P[bank][:, 0:NT], lhsT=fup_s[:, k * 2 * DFF + co: k * 2 * DFF + co + 128],
                            rhs=hTc[:, k * 512: k * 512 + NT], start=(k == 0), stop=(k == 7)),
                           r=["hTc", ("fup", j // 2)], w=[("ps", bank)])
                if nh:
                    for k in range(8):
                        op("tensor", lambda e, j=j, k=k: e.matmul(
                            P[5][:, j * 16: j * 16 + nh], lhsT=fup_s[:, k * 2 * DFF + j * 256: k * 2 * DFF + j * 256 + 128],
                            rhs=hTh[:, k * 16: k * 16 + nh], start=(k == 0), stop=(k == 7)),
                           r=["hTh", ("fup", j // 2)], w=[("ps", 5)])
                ax = aext[j % 2]
                akey = ("aext", j % 2)
                op("scalar", lambda e, ax=ax, pa=pa: e.activation(out=ax[:, 1:NT + 1], in_=P[pa][:, 0:NT], func=AF.Copy),
                   r=[("ps", pa)], w=[akey])
                for (hx, col) in ((hL, 0), (hR, NT + 1)):
                    if hx is None:
                        op("vector", lambda e, ax=ax, col=col: e.memset(ax[:, col:col + 1], 0.0), ww=[akey])
                    else:
                        op("vector", lambda e, ax=ax, col=col, hx=hx, j=j: e.tensor_copy(
                            out=ax[:, col:col + 1], in_=P[5][:, j * 16 + hx: j * 16 + hx + 1]), r=[("ps", 5)], ww=[akey])
                tcv = Fb[0][:, (j % 2) * 512:(j % 2) * 512 + NT]
                tk = ("tcv", j % 2)
                sil = Fb[1][:, (j % 2) * 512:(j % 2) * 512 + NT]
                sk = ("sil", j % 2)
                w0 = PF["fcw"] + (l * NJ + j) * 3
                bo = PF["fcb"] + l * NJ + j
                op("vector", lambda e, ax=ax, tcv=tcv, w0=w0, bo=bo: e.tensor_scalar(
                    out=tcv, in0=ax[:, 0:NT], scalar1=pf[:, w0:w0 + 1], scalar2=pf[:, bo:bo + 1],
                    op0=ALU.mult, op1=ALU.add), r=[akey], w=[tk])
                for kk in (1, 2):
                    op("vector", lambda e, ax=ax, tcv=tcv, w0=w0, kk=kk: e.scalar_tensor_tensor(
                        out=tcv, in0=ax[:, kk:kk + NT], scalar=pf[:, w0 + kk:w0 + kk + 1], in1=tcv,
                        op0=ALU.mult, op1=ALU.add), r=[akey, tk], w=[tk])
                op("scalar", lambda e, tcv=tcv, sil=sil: e.activation(out=sil, in_=tcv, func=AF.Silu), r=[tk], w=[sk])
                op("vector", lambda e, sil=sil, pu=pu, j=j: e.tensor_tensor(out=fT[:, j * 512: j * 512 + NT], in0=P[pu][:, 0:NT],
                                                                            in1=sil, op=ALU.mult),
                   r=[("ps", pu), sk], ww=["fT"])
            for sb in range(nsub):
                for half in range(2):
                    for j in range(NJ):
                        op("tensor", lambda e, half=half, j=j, sb=sb: e.matmul(
                            P[6 + half], lhsT=fT[:, j * 512 + sb * 128: j * 512 + (sb + 1) * 128],
                            rhs=fdn_s[:, j * 1024 + half * 512: j * 1024 + (half + 1) * 512],
                            start=(j == 0), stop=(j == NJ - 1)), r=["fT", ("fdn", j // 11)], w=[("ps", 6 + half)])
                residual(l, "C", s_.row0 + t0 + sb * 128)

    try:
      chk(1)
      for l in range(L):
          lam_init = 0.8 - 0.6 * math.exp(-0.3 * l)
          wi = w_in[l].rearrange("(k p) n -> p k n", p=128)
          wis = w_in_s.rearrange("p (k n) -> p k n", k=8)
          for c0_ in range(0, 2304, 768):
              dma("gpsimd", wis[:, :, c0_:c0_ + 768], wi[:, :, c0_:c0_ + 768], w=[] if c0_ else ["w_in"], key="w_in")
          b.lastw["w_in"] = {"w_in": b.dval["w_in"]}
          wo = w_out[l].rearrange("(k p) n -> p k n", p=128)
          dma("gpsimd", w_out_s.rearrange("p (k n) -> p k n", k=8), wo, w=["w_out"], key="w_out")
          dma("sync", pbl, AP(pb_d.tensor, l * PB_N, [[0, 128], [1, PB_N]]), w=["pbl"], key="pbl")
          for kind in range(2):
              op("vector", lambda e, kind=kind: e.scalar_tensor_tensor(
                  out=SBT[:, kind * 8:kind * 8 + 8], in0=mT(kind, l, 1), scalar=1.0, in1=pfv("n1g", l * 8, 8),
                  op0=ALU.add, op1=ALU.mult), r=["modsT", "pf"], ww=["SBT"])
              op("vector", lambda e, kind=kind: e.scalar_tensor_tensor(
                  out=SBT[:, 16 + kind * 8:16 + kind * 8 + 8], in0=mT(kind, l, 4), scalar=1.0, in1=pfv("n2g", l * 8, 8),
                  op0=ALU.add, op1=ALU.mult), r=["modsT", "pf"], ww=["SBT"])
          dl = pbl[:, 256:384]
          pr = small[:, 128:192]
          op("vector", lambda e: e.tensor_tensor(out=V(pr, [[32, 2], [1, 32]]), in0=V(dl, [[64, 2], [1, 32]]),
                                                 in1=V(dl, [[64, 2], [1, 32]], 32), op=ALU.mult), r=["pbl"], w=["pr"])
          op("vector", lambda e: e.tensor_reduce(out=lamt[:, 0:2], in_=V(pr, [[32, 2], [1, 32]]), axis=AX.X, op=ALU.add),
             r=["pr"], w=["lam0"])
          op("scalar", lambda e: e.activation(out=lamt[:, 2:4], in_=lamt[:, 0:2], func=AF.Exp), r=["lam0"], w=["lam1"])
          op("vector", lambda e, li=lam_init: e.scalar_tensor_tensor(
              out=lamt[:, 4 + l:5 + l], in0=lamt[:, 3:4], scalar=-li, in1=lamt[:, 2:3], op0=ALU.add, op1=ALU.subtract),
             r=["lam1"], w=["lamt"])
          op("vector", lambda e, li=lam_init: e.tensor_scalar(out=gsubS, in0=pbl[:, 192:256], scalar1=1.0 - li, scalar2=None,
                                                              op0=ALU.mult), r=["pbl"], w=["gsubS"])
          for s_ in (seqs[0], seqs[1]):
              nk_ = s_.nkc
              op("vector", lambda e, a=V(s_.Vg, [[65, nk_ * 2], [1, 1]], 64): e.memset(a, 1.0))
              op("vector", lambda e, a=V(s_.Vd, [[65, nk_ * 4], [1, 1]], 64): e.memset(a, 1.0))
          b.barrier()
          chk(2)
          prevk = -1
          for s_ in seqs:
              phaseA(l, s_)
              b.barrier()
              chk(3)
              phaseB(l, s_, s_.kind != prevk)
              prevk = s_.kind
              b.barrier()
          fu = f_up[l].rearrange("(k p) n -> p k n", p=128)
          fus = fup_s.rearrange("p (k n) -> p k n", k=8)
          for pc in range(NJ // 2):
              dma("gpsimd", fus[:, :, pc * 512:(pc + 1) * 512], fu[:, :, pc * 512:(pc + 1) * 512], w=[("fup", pc)],
                  key=("fup", pc))
          fd = f_dn[l].rearrange("(j p) n -> p j n", p=128)
          fds = fdn_s.rearrange("p (j n) -> p j n", j=NJ)
          for pc in range(2):
              dma("gpsimd", fds[:, pc * 11:(pc + 1) * 11, :], fd[:, pc * 11:(pc + 1) * 11, :], w=[("fdn", pc)],
                  key=("fdn", pc))
          prevk = -1
          for s_ in seqs:
              phaseC(l, s_, s_.kind != prevk)
              prevk = s_.kind
          b.barrier()
    except Stop:
        pass
    b.finish()
    return nc


_NC_CACHE = {}


def _rope_table(TS):
    n_rows = TS // GRID_W
    row = np.repeat(np.arange(n_rows), GRID_W).astype(np.float32)
    col = np.tile(np.arange(GRID_W), n_rows).astype(np.float32)
    out = []
    for dim in (64, 32):
        nf = dim // 4
        freqs = (np.float32(10000.0) ** (-np.arange(nf, dtype=np.float32) / np.float32(nf))).astype(np.float32)
        ar = row[:, None] * freqs[None, :]
        ac = col[:, None] * freqs[None, :]
        ang = np.concatenate([ar, ar, ac, ac], axis=-1).astype(np.float32)
        cos, sin = np.cos(ang), np.sin(ang)
        sgn = np.concatenate([-np.ones(nf), np.ones(nf), -np.ones(nf), np.ones(nf)]).astype(np.float32)
        out += [cos.astype(np.float32), (sin * sgn[None, :]).astype(np.float32)]
    return np.ascontiguousarray(np.concatenate(out, axis=-1), dtype=np.float32)


def _fm(v):
    v = np.asarray(v, dtype=np.float32)
    n = v.shape[-1] // 128
    r = v.reshape(v.shape[:-1] + (n, 128))
    return np.moveaxis(r, -1, 0)


def run(cfg, inp, n_cores):
    L, TS, TP, NP = cfg.L, cfg.TS, cfg.TP, cfg.NP
    f = lambda k: np.asarray(inp[k], dtype=np.float32)
    key = (cfg.TS, cfg.TP, cfg.NP, cfg.L, cfg.PAST)
    if key not in _NC_CACHE:
        _NC_CACHE[key] = build(cfg)
    nc = _NC_CACHE[key]
    PF = pf_layout(L)
    qg = [np.arange(h * 64, (h + 1) * 64) for h in (0, 4, 1, 5, 2, 6, 3, 7)]
    o_qg, o_kg, o_vg, o_cb, o_cc, o_cu, o_qd, o_kd, o_vd = 0, 512, 640, 768, 1024, 1280, 1536, 1792, 2048
    perm = np.concatenate([np.arange(o_kg, o_kg + 128), np.arange(o_kd, o_kd + 256), np.arange(o_vg, o_vg + 128),
                           np.arange(o_vd, o_vd + 256), np.arange(o_cb, o_cb + 256), np.arange(o_cc, o_cc + 256),
                           np.arange(o_cu, o_cu + 256), np.concatenate(qg), np.arange(o_qd, o_qd + 256)])
    w_in_p = np.ascontiguousarray(f("w_in")[:, :, perm])
    fu = f("ffn_up")
    fu_p = np.ascontiguousarray(
        np.stack([fu[:, :, :DFF].reshape(L, D, NJ, 128), fu[:, :, DFF:].reshape(L, D, NJ, 128)], axis=3).reshape(L, D, 2 * DFF))
    pb = np.ascontiguousarray(np.concatenate([f("gqa_qn_g"), f("gqa_kn_g"), f("diff_qn_g"), f("diff_kn_g"), f("diff_subln_g"),
                                              f("diff_lambda").reshape(L, 128)], axis=1))
    identb = np.eye(128, dtype=np.float32).astype(ml_dtypes.bfloat16)
    cst32 = np.ascontiguousarray(np.concatenate([np.eye(128, dtype=np.float32), np.ones((128, 128), np.float32)], axis=1))
    rope = _rope_table(TS)
    shared = dict(w_mod=f("w_mod"), w_in=w_in_p, w_out=f("w_out"), f_up=fu_p, f_dn=f("ffn_down"), rope=rope, identb=identb,
                  cst32=cst32, pb=pb)
    xp, xsm = f("x_prompt"), f("x_sample")
    in_maps = []
    for c in range(n_cores):
        pfa = np.zeros((128, PF["_n"]), np.float32)
        pfa[:, PF["cond"]:PF["cond"] + 8] = _fm(f("c")[c])
        pfa[:, PF["cond"] + 8:PF["cond"] + 16] = _fm(f("c_ctx"))
        pfa[:, PF["bmod"]:PF["bmod"] + L * 48] = _fm(f("b_mod")).reshape(128, L * 48)
        pfa[:, PF["n1g"]:PF["n1g"] + L * 8] = _fm(f("norm1_g")).reshape(128, L * 8)
        pfa[:, PF["n2g"]:PF["n2g"] + L * 8] = _fm(f("norm2_g")).reshape(128, L * 8)
        cw = _fm(f("conv_w"))
        pfa[:, PF["cw"]:PF["cw"] + L * 6] = np.transpose(cw, (0, 1, 3, 2)).reshape(128, L * 6)
        pfa[:, PF["cb"]:PF["cb"] + L * 2] = _fm(f("conv_b")).reshape(128, L * 2)
        fcw = _fm(f("ffn_conv_w"))
        pfa[:, PF["fcw"]:PF["fcw"] + L * NJ * 3] = np.transpose(fcw, (0, 1, 3, 2)).reshape(128, L * NJ * 3)
        pfa[:, PF["fcb"]:PF["fcb"] + L * NJ] = _fm(f("ffn_conv_b")).reshape(128, L * NJ)
        xall = np.ascontiguousarray(np.concatenate([xsm[c], xp[c * NP:(c + 1) * NP].reshape(NP * TP, D)], axis=0))
        m = dict(shared)
        m.update(xall=xall, pf=pfa,
                 ck=np.ascontiguousarray(f("cache_gqa_k")[c].reshape(L, cfg.PAST, 128)),
                 cv=np.ascontiguousarray(f("cache_gqa_v")[c].reshape(L, cfg.PAST, 128)),
                 cdk=np.ascontiguousarray(f("cache_diff_k")[c].reshape(L, cfg.PAST, 256)),
                 cdv=np.ascontiguousarray(f("cache_diff_v")[c].reshape(L, cfg.PAST, 256)))
        in_maps.append(m)
    res = run_bass_kernel_spmd(nc, in_maps, core_ids=list(range(n_cores)))
    R = res.results
    ys = np.stack([R[c]["y"][:TS] for c in range(n_cores)], axis=0)
    yp = np.concatenate([R[c]["y"][TS:].reshape(NP, TP, D) for c in range(n_cores)], axis=0)
    gk = np.concatenate([R[c]["ngk"] for c in range(n_cores)], axis=0).reshape(n_cores * NP, L, TP, 2, 64)
    gv = np.concatenate([R[c]["ngv"] for c in range(n_cores)], axis=0).reshape(n_cores * NP, L, TP, 2, 64)
    dk = np.concatenate([R[c]["ndk"] for c in range(n_cores)], axis=0).reshape(n_cores * NP, L, TP, 4, 2, 32)
    dv = np.concatenate([R[c]["ndv"] for c in range(n_cores)], axis=0).reshape(n_cores * NP, L, TP, 4, 64)
    return (yp.astype(np.float32), ys.astype(np.float32), gk.astype(np.float32), gv.astype(np.float32),
            dk.astype(np.float32), dv.astype(np.float32))


def kernel(**inputs):
    cfg = Cfg()
    return run(cfg, inputs, 8)
```
